# Optimizing a Trainium2 kernel written in Bass

```python
import math
import jax, jax.numpy as jnp
from jax import lax
import numpy as np

D_MODEL = 1024
BATCH = 32
SEQ = 2048
DEPTH = 1

N_META = 16
N_HEADS = 8
HEAD_DIM = 64
V_HEAD_DIM = 2 * HEAD_DIM
ATTN_WIDTH = N_HEADS * V_HEAD_DIM
Q_COLS = 2 * N_HEADS * HEAD_DIM
K_COLS = 2 * N_HEADS * HEAD_DIM
V_COLS = ATTN_WIDTH
CONV_WIDTH = D_MODEL
CONV_K = 3
GATE_COLS = 2 * D_MODEL
IN_COLS = Q_COLS + K_COLS + V_COLS + 3 * CONV_WIDTH + GATE_COLS
D_FF = ((8 * D_MODEL // 3 + 255) // 256) * 256
ROPE_THETA = 10000.0
NORM_EPS = 1e-5
Q_BLOCK = 128

kernel_name = "hybrid_diffattn_shortconv_gated_block"


def _rmsnorm(x, w):
    xf = x.astype(jnp.float32)
    y = xf * lax.rsqrt(jnp.mean(xf * xf, axis=-1, keepdims=True) + NORM_EPS)
    return (y * w.astype(jnp.float32)).astype(x.dtype)


def _rope_tables(T, dim):
    inv_freq = 1.0 / (ROPE_THETA ** (jnp.arange(0, dim, 2, dtype=jnp.float32) / dim))
    ang = jnp.arange(T, dtype=jnp.float32)[:, None] * inv_freq[None, :]
    ang = jnp.concatenate([ang, ang], axis=-1)
    return jnp.cos(ang), jnp.sin(ang)


def _apply_rope(x, cos, sin):
    half = x.shape[-1] // 2
    rot = jnp.concatenate([-x[..., half:], x[..., :half]], axis=-1)
    c = cos[None, :, None, None, :].astype(x.dtype)
    s = sin[None, :, None, None, :].astype(x.dtype)
    return x * c + rot * s


def _diff_attention(q, k, v, lam):
    T = q.shape[1]
    q = q * (HEAD_DIM ** -0.5)
    outs = []
    for start in range(0, T, Q_BLOCK):
        end = min(start + Q_BLOCK, T)
        qb = q[:, start:end]
        kb = k[:, :end]
        s = jnp.einsum('bqhmd,bkhmd->bhmqk', qb, kb).astype(jnp.float32)
        mask = jnp.arange(start, end)[:, None] >= jnp.arange(end)[None, :]
        s = jnp.where(mask[None, None, None], s, -jnp.inf)
        p = jax.nn.softmax(s, axis=-1)
        a = p[:, :, 0] - lam.astype(jnp.float32) * p[:, :, 1]
        outs.append(jnp.einsum('bhqk,bkhe->bqhe', a.astype(v.dtype), v[:, :end]))
    return jnp.concatenate(outs, axis=1)


def _causal_depthwise_conv(u, w):
    T = u.shape[1]
    up = jnp.pad(u, ((0, 0), (CONV_K - 1, 0), (0, 0)))
    return sum(up[:, j:j + T] * w[j].astype(u.dtype) for j in range(CONV_K))


def setup_inputs(seed: int = 0) -> dict:
    key = jax.random.key(seed)
    ks = jax.random.split(key, 17)
    f32 = jnp.float32
    nrm = lambda k, shape, scale: (jax.random.normal(k, shape, f32) * scale)
    return {
        "x": nrm(ks[0], (BATCH, SEQ, D_MODEL), 1.0),
        "meta_tokens": nrm(ks[1], (N_META, D_MODEL), 1.0),
        "norm_mix_w": 1.0 + nrm(ks[2], (DEPTH, D_MODEL), 0.02),
        "w_in": nrm(ks[3], (DEPTH, D_MODEL, IN_COLS), D_MODEL ** -0.5),
        "lambda_q1": nrm(ks[4], (DEPTH, HEAD_DIM), 0.1),
        "lambda_k1": nrm(ks[5], (DEPTH, HEAD_DIM), 0.1),
        "lambda_q2": nrm(ks[6], (DEPTH, HEAD_DIM), 0.1),
        "lambda_k2": nrm(ks[7], (DEPTH, HEAD_DIM), 0.1),
        "subln_w": 1.0 + nrm(ks[8], (DEPTH, V_HEAD_DIM), 0.02),
        "conv_w": nrm(ks[9], (DEPTH, CONV_K, CONV_WIDTH), CONV_K ** -0.5),
        "w_proj_attn": nrm(ks[10], (DEPTH, ATTN_WIDTH, D_MODEL), ATTN_WIDTH ** -0.5),
        "w_proj_conv": nrm(ks[11], (DEPTH, CONV_WIDTH, D_MODEL), CONV_WIDTH ** -0.5),
        "w_out": nrm(ks[12], (DEPTH, D_MODEL, D_MODEL), D_MODEL ** -0.5),
        "norm_ffn_w": 1.0 + nrm(ks[13], (DEPTH, D_MODEL), 0.02),
        "w_gate_up": nrm(ks[14], (DEPTH, D_MODEL, 2 * D_FF), D_MODEL ** -0.5),
        "w_down": nrm(ks[15], (DEPTH, D_FF, D_MODEL), D_FF ** -0.5),
        "norm_final_w": 1.0 + nrm(ks[16], (D_MODEL,), 0.02),
    }


def reference(x, meta_tokens, norm_mix_w, w_in, lambda_q1, lambda_k1, lambda_q2, lambda_k2,
              subln_w, conv_w, w_proj_attn, w_proj_conv, w_out, norm_ffn_w, w_gate_up,
              w_down, norm_final_w):
    B = x.shape[0]
    meta = jnp.broadcast_to(meta_tokens.astype(x.dtype)[None], (B, N_META, D_MODEL))
    h = jnp.concatenate([meta, x], axis=1)
    T = h.shape[1]
    cos, sin = _rope_tables(T, HEAD_DIM)
    split_at = np.cumsum([Q_COLS, K_COLS, V_COLS, CONV_WIDTH, CONV_WIDTH, CONV_WIDTH, D_MODEL])

    for l in range(DEPTH):
        lambda_init = 0.8 - 0.6 * math.exp(-0.3 * l)
        hn = _rmsnorm(h, norm_mix_w[l])
        proj = jnp.einsum('btd,dc->btc', hn, w_in[l])
        q, k, v, cb, cc, cx, ga, gb = jnp.split(proj, split_at, axis=-1)

        q = _apply_rope(q.reshape(B, T, N_HEADS, 2, HEAD_DIM), cos, sin)
        k = _apply_rope(k.reshape(B, T, N_HEADS, 2, HEAD_DIM), cos, sin)
        v = v.reshape(B, T, N_HEADS, V_HEAD_DIM)
        lam = (jnp.exp(jnp.sum(lambda_q1[l] * lambda_k1[l]))
               - jnp.exp(jnp.sum(lambda_q2[l] * lambda_k2[l])) + lambda_init)
        oa = _diff_attention(q, k, v, lam)
        oa = _rmsnorm(oa, subln_w[l]) * (1.0 - lambda_init)
        ya = jnp.einsum('btc,cd->btd', oa.reshape(B, T, ATTN_WIDTH), w_proj_attn[l])

        ob = cb * _causal_depthwise_conv(cc * cx, conv_w[l])
        yb = jnp.einsum('btc,cd->btd', ob, w_proj_conv[l])

        merged = jax.nn.sigmoid(ga) * ya + jax.nn.sigmoid(gb) * yb
        h = h + jnp.einsum('btd,de->bte', merged, w_out[l])

        hn = _rmsnorm(h, norm_ffn_w[l])
        g, u = jnp.split(jnp.einsum('btd,df->btf', hn, w_gate_up[l]), 2, axis=-1)
        h = h + jnp.einsum('btf,fd->btd', jax.nn.silu(g) * u, w_down[l])

    return _rmsnorm(h[:, N_META:], norm_final_w)
```

```python
import math
import numpy as np
import ml_dtypes
from contextlib import ExitStack
from collections import defaultdict
import concourse.bass as bass
import concourse.mybir as mybir
from concourse.bass_utils import run_bass_kernel_spmd

F32 = mybir.dt.float32
BF16 = mybir.dt.bfloat16
AF = mybir.ActivationFunctionType
ALU = mybir.AluOpType

PE, ACT, DVE, POOL, SP = "tensor", "scalar", "vector", "gpsimd", "sync"
ENGS = (PE, ACT, DVE, POOL, SP)

D = 1024
KC = 8
NH = 8
NM = 16
TQ = 512
DFF = 2816
NJ = 22
EPS = 1e-5
LAMBDA_INIT = 0.8 - 0.6 * math.exp(-0.3 * 0)
N_CORES = 8
BATCH = 32
SEQ = 2048

U_GB, U_CC, U_CX, U_CB, U_GA, U_Q, U_K, U_V = 0, 2, 4, 6, 8, 10, 12, 14
U_WPC, U_WPA, U_WO, U_WGU, U_WD = 16, 18, 20, 22, 33
NU = 39
NSLOT = 3


class _Op:
    __slots__ = ("eng", "fn", "waits", "semkey", "value", "is_dma", "idx")


class Sched:
    def __init__(self, nc):
        self.nc = nc
        self.ops = []
        self.last_writer = {}
        self.readers = defaultdict(list)
        self.overlaps = defaultdict(list)
        self.count = defaultdict(int)
        self.seen = {e: {} for e in ENGS}
        self.dma_keys = []
        self.psum_keys = set()
        self.access = defaultdict(dict)

    def alias(self, a_keys, b_keys):
        for a in a_keys:
            for b in b_keys:
                self.overlaps[a].append(b)
                self.overlaps[b].append(a)

    def add(self, eng, fn, reads=(), writes=(), dma=None):
        op = _Op()
        op.eng, op.fn, op.is_dma = eng, fn, dma is not None
        op.idx = len(self.ops)
        deps = {}

        def dep(o, raw):
            if o is None or o is op:
                return
            same = (o.eng == eng) and not o.is_dma and not op.is_dma
            if same and (eng == PE or eng == SP or not raw):
                return
            deps[o.idx] = o

        for k in reads:
            dep(self.last_writer.get(k), True)
        for k in writes:
            for kk in [k] + self.overlaps.get(k, []):
                dep(self.last_writer.get(kk), False)
                for r in self.readers.get(kk, ()):
                    dep(r, False)
        for k in list(reads) + list(writes):
            if k in self.psum_keys:
                for kk in [k] + self.overlaps.get(k, []):
                    for e2, o in self.access[kk].items():
                        if e2 != eng:
                            deps[o.idx] = o
                self.access[k][eng] = op
        waits = {}
        for o in deps.values():
            if waits.get(o.semkey, 0) < o.value:
                waits[o.semkey] = o.value
        seen = self.seen[eng]
        op.waits = []
        for sk, v in waits.items():
            if seen.get(sk, 0) < v:
                seen[sk] = v
                op.waits.append((sk, v))
        if op.is_dma:
            op.semkey = "dma_" + dma
            if op.semkey not in self.count:
                self.dma_keys.append(op.semkey)
            self.count[op.semkey] += 16
        else:
            op.semkey = "eng_" + eng
            self.count[op.semkey] += 1
        op.value = self.count[op.semkey]
        for k in reads:
            self.readers[k].append(op)
        for k in writes:
            self.last_writer[k] = op
            self.readers[k] = []
            for kk in self.overlaps.get(k, []):
                self.last_writer[kk] = op
                self.readers[kk] = []
        self.ops.append(op)
        return op

    def emit(self, final_dma_keys=()):
        nc = self.nc
        with ExitStack() as st:
            sems = {}
            for e in ENGS:
                sems["eng_" + e] = st.enter_context(nc.semaphore("s_" + e))
            for k in self.dma_keys:
                sems[k] = st.enter_context(nc.semaphore("s_" + k))
            block = st.enter_context(nc.Block())
            per = {e: [o for o in self.ops if o.eng == e] for e in ENGS}
            finals = [("dma_" + k, self.count["dma_" + k]) for k in final_dma_keys if ("dma_" + k) in sems]

            def make(e):
                def body(eng):
                    for o in per[e]:
                        for sk, v in o.waits:
                            eng.wait_ge(sems[sk], v)
                        ins = o.fn(eng)
                        ins.then_inc(sems[o.semkey], 16 if o.is_dma else 1)
                    if e == SP:
                        for sk, v in finals:
                            eng.wait_ge(sems[sk], v)
                return body

            for e in ENGS:
                if per[e] or e == SP:
                    getattr(block, e)(make(e))
        return nc


STOP = None
ROPE_ADD_DVE = False


def build(nseq, nch):
    T = nch * TQ
    NKB = T // 128
    nc = bass.Bass("TRN2", target_bir_lowering=False)

    def din(name, shape, dt=F32):
        return nc.dram_tensor(name, list(shape), dt, kind="ExternalInput").ap()

    x_d = din("x", [nseq, T, D])
    meta_d = din("meta", [NM, D])
    wun_d = din("wun", [NU, 128, 4096])
    cos_d = din("cos", [128, NM + T])
    sin_d = din("sin", [128, NM + T])
    ident_d = din("ident", [128, 128], BF16)
    perm_d = din("perm", [128, 128], BF16)
    mask_d = din("mask", [128, 128], BF16)
    nmw_d = din("nmw", [128, KC])
    nfw_d = din("nfw", [128, KC])
    convw_d = din("convw", [128, KC * 3])
    subw_d = din("subw", [128, 1])
    nfin_d = din("nfin", [128, D])
    lam_d = din("lam", [128, 4 * 64])
    y_d = nc.dram_tensor("y", [nseq, T, D], F32, kind="ExternalOutput").ap()
    scr_d = nc.dram_tensor("wscr", [NU, 128, 4096], BF16, kind="Internal").ap()

    S = Sched(nc)
    with ExitStack() as st:
        def sb(name, shape, dt):
            return st.enter_context(nc.sbuf_tensor("sb_" + name, list(shape), dt))

        def ps(name, shape, dt):
            return st.enter_context(nc.psum_tensor(name, list(shape), dt))

        KT = sb("KT", [128, NH * T], BF16)
        KTv = KT[:].rearrange("p (h t) -> p h t", h=NH)
        KTm = sb("KTm", [128, NH, NM], BF16)
        Vc = sb("Vc", [128, NKB, NH, 129], BF16)
        Vm = sb("Vm", [NM, NH, 129], BF16)
        cosb = sb("cosb", [128, TQ], F32)
        sinb = sb("sinb", [128, TQ], F32)
        ident = sb("ident", [128, 128], BF16)
        perm = sb("perm", [128, 128], BF16)
        mask = sb("mask", [128, 128], BF16)
        nmw = sb("nmw", [128, KC], F32)
        nfw = sb("nfw", [128, KC], F32)
        convw = sb("convw", [128, KC * 3], F32)
        sws = sb("sws", [128, 1], F32)
        nfin = sb("nfin", [128, D], F32)
        lamv = sb("lamv", [128, 4 * 64], F32)
        lamt = sb("lamt", [128, 2 * 64], F32)
        lams = sb("lams", [128, 4], F32)
        nlam = sb("nlam", [128, 1], F32)
        nhalf = sb("nhalf", [128, 1], F32)
        h = sb("h", [128, 4, D], F32)
        xs = sb("xs", [128, D], BF16)
        hnT = sb("hnT", [128, KC, TQ], BF16)
        arena = sb("arena", [128, 4096 + 4096 + KC * 514], BF16)
        QT = arena[:, 0:4096].rearrange("p (h t) -> p h t", h=NH)
        cob = arena[:, 4096:8192].rearrange("p (c t) -> p c t", c=KC)
        ubuf = arena[:, 8192:8192 + KC * 514].rearrange("p (c t) -> p c t", c=KC)
        actT = arena[:, 0:NJ * TQ].rearrange("p (j t) -> p j t", j=NJ)
        ta = sb("ta", [128, KC, TQ], BF16)
        tb = sb("tb", [128, KC, TQ], BF16)
        oaT = sb("oaT", [128, NH, TQ], BF16)
        Eb = [sb("E%d" % i, [128, 2, TQ], BF16) for i in range(2)]
        qkb = sb("qkb", [128, TQ], BF16)
        rt1 = sb("rt1", [128, TQ], F32)
        rt2 = sb("rt2", [128, TQ], F32)
        osb = sb("osb", [128, 4, 128], F32)
        ttmp = sb("ttmp", [128, 128], F32)
        oab = sb("oab", [128, 4, 128], BF16)
        junk = sb("junk", [128, D], BF16)
        swt = sb("swt", [128, TQ], BF16)
        swa = sb("swa", [128, TQ], BF16)
        cv = sb("cv", [128, TQ], F32)
        m1 = sb("m1", [128, TQ], BF16)
        stat = sb("stat", [128, 64], F32)
        um = sb("um", [128, KC, NM], BF16)
        uhs = sb("uhs", [128, KC, 2], BF16)
        ccm = sb("ccm", [128, KC, NM], F32)
        wring = sb("wring", [128, NSLOT, 4096], BF16)
        if NH * T // 2 >= 8192:
            stg = KT[:].bitcast(F32)
        else:
            stg = sb("stg", [128, 8192], F32)[:]

        pS = [ps("pS%d" % i, [128, 2, 512], F32) for i in range(2)]
        pO = ps("pO", [128, 3, 512], F32)
        pX = ps("pX", [128, 512], F32)
        banks = [pS[0][:, 0, :], pS[0][:, 1, :], pS[1][:, 0, :], pS[1][:, 1, :],
                 pO[:, 0, :], pO[:, 1, :], pO[:, 2, :], pX[:]]
        bkey = ["S0", "S0", "S1", "S1", "O", "O", "O", "X"]
        pXb = pX[:].bitcast(BF16)

        acc_state = {"i": 0}
        ACCS = [("b%d" % i, banks[i]) for i in range(7)]
        S.alias(["b0", "b1"], ["S0"])
        S.alias(["b2", "b3"], ["S1"])
        S.alias(["b4", "b5", "b6"], ["O"])

        S.alias([("actT", j) for j in range(NJ)],
                [("QT", q) for q in range(NH)] + [("cob", q) for q in range(KC)] + [("u", q) for q in range(KC)] + [("uh", q) for q in range(KC)])
        S.alias([("stg", 0), ("stg", 1)], [("KT", q, cc_) for q in range(NH) for cc_ in range(nch)])
        for sl_ in range(NSLOT):
            S.alias([("w", sl_)], [("wx", sl_, kc_) for kc_ in range(KC)])
        S.alias(["h0"], [("h", 0)])
        S.alias(["hnT0"], [("hnT", 0)])

        S.psum_keys = set(["S0", "S1", "O", "X"] + ["b%d" % i for i in range(7)])

        def next_acc():
            i = acc_state["i"]
            acc_state["i"] = (i + 1) % 7
            return ACCS[i]

        wseq = []
        wst = {"issued": 0, "cons": 0}

        def wget(u):
            n = wst["cons"]
            assert wseq[n] == u, (n, wseq[n], u)
            while wst["issued"] < min(len(wseq), n + NSLOT):
                i = wst["issued"]
                slot = i % NSLOT
                uu = wseq[i]
                S.add(SP, lambda e, slot=slot, uu=uu: e.dma_start(out=wring[:, slot, :], in_=scr_d[uu]),
                      reads=[("scr", uu)], writes=[("w", slot)], dma="w%d" % slot)
                wst["issued"] += 1
            wst["cons"] += 1
            slot = n % NSLOT
            return wring[:, slot, :], ("w", slot)

        meta_units = [U_CC, U_CC + 1, U_CX, U_CX + 1, U_K, U_K + 1, U_V, U_V + 1]
        chunk_units = list(range(NU))
        wseq.extend(meta_units)
        for _ in range(nseq * nch):
            wseq.extend(chunk_units)

        def cload(dst, src, key):
            S.add(SP, lambda e: e.dma_start(out=dst, in_=src), writes=[key], dma="c_" + key)

        cload(ident[:], ident_d, "ident")
        cload(perm[:], perm_d, "perm")
        cload(mask[:], mask_d, "mask")
        cload(nmw[:], nmw_d, "nmw")
        cload(nfw[:], nfw_d, "nfw")
        cload(convw[:], convw_d, "convw")
        cload(sws[:], subw_d, "sws0")
        cload(nfin[:], nfin_d, "nfin")
        cload(lamv[:], lam_d, "lamv")
        S.add(POOL, lambda e: e.memset(nhalf[:], -0.5), writes=["nhalf"])
        S.add(POOL, lambda e: e.memset(Vc[:, :, :, 128:129], 1.0), writes=["Vones"])
        S.add(POOL, lambda e: e.memset(Vm[:, :, 128:129], 1.0), writes=["Vmones"])
        S.add(DVE, lambda e: e.tensor_scalar(out=sws[:], in0=sws[:], scalar1=float(1.0 - LAMBDA_INIT), scalar2=None,
                                             op0=ALU.mult), reads=["sws0"], writes=["sws"])
        lv = lamv[:].rearrange("p (a b) -> p a b", a=4)
        lt = lamt[:].rearrange("p (a b) -> p a b", a=2)
        S.add(DVE, lambda e: e.tensor_tensor(out=lt[:, 0, :], in0=lv[:, 0, :], in1=lv[:, 1, :], op=ALU.mult),
              reads=["lamv"], writes=["lt0"])
        S.add(DVE, lambda e: e.tensor_tensor(out=lt[:, 1, :], in0=lv[:, 2, :], in1=lv[:, 3, :], op=ALU.mult),
              reads=["lamv"], writes=["lt1"])
        S.add(DVE, lambda e: e.tensor_reduce(out=lams[:, 0:2], in_=lt, op=ALU.add, axis=mybir.AxisListType.X),
              reads=["lt0", "lt1"], writes=["lams01"])
        S.add(ACT, lambda e: e.activation(out=lams[:, 2:4], in_=lams[:, 0:2], func=AF.Exp),
              reads=["lams01"], writes=["lams23"])
        S.add(DVE, lambda e: e.scalar_tensor_tensor(out=nlam[:], in0=lams[:, 3:4], scalar=float(-LAMBDA_INIT),
                                                    in1=lams[:, 2:3], op0=ALU.add, op1=ALU.subtract),
              reads=["lams23"], writes=["nlam"])

        for u in range(NU):
            sl = u % 2
            sv = stg[:, sl * 4096:(sl + 1) * 4096]
            S.add(SP, lambda e, sv=sv, u=u: e.dma_start(out=sv, in_=wun_d[u]), writes=[("stg", sl)], dma="stg%d" % sl)
            slot = u % NSLOT
            dst = wring[:, slot, :]
            if u < U_WPC or (U_WGU <= u < U_WD):
                sc = nmw if u < U_WPC else nfw
                sck = "nmw" if u < U_WPC else "nfw"
                for kc in range(KC):
                    o_ = dst[:, kc * 512:(kc + 1) * 512]
                    i_ = sv[:, kc * 512:(kc + 1) * 512]
                    if (u + kc) % 2 == 0:
                        S.add(ACT, lambda e, o_=o_, i_=i_, kc=kc, sc=sc: e.activation(out=o_, in_=i_, func=AF.Copy, scale=sc[:, kc:kc + 1]),
                              reads=[("stg", sl), sck], writes=[("wx", slot, kc)])
                    else:
                        S.add(DVE, lambda e, o_=o_, i_=i_, kc=kc, sc=sc: e.tensor_scalar(out=o_, in0=i_, scalar1=sc[:, kc:kc + 1], scalar2=None, op0=ALU.mult),
                              reads=[("stg", sl), sck], writes=[("wx", slot, kc)])
                rk = [("wx", slot, kc) for kc in range(KC)]
            else:
                for hf in range(2):
                    o_ = dst[:, hf * 2048:(hf + 1) * 2048]
                    i_ = sv[:, hf * 2048:(hf + 1) * 2048]
                    if hf == 0:
                        S.add(ACT, lambda e, o_=o_, i_=i_: e.activation(out=o_, in_=i_, func=AF.Copy),
                              reads=[("stg", sl)], writes=[("wx", slot, 0)])
                    else:
                        S.add(DVE, lambda e, o_=o_, i_=i_: e.tensor_copy(out=o_, in_=i_),
                              reads=[("stg", sl)], writes=[("wx", slot, 1)])
                rk = [("wx", slot, 0), ("wx", slot, 1)]
            S.add(SP, lambda e, dst=dst, u=u: e.dma_start(out=scr_d[u], in_=dst), reads=rk, writes=[("scr", u)], dma="scrw%d" % slot)

        def fm_tile(wv, wk, ci, rhsT, rkeys, n, acc=None):
            ak, ab = acc if acc is not None else next_acc()

            def f(e):
                ins = None
                for kc in range(KC):
                    ins = e.matmul(ab[:, 0:n], lhsT=wv[:, kc * 512 + ci * 128: kc * 512 + (ci + 1) * 128],
                                   rhs=rhsT[:, kc, 0:n], start=(kc == 0), stop=(kc == KC - 1))
                return ins
            S.add(PE, f, reads=[wk] + list(rkeys), writes=[ak])
            return ak, ab

        def rope_tile(ak, ab, n, cs, sn, cskeys, dst, dkey):
            import os
            dbg = os.environ.get("ROPEDBG", "")
            if "a" not in dbg:
                S.add(ACT, lambda e: e.activation(out=qkb[:, 0:n], in_=ab[:, 0:n], func=AF.Copy), reads=[ak], writes=["qkb"])
            if "b" not in dbg:
                S.add(PE, lambda e: e.matmul(pX[:, 0:n], lhsT=perm[:], rhs=qkb[:, 0:n], start=True, stop=True),
                      reads=["qkb", "perm"], writes=["X"])
            if "c" not in dbg:
                S.add(DVE, lambda e: e.tensor_tensor(out=rt1[:, 0:n], in0=ab[:, 0:n], in1=cs, op=ALU.mult),
                      reads=[ak] + cskeys, writes=["rt1"])
            if "d" not in dbg:
                S.add(DVE, lambda e: e.tensor_tensor(out=rt2[:, 0:n], in0=pX[:, 0:n], in1=sn, op=ALU.mult),
                      reads=["X"] + cskeys, writes=["rt2"])
            if "e" not in dbg:
                S.add(DVE if ROPE_ADD_DVE else POOL, lambda e: e.tensor_tensor(out=dst, in0=rt1[:, 0:n], in1=rt2[:, 0:n], op=ALU.add),
                      reads=["rt1", "rt2"], writes=[dkey])

        def norm_tile(src, skey, npart, dst_hnT, dkeys, col0):
            S.add(ACT, lambda e: e.activation(out=junk[0:npart, :], in_=src, func=AF.Square, accum_out=stat[0:npart, 0:1]),
                  reads=[skey], writes=["junk", "st0"])
            S.add(DVE, lambda e: e.tensor_scalar(out=stat[0:npart, 1:2], in0=stat[0:npart, 0:1], scalar1=1.0 / D, scalar2=EPS,
                                                 op0=ALU.mult, op1=ALU.add), reads=["st0"], writes=["st1"])
            S.add(POOL, lambda e: e.tensor_tensor(out=stat[0:npart, 2:3], in0=stat[0:npart, 1:2], in1=nhalf[0:npart, :], op=ALU.pow),
                  reads=["st1", "nhalf"], writes=["st2"])
            S.add(ACT, lambda e: e.activation(out=xs[0:npart, :], in_=src, func=AF.Copy, scale=stat[0:npart, 2:3]),
                  reads=[skey, "st2"], writes=["xs"])

            def f(e):
                ins = None
                for kc in range(KC):
                    ins = e.transpose(pXb[:, kc * 128: kc * 128 + npart], xs[0:npart, kc * 128:(kc + 1) * 128], ident[0:npart, 0:npart])
                return ins
            S.add(PE, f, reads=["xs", "ident"], writes=["X"])
            S.add(DVE, lambda e: e.tensor_copy(out=dst_hnT[:, :, col0:col0 + npart],
                                               in_=pXb.rearrange("p (k t) -> p k t", k=KC)[:, :, 0:npart]),
                  reads=["X"], writes=dkeys)

        def dump():
            items = [("QT", arena[:, 0:4096], [128, 4096], BF16), ("cob", arena[:, 4096:8192], [128, 4096], BF16),
                     ("ub", arena[:, 8192:8192 + KC * 514], [128, KC * 514], BF16),
                     ("KT", KT[:, 0:4096], [128, 4096], BF16), ("Vc", Vc[:, 0:4, :, :].rearrange("p a b c -> p (a b c)"), [128, 4 * NH * 129], BF16),
                     ("ta", ta[:].rearrange("p a b -> p (a b)"), [128, 4096], BF16), ("tb", tb[:].rearrange("p a b -> p (a b)"), [128, 4096], BF16),
                     ("oaT", oaT[:].rearrange("p a b -> p (a b)"), [128, 4096], BF16), ("hnT", hnT[:].rearrange("p a b -> p (a b)"), [128, 4096], BF16),
                     ("h", h[:].rearrange("p a b -> p (a b)"), [128, 4 * D], F32), ("stat", stat[:], [128, 64], F32)]
            import itertools
            allk = list(S.last_writer.keys())
            for name, ap_, shp, dt_ in items:
                dd = nc.dram_tensor("dbg_" + name, shp, dt_, kind="ExternalOutput").ap()
                S.add(SP, lambda e, dd=dd, ap_=ap_: e.dma_start(out=dd, in_=ap_), reads=allk, dma="dbg_" + name)
            S.emit(final_dma_keys=["dbg_" + it[0] for it in items])

        if STOP == "pro":
            S.emit()
            return nc
        S.add(SP, lambda e: e.dma_start(out=h[0:NM, 0, :], in_=meta_d), writes=["h0"], dma="x0")
        S.add(SP, lambda e: e.dma_start(out=cosb[:, 0:NM], in_=cos_d[:, 0:NM]), writes=["cos"], dma="cos")
        S.add(SP, lambda e: e.dma_start(out=sinb[:, 0:NM], in_=sin_d[:, 0:NM]), writes=["sin"], dma="sin")
        norm_tile(h[0:NM, 0, :], "h0", NM, hnT, ["hnT0"], 0)
        for t8 in range(8):
            if t8 % 4 == 0:
                wv, wk = wget(U_CC + t8 // 4)
            ak, ab = fm_tile(wv, wk, t8 % 4, hnT, ["hnT0"], NM)
            S.add(ACT, lambda e, ab=ab, t8=t8: e.activation(out=ccm[:, t8, :], in_=ab[:, 0:NM], func=AF.Copy),
                  reads=[ak], writes=[("ccm", t8)])
        for t8 in range(8):
            if t8 % 4 == 0:
                wv, wk = wget(U_CX + t8 // 4)
            ak, ab = fm_tile(wv, wk, t8 % 4, hnT, ["hnT0"], NM)
            S.add(DVE, lambda e, ab=ab, t8=t8: e.tensor_tensor(out=um[:, t8, :], in0=ab[:, 0:NM], in1=ccm[:, t8, :], op=ALU.mult),
                  reads=[ak, ("ccm", t8)], writes=[("um", t8)])
        for t8 in range(8):
            if t8 % 4 == 0:
                wv, wk = wget(U_K + t8 // 4)
            ak, ab = fm_tile(wv, wk, t8 % 4, hnT, ["hnT0"], NM)
            rope_tile(ak, ab, NM, cosb[:, 0:NM], sinb[:, 0:NM], ["cos", "sin"], KTm[:, t8, :], ("KTm", t8))
        for hf in range(2):
            wv, wk = wget(U_V + hf)
            ak, ab = next_acc()

            def f(e, wv=wv, ab=ab):
                ins = None
                for kc in range(KC):
                    ins = e.matmul(ab[0:NM, :], lhsT=hnT[:, kc, 0:NM], rhs=wv[:, kc * 512:(kc + 1) * 512],
                                   start=(kc == 0), stop=(kc == KC - 1))
                return ins
            S.add(PE, f, reads=[wk, "hnT0"], writes=[ak])
            S.add(ACT, lambda e, ab=ab, hf=hf: e.activation(out=Vm[:, hf * 4:(hf + 1) * 4, 0:128],
                                                           in_=ab[0:NM, :].rearrange("p (a b) -> p a b", a=4), func=AF.Copy),
                  reads=[ak, "Vmones"], writes=[("Vm", hf)])

        if STOP == "meta":
            S.emit()
            return nc
        mview = mask[:]
        def do_chunk(s, c):
            if True:
                t0 = c * TQ
                for tt in range(4):
                    S.add(SP, lambda e, tt=tt: e.dma_start(out=h[:, tt, :], in_=x_d[s, t0 + tt * 128: t0 + (tt + 1) * 128, :]),
                          writes=[("h", tt)], dma="x%d" % tt)
                S.add(SP, lambda e: e.dma_start(out=cosb[:], in_=cos_d[:, NM + t0: NM + t0 + TQ]), writes=["cos"], dma="cos")
                S.add(SP, lambda e: e.dma_start(out=sinb[:], in_=sin_d[:, NM + t0: NM + t0 + TQ]), writes=["sin"], dma="sin")
                for tt in range(4):
                    norm_tile(h[:, tt, :], ("h", tt), 128, hnT, [("hnT", tt)], tt * 128)
                hkeys = [("hnT", tt) for tt in range(4)]

                if STOP == "p1":
                    return
                for t8 in range(8):
                    if t8 % 4 == 0:
                        wv, wk = wget(U_GB + t8 // 4)
                    ak, ab = fm_tile(wv, wk, t8 % 4, hnT, hkeys, TQ)
                    S.add(ACT, lambda e, ab=ab, t8=t8: e.activation(out=tb[:, t8, :], in_=ab, func=AF.Tanh, scale=0.5),
                          reads=[ak], writes=[("tb", t8)])
                if STOP == "p2a":
                    return
                for t8 in range(8):
                    if t8 % 4 == 0:
                        wv, wk = wget(U_CC + t8 // 4)
                    ak, ab = fm_tile(wv, wk, t8 % 4, hnT, hkeys, TQ)
                    S.add(ACT, lambda e, ab=ab, t8=t8: e.activation(out=cob[:, t8, :], in_=ab, func=AF.Copy),
                          reads=[ak], writes=[("cob", t8)])
                if STOP == "p2b":
                    return
                for t8 in range(8):
                    if t8 % 4 == 0:
                        wv, wk = wget(U_CX + t8 // 4)
                    ak, ab = fm_tile(wv, wk, t8 % 4, hnT, hkeys, TQ)
                    if c == 0:
                        S.add(POOL, lambda e, t8=t8: e.tensor_copy(out=ubuf[:, t8, 0:2], in_=um[:, t8, NM - 2:NM]),
                              reads=[("um", t8)], writes=[("uh", t8)])
                    else:
                        S.add(POOL, lambda e, t8=t8: e.tensor_copy(out=ubuf[:, t8, 0:2], in_=uhs[:, t8, :]),
                              reads=[("uhs", t8)], writes=[("uh", t8)])
                    S.add(DVE, lambda e, ab=ab, t8=t8: e.tensor_tensor(out=ubuf[:, t8, 2:2 + TQ], in0=ab, in1=cob[:, t8, :], op=ALU.mult),
                          reads=[ak, ("cob", t8), ("uh", t8)], writes=[("u", t8)])
                    S.add(POOL, lambda e, t8=t8: e.tensor_copy(out=uhs[:, t8, :], in_=ubuf[:, t8, TQ:TQ + 2]),
                          reads=[("u", t8)], writes=[("uhs", t8)])
                for t8 in range(8):
                    if t8 % 4 == 0:
                        wv, wk = wget(U_CB + t8 // 4)
                    ak, ab = fm_tile(wv, wk, t8 % 4, hnT, hkeys, TQ)
                    S.add(DVE, lambda e, t8=t8: e.tensor_scalar(out=cv[:], in0=ubuf[:, t8, 0:TQ], scalar1=convw[:, t8 * 3:t8 * 3 + 1],
                                                                 scalar2=None, op0=ALU.mult),
                          reads=[("u", t8), ("uh", t8), "convw"], writes=["cv"])
                    S.add(DVE, lambda e, t8=t8: e.scalar_tensor_tensor(out=cv[:], in0=ubuf[:, t8, 1:1 + TQ], scalar=convw[:, t8 * 3 + 1:t8 * 3 + 2],
                                                                        in1=cv[:], op0=ALU.mult, op1=ALU.add),
                          reads=[("u", t8), ("uh", t8), "convw", "cv"], writes=["cv"])
                    S.add(DVE, lambda e, t8=t8: e.scalar_tensor_tensor(out=cv[:], in0=ubuf[:, t8, 2:2 + TQ], scalar=convw[:, t8 * 3 + 2:t8 * 3 + 3],
                                                                        in1=cv[:], op0=ALU.mult, op1=ALU.add),
                          reads=[("u", t8), "convw", "cv"], writes=["cv"])
                    S.add(DVE, lambda e, ab=ab, t8=t8: e.tensor_tensor(out=cob[:, t8, :], in0=ab, in1=cv[:], op=ALU.mult),
                          reads=[ak, "cv"], writes=[("cob", t8)])
                if STOP == "p2d":
                    return
                for t8 in range(8):
                    if t8 % 4 == 0:
                        wv, wk = wget(U_GA + t8 // 4)
                    ak, ab = fm_tile(wv, wk, t8 % 4, hnT, hkeys, TQ)
                    S.add(ACT, lambda e, ab=ab, t8=t8: e.activation(out=ta[:, t8, :], in_=ab, func=AF.Tanh, scale=0.5),
                          reads=[ak], writes=[("ta", t8)])
                if STOP == "p2e":
                    return
                for t8 in range(8):
                    if t8 % 4 == 0:
                        wv, wk = wget(U_Q + t8 // 4)
                    ak, ab = fm_tile(wv, wk, t8 % 4, hnT, hkeys, TQ)
                    rope_tile(ak, ab, TQ, cosb[:], sinb[:], ["cos", "sin"], QT[:, t8, :], ("QT", t8))
                if STOP == "p2f":
                    return
                for t8 in range(8):
                    if t8 % 4 == 0:
                        wv, wk = wget(U_K + t8 // 4)
                    ak, ab = fm_tile(wv, wk, t8 % 4, hnT, hkeys, TQ)
                    rope_tile(ak, ab, TQ, cosb[:], sinb[:], ["cos", "sin"], KTv[:, t8, t0:t0 + TQ], ("KT", t8, c))
                if STOP == "p2g":
                    return
                for hf in range(2):
                    wv, wk = wget(U_V + hf)
                    for tt in range(4):
                        ak, ab = next_acc()

                        def f(e, wv=wv, ab=ab, tt=tt):
                            ins = None
                            for kc in range(KC):
                                ins = e.matmul(ab, lhsT=hnT[:, kc, tt * 128:(tt + 1) * 128], rhs=wv[:, kc * 512:(kc + 1) * 512],
                                               start=(kc == 0), stop=(kc == KC - 1))
                            return ins
                        S.add(PE, f, reads=[wk, ("hnT", tt)], writes=[ak])
                        kb = c * 4 + tt
                        S.add(ACT, lambda e, ab=ab, hf=hf, kb=kb: e.activation(out=Vc[:, kb, hf * 4:(hf + 1) * 4, 0:128],
                                                                             in_=ab.rearrange("p (a b) -> p a b", a=4), func=AF.Copy),
                              reads=[ak, "Vones"], writes=[("V", kb, hf)])

                if STOP == "p2":
                    return
                def do_head(hd):
                    kbl = [("m", 0, 0)] + [("f", kb, 0) for kb in range(4 * c)] + [("d", 4 * c + i, 128 * i) for i in range(4)]
                    nk = len(kbl)
                    started = set()

                    def qk_step(i):
                        kind, kb, q0 = kbl[i]
                        pSi = pS[i % 2]
                        E = Eb[i % 2]
                        nkeys = NM if kind == "m" else 128

                        def f(e, kind=kind, kb=kb, q0=q0, pSi=pSi, nkeys=nkeys):
                            ins = None
                            for sub in range(2):
                                r0 = sub * 64
                                if kind == "m":
                                    lt_ = KTm[r0:r0 + 64, hd, :]
                                else:
                                    lt_ = KTv[r0:r0 + 64, hd, kb * 128:(kb + 1) * 128]
                                ins = e.matmul(pSi[0:nkeys, sub, q0:TQ], lhsT=lt_, rhs=QT[r0:r0 + 64, hd, q0:TQ], start=True, stop=True)
                            return ins
                        rk = [("QT", hd)] + ([("KTm", hd)] if kind == "m" else [("KT", hd, kb // 4)])
                        S.add(PE, f, reads=rk, writes=["S%d" % (i % 2)])
                        S.add(ACT, lambda e, pSi=pSi, E=E, q0=q0, nkeys=nkeys: e.activation(out=E[0:nkeys, :, q0:TQ], in_=pSi[0:nkeys, :, q0:TQ], func=AF.Exp, scale=0.125),
                              reads=["S%d" % (i % 2)], writes=["E%d" % (i % 2)])
                        if kind == "d":
                            S.add(DVE, lambda e, E=E, q0=q0: e.tensor_tensor(out=E[:, :, q0:q0 + 128], in0=E[:, :, q0:q0 + 128],
                                                                              in1=mview.unsqueeze(1).to_broadcast([128, 2, 128]), op=ALU.mult),
                                  reads=["E%d" % (i % 2), "mask"], writes=["E%d" % (i % 2)])

                    def pv_step(i):
                        kind, kb, q0 = kbl[i]
                        E = Eb[i % 2]
                        nkeys = NM if kind == "m" else 128
                        last = (i == nk - 1)

                        def f(e, kind=kind, kb=kb, q0=q0, E=E, nkeys=nkeys):
                            ins = None
                            for qi in range(q0 // 128, 4):
                                for sub in range(2):
                                    r = qi * 2 + sub
                                    bank, off = r // 3, (r % 3) * 129
                                    st_ = (bank not in started)
                                    started.add(bank)
                                    if kind == "m":
                                        rhs = Vm[:, hd, :]
                                    else:
                                        rhs = Vc[:, kb, hd, :]
                                    sp_ = (kind == "d" and kb % 4 == qi)
                                    ins = e.matmul(pO[:, bank, off:off + 129], lhsT=E[0:nkeys, sub, qi * 128:(qi + 1) * 128], rhs=rhs,
                                                   start=st_, stop=sp_, skip_group_check=True)
                            return ins
                        rk = ["E%d" % (i % 2)] + (["Vmones", ("Vm", hd // 4)] if kind == "m" else ["Vones", ("V", kb, hd // 4)])
                        S.add(PE, f, reads=rk, writes=["O"])

                    for i in range(nk + 1):
                        if i < nk:
                            qk_step(i)
                        if i >= 1:
                            pv_step(i - 1)

                    def zcols(bank, n):
                        return pO[:, bank, 0:n * 129].rearrange("p (a b) -> p a b", b=129)[:, :, 128]
                    S.add(DVE, lambda e: e.reciprocal(out=stat[:, 8:11], in_=zcols(0, 3)), reads=["O"], writes=["rz0"])
                    S.add(DVE, lambda e: e.reciprocal(out=stat[:, 11:14], in_=zcols(1, 3)), reads=["O"], writes=["rz1"])
                    S.add(DVE, lambda e: e.reciprocal(out=stat[:, 14:16], in_=zcols(2, 2)), reads=["O"], writes=["rz2"])
                    S.add(DVE, lambda e: e.tensor_scalar(out=stat[:, 16:24], in0=stat[:, 8:16], scalar1=nlam[:, 0:1], scalar2=None, op0=ALU.mult),
                          reads=["rz0", "rz1", "rz2", "nlam"], writes=["rzs"])
                    for qi in range(4):
                        r0_, r1_ = 2 * qi, 2 * qi + 1
                        o0 = pO[:, r0_ // 3, (r0_ % 3) * 129:(r0_ % 3) * 129 + 128]
                        o1 = pO[:, r1_ // 3, (r1_ % 3) * 129:(r1_ % 3) * 129 + 128]
                        S.add(DVE, lambda e, o1=o1, r1_=r1_: e.tensor_scalar(out=ttmp[:], in0=o1, scalar1=stat[:, 16 + r1_:17 + r1_], scalar2=None, op0=ALU.mult),
                              reads=["O", "rzs"], writes=["ttmp"])
                        S.add(DVE, lambda e, o0=o0, r0_=r0_, qi=qi: e.scalar_tensor_tensor(out=osb[:, qi, :], in0=o0, scalar=stat[:, 8 + r0_:9 + r0_], in1=ttmp[:],
                                                                                          op0=ALU.mult, op1=ALU.add),
                              reads=["O", "rz0", "rz1", "rz2", "ttmp"], writes=[("osb", qi)])
                        S.add(ACT, lambda e, qi=qi: e.activation(out=junk[:, 0:128], in_=osb[:, qi, :], func=AF.Square, accum_out=stat[:, 24 + qi:25 + qi]),
                              reads=[("osb", qi)], writes=["junk", ("ss", qi)])
                    S.add(DVE, lambda e: e.tensor_scalar(out=stat[:, 28:32], in0=stat[:, 24:28], scalar1=1.0 / 128, scalar2=EPS, op0=ALU.mult, op1=ALU.add),
                          reads=[("ss", q) for q in range(4)], writes=["ssv"])
                    S.add(POOL, lambda e: e.tensor_tensor(out=stat[:, 32:36], in0=stat[:, 28:32], in1=nhalf[:, 0:1].to_broadcast([128, 4]), op=ALU.pow),
                          reads=["ssv", "nhalf"], writes=["srs"])
                    S.add(DVE, lambda e: e.tensor_tensor(out=oab[:], in0=osb[:], in1=stat[:, 32:36].unsqueeze(2).to_broadcast([128, 4, 128]), op=ALU.mult),
                          reads=[("osb", q) for q in range(4)] + ["srs"], writes=["oab"])

                    def ftr(e):
                        ins = None
                        for qi in range(4):
                            ins = e.transpose(pXb[:, qi * 128:(qi + 1) * 128], oab[:, qi, :], ident[:])
                        return ins
                    S.add(PE, ftr, reads=["oab", "ident"], writes=["X"])
                    S.add(ACT, lambda e, hd=hd: e.activation(out=oaT[:, hd, :], in_=pXb[:, 0:TQ], func=AF.Copy, scale=sws[:, 0:1]),
                          reads=["X", "sws"], writes=[("oaT", hd)])

                for hd_ in range(NH):
                    do_head(hd_)

                if STOP == "p3":
                    return
                okeys = [("oaT", q) for q in range(NH)]
                ckeys = [("cob", q) for q in range(KC)]
                for t8 in range(8):
                    if t8 % 4 == 0:
                        wv, wk = wget(U_WPC + t8 // 4)
                    ak, ab = fm_tile(wv, wk, t8 % 4, cob, ckeys, TQ)
                    S.add(DVE, lambda e, ab=ab, t8=t8: e.scalar_tensor_tensor(out=tb[:, t8, :], in0=tb[:, t8, :], scalar=1.0, in1=ab, op0=ALU.add, op1=ALU.mult),
                          reads=[ak, ("tb", t8)], writes=[("tb", t8)])
                for t8 in range(8):
                    if t8 % 4 == 0:
                        wv, wk = wget(U_WPA + t8 // 4)
                    ak, ab = fm_tile(wv, wk, t8 % 4, oaT, okeys, TQ)
                    S.add(DVE, lambda e, ab=ab, t8=t8: e.scalar_tensor_tensor(out=m1[:], in0=ta[:, t8, :], scalar=1.0, in1=ab, op0=ALU.add, op1=ALU.mult),
                          reads=[ak, ("ta", t8)], writes=["m1"])
                    S.add(POOL, lambda e, t8=t8: e.tensor_tensor(out=tb[:, t8, :], in0=tb[:, t8, :], in1=m1[:], op=ALU.add),
                          reads=["m1", ("tb", t8)], writes=[("tb", t8)])
                mkeys = [("tb", q) for q in range(KC)]

                if STOP == "p4":
                    return
                for hf in range(2):
                    wv, wk = wget(U_WO + hf)
                    for tt in range(4):
                        ak, ab = next_acc()

                        def f(e, wv=wv, ab=ab, tt=tt):
                            ins = None
                            for kc in range(KC):
                                ins = e.matmul(ab, lhsT=tb[:, kc, tt * 128:(tt + 1) * 128], rhs=wv[:, kc * 512:(kc + 1) * 512],
                                               start=(kc == 0), stop=(kc == KC - 1))
                            return ins
                        S.add(PE, f, reads=[wk] + mkeys, writes=[ak])
                        hv = h[:, tt, hf * 512:(hf + 1) * 512]
                        S.add(DVE, lambda e, ab=ab, hv=hv: e.scalar_tensor_tensor(out=hv, in0=ab, scalar=0.5, in1=hv, op0=ALU.mult, op1=ALU.add),
                              reads=[ak, ("h", tt)], writes=[("h", tt)] if hf == 1 else [("hx", tt)])

                if STOP == "p5":
                    return
                for tt in range(4):
                    norm_tile(h[:, tt, :], ("h", tt), 128, hnT, [("hnT", tt)], tt * 128)

                if STOP == "p6":
                    return
                for ug in range(11):
                    wv, wk = wget(U_WGU + ug)
                    for t2 in range(2):
                        j = 2 * ug + t2
                        gk, gb_ = fm_tile(wv, wk, 2 * t2, hnT, hkeys, TQ)
                        uk, ub_ = fm_tile(wv, wk, 2 * t2 + 1, hnT, hkeys, TQ)
                        S.add(ACT, lambda e, gb_=gb_: e.activation(out=swt[:], in_=gb_, func=AF.Tanh, scale=0.5), reads=[gk], writes=["swt"])
                        S.add(DVE, lambda e, gb_=gb_: e.scalar_tensor_tensor(out=swa[:], in0=swt[:], scalar=1.0, in1=gb_, op0=ALU.add, op1=ALU.mult),
                              reads=[gk, "swt"], writes=["swa"])
                        S.add(DVE, lambda e, ub_=ub_, j=j: e.tensor_tensor(out=actT[:, j, :], in0=ub_, in1=swa[:], op=ALU.mult),
                              reads=[uk, "swa"], writes=[("actT", j)])

                if STOP == "p7":
                    return
                def dacc(tt, hf):
                    i = tt * 2 + hf
                    return (("b%d" % i) if i < 7 else "X"), banks[i]
                for gd in range(6):
                    wv, wk = wget(U_WD + gd)
                    nj = 4 if gd < 5 else 2
                    for jj in range(nj):
                        j = gd * 4 + jj
                        for tt in range(4):
                            def f(e, wv=wv, jj=jj, j=j, tt=tt):
                                ins = None
                                for hf in range(2):
                                    _, ab = dacc(tt, hf)
                                    ins = e.matmul(ab, lhsT=actT[:, j, tt * 128:(tt + 1) * 128], rhs=wv[:, jj * 1024 + hf * 512: jj * 1024 + (hf + 1) * 512],
                                                   start=(j == 0), stop=(j == NJ - 1))
                                return ins
                            S.add(PE, f, reads=[wk, ("actT", j)], writes=[dacc(tt, 0)[0], dacc(tt, 1)[0]])
                for tt in range(4):
                    for hf in range(2):
                        ak, ab = dacc(tt, hf)
                        hv = h[:, tt, hf * 512:(hf + 1) * 512]
                        S.add(DVE, lambda e, ab=ab, hv=hv: e.scalar_tensor_tensor(out=hv, in0=ab, scalar=0.5, in1=hv, op0=ALU.mult, op1=ALU.add),
                              reads=[ak, ("h", tt)], writes=[("h", tt)] if hf == 1 else [("hx", tt)])
                    S.add(ACT, lambda e, tt=tt: e.activation(out=junk[:], in_=h[:, tt, :], func=AF.Square, accum_out=stat[:, 40 + tt:41 + tt]),
                          reads=[("h", tt)], writes=["junk", ("fs", tt)])
                    S.add(DVE, lambda e, tt=tt: e.tensor_scalar(out=stat[:, 44 + tt:45 + tt], in0=stat[:, 40 + tt:41 + tt], scalar1=1.0 / D, scalar2=EPS,
                                                                op0=ALU.mult, op1=ALU.add), reads=[("fs", tt)], writes=[("fv", tt)])
                    S.add(POOL, lambda e, tt=tt: e.tensor_tensor(out=stat[:, 48 + tt:49 + tt], in0=stat[:, 44 + tt:45 + tt], in1=nhalf[:], op=ALU.pow),
                          reads=[("fv", tt), "nhalf"], writes=[("fr", tt)])
                    S.add(DVE, lambda e, tt=tt: e.scalar_tensor_tensor(out=h[:, tt, :], in0=h[:, tt, :], scalar=stat[:, 48 + tt:49 + tt], in1=nfin[:],
                                                                       op0=ALU.mult, op1=ALU.mult),
                          reads=[("h", tt), ("fr", tt), "nfin"], writes=[("h", tt)])
                    S.add(SP, lambda e, tt=tt: e.dma_start(out=y_d[s, t0 + tt * 128: t0 + (tt + 1) * 128, :], in_=h[:, tt, :]),
                          reads=[("h", tt)], dma="y%d" % tt)

        for s_ in range(nseq):
            for c_ in range(nch):
                do_chunk(s_, c_)
                if STOP is not None:
                    dump()
                    return nc
        S.emit(final_dma_keys=["y0", "y1", "y2", "y3"])
    return nc


def _host_consts(T):
    pos = np.arange(NM + T, dtype=np.float32)
    inv_freq = (1.0 / (10000.0 ** (np.arange(0, 64, 2, dtype=np.float32) / np.float32(64)))).astype(np.float32)
    ang = pos[:, None] * inv_freq[None, :]
    ang = np.concatenate([ang, ang], axis=-1)
    cos = np.cos(ang).astype(np.float32).T
    sin = np.sin(ang).astype(np.float32).T
    sgn = np.where(np.arange(64) < 32, -1.0, 1.0).astype(np.float32)[:, None]
    cos128 = np.concatenate([cos, cos], 0)
    sin128 = np.concatenate([sin * sgn, sin * sgn], 0)
    bf = ml_dtypes.bfloat16
    ident = np.eye(128, dtype=np.float32).astype(bf)
    perm = np.zeros((128, 128), np.float32)
    for m in range(128):
        k = (m % 64 + 32) % 64 + 64 * (m // 64)
        perm[k, m] = 1.0
    mask = (np.arange(128)[None, :] >= np.arange(128)[:, None]).astype(np.float32)
    return (np.ascontiguousarray(cos128), np.ascontiguousarray(sin128), ident, perm.astype(bf), mask.astype(bf))


def _a_unit(W, cols):
    return W[:, cols].reshape(KC, 128, 512).transpose(1, 0, 2).reshape(128, 4096)


def _host_units(w_in, wpc, wpa, wo, wgu, wd):
    units = np.zeros((NU, 128, 4096), np.float32)
    starts = {U_Q: 0, U_K: 1024, U_V: 2048, U_CB: 3072, U_CC: 4096, U_CX: 5120, U_GA: 6144, U_GB: 7168}
    for u0, c0 in starts.items():
        for i in range(2):
            units[u0 + i] = _a_unit(w_in, np.arange(c0 + 512 * i, c0 + 512 * (i + 1)))
    for i in range(2):
        units[U_WPC + i] = _a_unit(wpc, np.arange(512 * i, 512 * (i + 1)))
        units[U_WPA + i] = _a_unit(wpa, np.arange(512 * i, 512 * (i + 1)))
        units[U_WO + i] = _a_unit(wo, np.arange(512 * i, 512 * (i + 1)))
    for i in range(11):
        cols = np.concatenate([np.arange(128 * (2 * i), 128 * (2 * i + 1)), DFF + np.arange(128 * (2 * i), 128 * (2 * i + 1)),
                               np.arange(128 * (2 * i + 1), 128 * (2 * i + 2)), DFF + np.arange(128 * (2 * i + 1), 128 * (2 * i + 2))])
        units[U_WGU + i] = _a_unit(wgu, cols)
    for g in range(6):
        nj = 4 if g < 5 else 2
        blk = wd[512 * g: 512 * g + 128 * nj].reshape(nj, 128, 1024).transpose(1, 0, 2).reshape(128, nj * 1024)
        units[U_WD + g, :, :nj * 1024] = blk
    return units


def _host_inputs(inputs, nseq, nch, ncores):
    T = nch * TQ
    f = lambda a: np.ascontiguousarray(np.asarray(a, dtype=np.float32))
    cos128, sin128, ident, perm, mask = _host_consts(T)
    units = _host_units(f(inputs["w_in"])[0], f(inputs["w_proj_conv"])[0], f(inputs["w_proj_attn"])[0], f(inputs["w_out"])[0],
                        f(inputs["w_gate_up"])[0], f(inputs["w_down"])[0])
    col8 = lambda v: np.ascontiguousarray(f(v).reshape(KC, 128).T)
    convw = np.ascontiguousarray(f(inputs["conv_w"])[0].reshape(3, KC, 128).transpose(2, 1, 0).reshape(128, KC * 3))
    lam = np.concatenate([f(inputs["lambda_q1"])[0], f(inputs["lambda_k1"])[0], f(inputs["lambda_q2"])[0], f(inputs["lambda_k2"])[0]])
    common = dict(
        meta=f(inputs["meta_tokens"]), wun=units, cos=cos128, sin=sin128, ident=ident, perm=perm, mask=mask,
        nmw=col8(inputs["norm_mix_w"][0]), nfw=col8(inputs["norm_ffn_w"][0]), convw=convw,
        subw=np.ascontiguousarray(f(inputs["subln_w"])[0].reshape(128, 1)),
        nfin=np.ascontiguousarray(np.broadcast_to(f(inputs["norm_final_w"])[None, :], (128, D))),
        lam=np.ascontiguousarray(np.broadcast_to(lam[None, :], (128, 256))),
    )
    x = f(inputs["x"])
    maps = []
    for ci in range(ncores):
        m = dict(common)
        m["x"] = np.ascontiguousarray(x[ci * nseq:(ci + 1) * nseq, :T])
        maps.append(m)
    return maps


def kernel(**inputs):
    nseq = BATCH // N_CORES
    nch = SEQ // TQ
    maps = _host_inputs(inputs, nseq, nch, N_CORES)
    nc = build(nseq, nch)
    res = run_bass_kernel_spmd(nc, maps, core_ids=list(range(N_CORES)))
    out = np.concatenate([np.asarray(r["y"]) for r in res.results], axis=0)
    return out.astype(np.float32)
```

```python
import math
import numpy as np
import ml_dtypes
from contextlib import ExitStack
from collections import defaultdict
import concourse.bass as bass
import concourse.mybir as mybir
from concourse.bass_utils import run_bass_kernel_spmd

F32 = mybir.dt.float32
BF16 = mybir.dt.bfloat16
AF = mybir.ActivationFunctionType
ALU = mybir.AluOpType

PE, ACT, DVE, POOL, SP = "tensor", "scalar", "vector", "gpsimd", "sync"
ENGS = (PE, ACT, DVE, POOL, SP)

D = 1024
KC = 8
NH = 8
NM = 16
TQ = 512
DFF = 2816
NJ = 22
EPS = 1e-5
LAMBDA_INIT = 0.8 - 0.6 * math.exp(-0.3 * 0)
N_CORES = 8
BATCH = 32
SEQ = 2048

U_GB, U_CC, U_CX, U_CB, U_GA, U_Q, U_K, U_V = 0, 2, 4, 6, 8, 10, 12, 14
U_WPC, U_WPA, U_WO, U_WGU, U_WD = 16, 18, 20, 22, 33
NU = 39
NSLOT = 3


class _Op:
    __slots__ = ("eng", "fn", "waits", "semkey", "value", "is_dma", "idx")


class Sched:
    def __init__(self, nc):
        self.nc = nc
        self.ops = []
        self.last_writer = {}
        self.readers = defaultdict(list)
        self.overlaps = defaultdict(list)
        self.count = defaultdict(int)
        self.seen = {e: {} for e in ENGS}
        self.dma_keys = []
        self.psum_keys = set()
        self.access = defaultdict(dict)

    def alias(self, a_keys, b_keys):
        for a in a_keys:
            for b in b_keys:
                self.overlaps[a].append(b)
                self.overlaps[b].append(a)

    def add(self, eng, fn, reads=(), writes=(), dma=None):
        op = _Op()
        op.eng, op.fn, op.is_dma = eng, fn, dma is not None
        op.idx = len(self.ops)
        deps = {}

        def dep(o, raw):
            if o is None or o is op:
                return
            same = (o.eng == eng) and not o.is_dma and not op.is_dma
            if same and (eng == PE or eng == SP or not raw):
                return
            deps[o.idx] = o

        for k in reads:
            dep(self.last_writer.get(k), True)
        for k in writes:
            for kk in [k] + self.overlaps.get(k, []):
                dep(self.last_writer.get(kk), False)
                for r in self.readers.get(kk, ()):
                    dep(r, False)
        for k in list(reads) + list(writes):
            if k in self.psum_keys:
                for kk in [k] + self.overlaps.get(k, []):
                    for e2, o in self.access[kk].items():
                        if e2 != eng:
                            deps[o.idx] = o
                self.access[k][eng] = op
        waits = {}
        for o in deps.values():
            if waits.get(o.semkey, 0) < o.value:
                waits[o.semkey] = o.value
        seen = self.seen[eng]
        op.waits = []
        for sk, v in waits.items():
            if seen.get(sk, 0) < v:
                seen[sk] = v
                op.waits.append((sk, v))
        if op.is_dma:
            op.semkey = "dma_" + dma
            if op.semkey not in self.count:
                self.dma_keys.append(op.semkey)
            self.count[op.semkey] += 16
        else:
            op.semkey = "eng_" + eng
            self.count[op.semkey] += 1
        op.value = self.count[op.semkey]
        for k in reads:
            self.readers[k].append(op)
        for k in writes:
            self.last_writer[k] = op
            self.readers[k] = []
            for kk in self.overlaps.get(k, []):
                self.last_writer[kk] = op
                self.readers[kk] = []
        self.ops.append(op)
        return op

    def emit(self, final_dma_keys=()):
        nc = self.nc
        with ExitStack() as st:
            sems = {}
            for e in ENGS:
                sems["eng_" + e] = st.enter_context(nc.semaphore("s_" + e))
            for k in self.dma_keys:
                sems[k] = st.enter_context(nc.semaphore("s_" + k))
            block = st.enter_context(nc.Block())
            per = {e: [o for o in self.ops if o.eng == e] for e in ENGS}
            finals = [("dma_" + k, self.count["dma_" + k]) for k in final_dma_keys if ("dma_" + k) in sems]

            def make(e):
                def body(eng):
                    for o in per[e]:
                        for sk, v in o.waits:
                            eng.wait_ge(sems[sk], v)
                        ins = o.fn(eng)
                        ins.then_inc(sems[o.semkey], 16 if o.is_dma else 1)
                    if e == SP:
                        for sk, v in finals:
                            eng.wait_ge(sems[sk], v)
                return body

            for e in ENGS:
                if per[e] or e == SP:
                    getattr(block, e)(make(e))
        return nc


DBG = None


class _Stop(Exception):
    pass


def build(nseq, nch):
    T = nch * TQ
    NKB = T // 128
    nc = bass.Bass("TRN2", target_bir_lowering=False)

    def din(name, shape, dt=F32):
        return nc.dram_tensor(name, list(shape), dt, kind="ExternalInput").ap()

    x_d = din("x", [nseq, T, D])
    meta_d = din("meta", [NM, D])
    wun_d = din("wun", [NU, 128, 4096])
    cos_d = din("cos", [128, NM + T])
    sin_d = din("sin", [128, NM + T])
    ident_d = din("ident", [128, 128], BF16)
    perm_d = din("perm", [128, 128], BF16)
    mask_d = din("mask", [128, 128], BF16)
    nmw_d = din("nmw", [128, KC])
    nfw_d = din("nfw", [128, KC])
    convw_d = din("convw", [128, KC * 3])
    subw_d = din("subw", [128, 1])
    nfin_d = din("nfin", [128, D])
    lam_d = din("lam", [128, 4 * 64])
    y_d = nc.dram_tensor("y", [nseq, T, D], F32, kind="ExternalOutput").ap()
    scr_d = nc.dram_tensor("wscr", [NU, 128, 4096], BF16, kind="Internal").ap()

    S = Sched(nc)
    with ExitStack() as st:
        def sb(name, shape, dt):
            return st.enter_context(nc.sbuf_tensor("sb_" + name, list(shape), dt))

        def ps(name, shape, dt):
            return st.enter_context(nc.psum_tensor(name, list(shape), dt))

        KT = sb("KT", [128, NH * T], BF16)
        KTv = KT[:].rearrange("p (h t) -> p h t", h=NH)
        KTm = sb("KTm", [128, NH, NM], BF16)
        Vc = sb("Vc", [128, NKB, NH, 129], BF16)
        Vm = sb("Vm", [NM, NH, 129], BF16)
        cosb = sb("cosb", [128, TQ], F32)
        sinb = sb("sinb", [128, TQ], F32)
        ident = sb("ident", [128, 128], BF16)
        perm = sb("perm", [128, 128], BF16)
        mask = sb("mask", [128, 128], BF16)
        nmw = sb("nmw", [128, KC], F32)
        nfw = sb("nfw", [128, KC], F32)
        convw = sb("convw", [128, KC * 3], F32)
        sws = sb("sws", [128, 1], F32)
        nfin = sb("nfin", [128, D], F32)
        lamv = sb("lamv", [128, 4 * 64], F32)
        lamt = sb("lamt", [128, 2 * 64], F32)
        lams = sb("lams", [128, 4], F32)
        nlam = sb("nlam", [128, 1], F32)
        nhalf = sb("nhalf", [128, 1], F32)
        h = sb("h", [128, 4, D], F32)
        xin = sb("xin", [128, D], F32)
        xsb = [sb("xs%d" % i, [128, D], BF16) for i in range(4)]
        hnT = sb("hnT", [128, KC, TQ], BF16)
        arena = sb("arena", [128, 4096 + 4096 + KC * 514], BF16)
        QT = arena[:, 0:4096].rearrange("p (h t) -> p h t", h=NH)
        cob = arena[:, 4096:8192].rearrange("p (c t) -> p c t", c=KC)
        ubuf = arena[:, 8192:8192 + KC * 514].rearrange("p (c t) -> p c t", c=KC)
        actT = arena[:, 0:NJ * TQ].rearrange("p (j t) -> p j t", j=NJ)
        tg = [sb("tg%d" % i, [128, TQ], BF16) for i in range(2)]
        mg = sb("mg", [128, KC, TQ], BF16)
        oaT = sb("oaT", [128, NH, TQ], BF16)
        Eb = [sb("E%d" % i, [128, 2, TQ], BF16) for i in range(2)]
        qkb = sb("qkb", [128, TQ], BF16)
        rt = sb("rt", [128, 1040], F32)
        rt1 = rt[:, 0:TQ]
        rt2 = rt[:, 520:520 + TQ]
        ocp = rt[:, 0:8 * 129].rearrange("p (r e) -> p r e", e=129)
        osb = sb("osb", [128, 4, 128], F32)
        ttmp = sb("ttmp", [128, 128], F32)
        oab = sb("oab", [128, 4, 128], BF16)
        junk = sb("junk", [128, D], BF16)
        swt = sb("swt", [128, TQ], BF16)
        swa = sb("swa", [128, TQ], BF16)
        cvb = [sb("cv%d" % i, [128, TQ], F32) for i in range(2)]
        m1 = sb("m1", [128, TQ], BF16)
        stat = sb("stat", [128, 96], F32)
        um = sb("um", [128, KC, NM], BF16)
        uhs = sb("uhs", [128, KC, 2], BF16)
        ccm = sb("ccm", [128, KC, NM], F32)
        wring = sb("wring", [128, NSLOT, 4096], BF16)
        if NH * T // 2 >= 8192:
            stg = KT[:].bitcast(F32)
        else:
            stg = sb("stg", [128, 8192], F32)[:]

        pS = [ps("pS%d" % i, [128, 2, 512], F32) for i in range(2)]
        pO = ps("pO", [128, 3, 512], F32)
        pX = ps("pX", [128, 512], F32)
        banks = [pS[0][:, 0, :], pS[0][:, 1, :], pS[1][:, 0, :], pS[1][:, 1, :],
                 pO[:, 0, :], pO[:, 1, :], pO[:, 2, :], pX[:]]
        pXb = pX[:].bitcast(BF16)

        acc_state = {"i": 0}
        ACCS = [("b%d" % i, banks[i]) for i in range(7)]
        S.alias(["b0", "b1"], ["S0"])
        S.alias(["b2", "b3"], ["S1"])
        S.alias(["b4", "b5", "b6"], ["O"])
        S.alias([("actT", j) for j in range(NJ)],
                [("QT", q) for q in range(NH)] + [("cob", q) for q in range(KC)] + [("u", q) for q in range(KC)] + [("uh", q) for q in range(KC)])
        S.alias([("stg", 0), ("stg", 1)], [("KT", q, cc_) for q in range(NH) for cc_ in range(nch)])
        for sl_ in range(NSLOT):
            S.alias([("w", sl_)], [("wx", sl_, kc_) for kc_ in range(KC)])
        S.alias(["ocp"], ["rt1", "rt2"])
        S.psum_keys = set(["S0", "S1", "O", "X"] + ["b%d" % i for i in range(7)])

        def next_acc():
            i = acc_state["i"]
            acc_state["i"] = (i + 1) % 7
            return ACCS[i]

        wseq = []
        wst = {"issued": 0, "cons": 0}

        def wget(u, held=0):
            n = wst["cons"]
            assert wseq[n] == u, (n, wseq[n], u)
            while wst["issued"] < min(len(wseq), n - held + NSLOT):
                i = wst["issued"]
                slot = i % NSLOT
                uu = wseq[i]
                S.add(SP, lambda e, slot=slot, uu=uu: e.dma_start(out=wring[:, slot, :], in_=scr_d[uu]),
                      reads=[("scr", uu)], writes=[("w", slot)], dma="w%d" % slot)
                wst["issued"] += 1
            wst["cons"] += 1
            slot = n % NSLOT
            return wring[:, slot, :], ("w", slot)

        meta_units = [U_CC, U_CC + 1, U_CX, U_CX + 1, U_K, U_K + 1, U_V, U_V + 1]
        chunk_units = ([U_CC, U_CC + 1, U_CX, U_CX + 1, U_CB, U_CB + 1, U_Q, U_Q + 1, U_K, U_K + 1, U_V, U_V + 1,
                        U_GB, U_WPC, U_GB + 1, U_WPC + 1, U_GA, U_WPA, U_GA + 1, U_WPA + 1, U_WO, U_WO + 1]
                       + [U_WGU + i for i in range(11)] + [U_WD + i for i in range(6)])
        assert sorted(chunk_units) == list(range(NU))
        wseq.extend(meta_units)
        for _ in range(nseq * nch):
            wseq.extend(chunk_units)
        cast_order = meta_units + [u for u in chunk_units if u not in meta_units]

        def cload(dst, src, key):
            S.add(SP, lambda e: e.dma_start(out=dst, in_=src), writes=[key], dma="c_" + key)

        cload(ident[:], ident_d, "ident")
        cload(perm[:], perm_d, "perm")
        cload(mask[:], mask_d, "mask")
        cload(nmw[:], nmw_d, "nmw")
        cload(nfw[:], nfw_d, "nfw")
        cload(convw[:], convw_d, "convw")
        cload(sws[:], subw_d, "sws0")
        cload(nfin[:], nfin_d, "nfin")
        cload(lamv[:], lam_d, "lamv")
        S.add(POOL, lambda e: e.memset(nhalf[:], -0.5), writes=["nhalf"])
        S.add(POOL, lambda e: e.memset(Vc[:, :, :, 128:129], 1.0), writes=["Vones"])
        S.add(POOL, lambda e: e.memset(Vm[:, :, 128:129], 1.0), writes=["Vmones"])
        S.add(DVE, lambda e: e.tensor_scalar(out=sws[:], in0=sws[:], scalar1=float(1.0 - LAMBDA_INIT), scalar2=None,
                                             op0=ALU.mult), reads=["sws0"], writes=["sws"])
        lv = lamv[:].rearrange("p (a b) -> p a b", a=4)
        lt = lamt[:].rearrange("p (a b) -> p a b", a=2)
        S.add(DVE, lambda e: e.tensor_tensor(out=lt[:, 0, :], in0=lv[:, 0, :], in1=lv[:, 1, :], op=ALU.mult),
              reads=["lamv"], writes=["lt0"])
        S.add(DVE, lambda e: e.tensor_tensor(out=lt[:, 1, :], in0=lv[:, 2, :], in1=lv[:, 3, :], op=ALU.mult),
              reads=["lamv"], writes=["lt1"])
        S.add(DVE, lambda e: e.tensor_reduce(out=lams[:, 0:2], in_=lt, op=ALU.add, axis=mybir.AxisListType.X),
              reads=["lt0", "lt1"], writes=["lams01"])
        S.add(ACT, lambda e: e.activation(out=lams[:, 2:4], in_=lams[:, 0:2], func=AF.Exp),
              reads=["lams01"], writes=["lams23"])
        S.add(DVE, lambda e: e.scalar_tensor_tensor(out=nlam[:], in0=lams[:, 3:4], scalar=float(-LAMBDA_INIT),
                                                    in1=lams[:, 2:3], op0=ALU.add, op1=ALU.subtract),
              reads=["lams23"], writes=["nlam"])

        for ci_, u in enumerate(cast_order):
            sl = ci_ % 2
            sv = stg[:, sl * 4096:(sl + 1) * 4096]
            S.add(SP, lambda e, sv=sv, u=u: e.dma_start(out=sv, in_=wun_d[u]), writes=[("stg", sl)], dma="stg%d" % sl)
            slot = ci_ % NSLOT
            dst = wring[:, slot, :]
            if u < U_WPC or (U_WGU <= u < U_WD):
                sc = nmw if u < U_WPC else nfw
                sck = "nmw" if u < U_WPC else "nfw"
                for kc in range(KC):
                    o_ = dst[:, kc * 512:(kc + 1) * 512]
                    i_ = sv[:, kc * 512:(kc + 1) * 512]
                    if (ci_ + kc) % 2 == 0:
                        S.add(ACT, lambda e, o_=o_, i_=i_, kc=kc, sc=sc: e.activation(out=o_, in_=i_, func=AF.Copy, scale=sc[:, kc:kc + 1]),
                              reads=[("stg", sl), sck], writes=[("wx", slot, kc)])
                    else:
                        S.add(DVE, lambda e, o_=o_, i_=i_, kc=kc, sc=sc: e.tensor_scalar(out=o_, in0=i_, scalar1=sc[:, kc:kc + 1], scalar2=None, op0=ALU.mult),
                              reads=[("stg", sl), sck], writes=[("wx", slot, kc)])
                rk = [("wx", slot, kc) for kc in range(KC)]
            else:
                for hf in range(2):
                    o_ = dst[:, hf * 2048:(hf + 1) * 2048]
                    i_ = sv[:, hf * 2048:(hf + 1) * 2048]
                    if hf == 0:
                        S.add(ACT, lambda e, o_=o_, i_=i_: e.activation(out=o_, in_=i_, func=AF.Copy),
                              reads=[("stg", sl)], writes=[("wx", slot, 0)])
                    else:
                        S.add(DVE, lambda e, o_=o_, i_=i_: e.tensor_copy(out=o_, in_=i_),
                              reads=[("stg", sl)], writes=[("wx", slot, 1)])
                rk = [("wx", slot, 0), ("wx", slot, 1)]
            S.add(SP, lambda e, dst=dst, u=u: e.dma_start(out=scr_d[u], in_=dst), reads=rk, writes=[("scr", u)], dma="scrw%d" % slot)

        def fm_tile(wv, wk, ci, rhsT, rkeys, n, acc=None):
            ak, ab = acc if acc is not None else next_acc()

            def f(e):
                ins = None
                for kc in range(KC):
                    ins = e.matmul(ab[:, 0:n], lhsT=wv[:, kc * 512 + ci * 128: kc * 512 + (ci + 1) * 128],
                                   rhs=rhsT[:, kc, 0:n], start=(kc == 0), stop=(kc == KC - 1))
                return ins
            S.add(PE, f, reads=[wk] + list(rkeys), writes=[ak])
            return ak, ab

        def rope_tile(ak, ab, n, cs, sn, cskeys, dst, dkey):
            S.add(ACT, lambda e: e.activation(out=qkb[:, 0:n], in_=ab[:, 0:n], func=AF.Copy), reads=[ak], writes=["qkb"])
            S.add(PE, lambda e: e.matmul(pX[:, 0:n], lhsT=perm[:], rhs=qkb[:, 0:n], start=True, stop=True),
                  reads=["qkb", "perm"], writes=["X"])
            S.add(DVE, lambda e: e.tensor_tensor(out=rt1[:, 0:n], in0=ab[:, 0:n], in1=cs, op=ALU.mult),
                  reads=[ak] + cskeys, writes=["rt1"])
            S.add(DVE, lambda e: e.tensor_tensor(out=rt2[:, 0:n], in0=pX[:, 0:n], in1=sn, op=ALU.mult),
                  reads=["X"] + cskeys, writes=["rt2"])
            S.add(POOL, lambda e: e.tensor_tensor(out=dst, in0=rt1[:, 0:n], in1=rt2[:, 0:n], op=ALU.add),
                  reads=["rt1", "rt2"], writes=[dkey])

        nt_state = {"i": 0}

        def norm_tile(src, skey, npart, dst_hnT, dkeys, col0, defer=False):
            i = nt_state["i"]
            nt_state["i"] = i + 1
            xs = xsb[i % 4]
            xk = "xs%d" % (i % 4)
            c0 = 64 + 3 * (i % 8)
            sk = "nst%d" % (i % 8)
            S.add(ACT, lambda e: e.activation(out=junk[0:npart, :], in_=src, func=AF.Square, accum_out=stat[0:npart, c0:c0 + 1]),
                  reads=[skey], writes=["junk", sk + "a"])
            S.add(DVE, lambda e: e.tensor_scalar(out=stat[0:npart, c0 + 1:c0 + 2], in0=stat[0:npart, c0:c0 + 1], scalar1=1.0 / D, scalar2=EPS,
                                                 op0=ALU.mult, op1=ALU.add), reads=[sk + "a"], writes=[sk + "b"])
            S.add(POOL, lambda e: e.tensor_tensor(out=stat[0:npart, c0 + 2:c0 + 3], in0=stat[0:npart, c0 + 1:c0 + 2], in1=nhalf[0:npart, :], op=ALU.pow),
                  reads=[sk + "b", "nhalf"], writes=[sk + "c"])
            S.add(ACT, lambda e: e.activation(out=xs[0:npart, :], in_=src, func=AF.Copy, scale=stat[0:npart, c0 + 2:c0 + 3]),
                  reads=[skey, sk + "c"], writes=[xk])

            def back():
                def f(e):
                    ins = None
                    for kc in range(KC):
                        ins = e.transpose(pXb[:, kc * 128: kc * 128 + npart], xs[0:npart, kc * 128:(kc + 1) * 128], ident[0:npart, 0:npart])
                    return ins
                S.add(PE, f, reads=[xk, "ident"], writes=["X"])
                S.add(DVE, lambda e: e.tensor_copy(out=dst_hnT[:, :, col0:col0 + npart],
                                                   in_=pXb.rearrange("p (k t) -> p k t", k=KC)[:, :, 0:npart]),
                      reads=["X"], writes=dkeys)
            if defer:
                return back
            back()
            return None

        S.add(SP, lambda e: e.dma_start(out=xin[0:NM, :], in_=meta_d), writes=["xin"], dma="xin")
        S.add(SP, lambda e: e.dma_start(out=cosb[:, 0:NM], in_=cos_d[:, 0:NM]), writes=["cos"], dma="cos")
        S.add(SP, lambda e: e.dma_start(out=sinb[:, 0:NM], in_=sin_d[:, 0:NM]), writes=["sin"], dma="sin")
        hk_all = [("hnT", tt) for tt in range(4)]
        norm_tile(xin[0:NM, :], "xin", NM, hnT, hk_all, 0)
        for t8 in range(8):
            if t8 % 4 == 0:
                wv, wk = wget(U_CC + t8 // 4)
            ak, ab = fm_tile(wv, wk, t8 % 4, hnT, hk_all, NM)
            S.add(ACT, lambda e, ab=ab, t8=t8: e.activation(out=ccm[:, t8, :], in_=ab[:, 0:NM], func=AF.Copy),
                  reads=[ak], writes=[("ccm", t8)])
        for t8 in range(8):
            if t8 % 4 == 0:
                wv, wk = wget(U_CX + t8 // 4)
            ak, ab = fm_tile(wv, wk, t8 % 4, hnT, hk_all, NM)
            S.add(DVE, lambda e, ab=ab, t8=t8: e.tensor_tensor(out=um[:, t8, :], in0=ab[:, 0:NM], in1=ccm[:, t8, :], op=ALU.mult),
                  reads=[ak, ("ccm", t8)], writes=[("um", t8)])
        for t8 in range(8):
            if t8 % 4 == 0:
                wv, wk = wget(U_K + t8 // 4)
            ak, ab = fm_tile(wv, wk, t8 % 4, hnT, hk_all, NM)
            rope_tile(ak, ab, NM, cosb[:, 0:NM], sinb[:, 0:NM], ["cos", "sin"], KTm[:, t8, :], ("KTm", t8))
        for hf in range(2):
            wv, wk = wget(U_V + hf)
            ak, ab = next_acc()

            def f(e, wv=wv, ab=ab):
                ins = None
                for kc in range(KC):
                    ins = e.matmul(ab[0:NM, :], lhsT=hnT[:, kc, 0:NM], rhs=wv[:, kc * 512:(kc + 1) * 512],
                                   start=(kc == 0), stop=(kc == KC - 1))
                return ins
            S.add(PE, f, reads=[wk] + hk_all, writes=[ak])
            S.add(ACT, lambda e, ab=ab, hf=hf: e.activation(out=Vm[:, hf * 4:(hf + 1) * 4, 0:128],
                                                           in_=ab[0:NM, :].rearrange("p (a b) -> p a b", a=4), func=AF.Copy),
                  reads=[ak, "Vmones"], writes=[("Vm", hf)])

        mview = mask[:]

        def dbg(stage):
            if DBG != stage:
                return
            items = [("QT", arena[:, 0:4096], [128, 4096], BF16), ("cob", arena[:, 4096:8192], [128, 4096], BF16),
                     ("actT", arena[:, 0:NJ * TQ], [128, NJ * TQ], BF16),
                     ("KT", KT[:, 0:4096], [128, 4096], BF16), ("Vc", Vc[:, 0:4, :, :].rearrange("p a b c -> p (a b c)"), [128, 4 * NH * 129], BF16),
                     ("mg", mg[:].rearrange("p a b -> p (a b)"), [128, 4096], BF16),
                     ("oaT", oaT[:].rearrange("p a b -> p (a b)"), [128, 4096], BF16), ("hnT", hnT[:].rearrange("p a b -> p (a b)"), [128, 4096], BF16),
                     ("h", h[:].rearrange("p a b -> p (a b)"), [128, 4 * D], F32), ("stat", stat[:], [128, 96], F32)]
            allk = list(S.last_writer.keys())
            for name, ap_, shp, dt_ in items:
                dd = nc.dram_tensor("dbg_" + name, shp, dt_, kind="ExternalOutput").ap()
                S.add(SP, lambda e, dd=dd, ap_=ap_: e.dma_start(out=dd, in_=ap_), reads=allk, dma="dbg_" + name)
            S.emit(final_dma_keys=["dbg_" + it[0] for it in items])
            raise _Stop()

        def p1_prefetch(s, c):
            t0 = c * TQ
            backs = []
            S.add(SP, lambda e: e.dma_start(out=cosb[:], in_=cos_d[:, NM + t0: NM + t0 + TQ]), writes=["cos"], dma="cos")
            S.add(SP, lambda e: e.dma_start(out=sinb[:], in_=sin_d[:, NM + t0: NM + t0 + TQ]), writes=["sin"], dma="sin")
            for tt in range(4):
                S.add(POOL, lambda e, tt=tt: e.dma_start(out=xin[:], in_=x_d[s, t0 + tt * 128: t0 + (tt + 1) * 128, :]),
                      writes=["xin"], dma="xin")
                backs.append(norm_tile(xin[:], "xin", 128, hnT, [("hnT", tt)], tt * 128, defer=True))
            return backs

        def do_chunk(s, c, nxt):
            t0 = c * TQ
            hkeys = [("hnT", tt) for tt in range(4)]
            for tt in range(4):
                S.add(POOL, lambda e, tt=tt: e.dma_start(out=h[:, tt, :], in_=x_d[s, t0 + tt * 128: t0 + (tt + 1) * 128, :]),
                      writes=[("h", tt)], dma="x%d" % tt)

            dbg("p1")
            for t8 in range(8):
                if t8 % 4 == 0:
                    wv, wk = wget(U_CC + t8 // 4)
                ak, ab = fm_tile(wv, wk, t8 % 4, hnT, hkeys, TQ)
                S.add(ACT, lambda e, ab=ab, t8=t8: e.activation(out=cob[:, t8, :], in_=ab, func=AF.Copy),
                      reads=[ak], writes=[("cob", t8)])
            for t8 in range(8):
                if t8 % 4 == 0:
                    wv, wk = wget(U_CX + t8 // 4)
                ak, ab = fm_tile(wv, wk, t8 % 4, hnT, hkeys, TQ)
                if c == 0:
                    S.add(POOL, lambda e, t8=t8: e.tensor_copy(out=ubuf[:, t8, 0:2], in_=um[:, t8, NM - 2:NM]),
                          reads=[("um", t8)], writes=[("uh", t8)])
                else:
                    S.add(POOL, lambda e, t8=t8: e.tensor_copy(out=ubuf[:, t8, 0:2], in_=uhs[:, t8, :]),
                          reads=[("uhs", t8)], writes=[("uh", t8)])
                S.add(DVE, lambda e, ab=ab, t8=t8: e.tensor_tensor(out=ubuf[:, t8, 2:2 + TQ], in0=ab, in1=cob[:, t8, :], op=ALU.mult),
                      reads=[ak, ("cob", t8), ("uh", t8)], writes=[("u", t8)])
                S.add(POOL, lambda e, t8=t8: e.tensor_copy(out=uhs[:, t8, :], in_=ubuf[:, t8, TQ:TQ + 2]),
                      reads=[("u", t8)], writes=[("uhs", t8)])
            for t8 in range(8):
                if t8 % 4 == 0:
                    wv, wk = wget(U_CB + t8 // 4)
                ak, ab = fm_tile(wv, wk, t8 % 4, hnT, hkeys, TQ)
                cv = cvb[t8 % 2]
                cvk = "cv%d" % (t8 % 2)
                S.add(POOL, lambda e, t8=t8, cv=cv: e.tensor_scalar(out=cv[:], in0=ubuf[:, t8, 0:TQ], scalar1=convw[:, t8 * 3:t8 * 3 + 1],
                                                                    scalar2=0.0, op0=ALU.mult, op1=ALU.add),
                      reads=[("u", t8), ("uh", t8), "convw"], writes=[cvk])
                S.add(DVE, lambda e, t8=t8, cv=cv: e.scalar_tensor_tensor(out=cv[:], in0=ubuf[:, t8, 1:1 + TQ], scalar=convw[:, t8 * 3 + 1:t8 * 3 + 2],
                                                                           in1=cv[:], op0=ALU.mult, op1=ALU.add),
                      reads=[("u", t8), ("uh", t8), "convw", cvk], writes=[cvk])
                S.add(DVE, lambda e, t8=t8, cv=cv: e.scalar_tensor_tensor(out=cv[:], in0=ubuf[:, t8, 2:2 + TQ], scalar=convw[:, t8 * 3 + 2:t8 * 3 + 3],
                                                                           in1=cv[:], op0=ALU.mult, op1=ALU.add),
                      reads=[("u", t8), "convw", cvk], writes=[cvk])
                S.add(DVE, lambda e, ab=ab, t8=t8, cv=cv: e.tensor_tensor(out=cob[:, t8, :], in0=ab, in1=cv[:], op=ALU.mult),
                      reads=[ak, cvk], writes=[("cob", t8)])
            for t8 in range(8):
                if t8 % 4 == 0:
                    wv, wk = wget(U_Q + t8 // 4)
                ak, ab = fm_tile(wv, wk, t8 % 4, hnT, hkeys, TQ)
                rope_tile(ak, ab, TQ, cosb[:], sinb[:], ["cos", "sin"], QT[:, t8, :], ("QT", t8))
            for t8 in range(8):
                if t8 % 4 == 0:
                    wv, wk = wget(U_K + t8 // 4)
                ak, ab = fm_tile(wv, wk, t8 % 4, hnT, hkeys, TQ)
                rope_tile(ak, ab, TQ, cosb[:], sinb[:], ["cos", "sin"], KTv[:, t8, t0:t0 + TQ], ("KT", t8, c))
            for hf in range(2):
                wv, wk = wget(U_V + hf)
                for tt in range(4):
                    ak, ab = next_acc()

                    def f(e, wv=wv, ab=ab, tt=tt):
                        ins = None
                        for kc in range(KC):
                            ins = e.matmul(ab, lhsT=hnT[:, kc, tt * 128:(tt + 1) * 128], rhs=wv[:, kc * 512:(kc + 1) * 512],
                                           start=(kc == 0), stop=(kc == KC - 1))
                        return ins
                    S.add(PE, f, reads=[wk, ("hnT", tt)], writes=[ak])
                    kb = c * 4 + tt
                    S.add(ACT, lambda e, ab=ab, hf=hf, kb=kb: e.activation(out=Vc[:, kb, hf * 4:(hf + 1) * 4, 0:128],
                                                                         in_=ab.rearrange("p (a b) -> p a b", a=4), func=AF.Copy),
                          reads=[ak, "Vones"], writes=[("V", kb, hf)])

            dbg("p2")
            def do_head(hd, pending):
                kbl = [("m", 0, 0)] + [("f", kb, 0) for kb in range(4 * c)] + [("d", 4 * c + i, 128 * i) for i in range(4)]
                nk = len(kbl)
                started = set()

                def qk_step(i):
                    kind, kb, q0 = kbl[i]
                    pSi = pS[i % 2]
                    E = Eb[i % 2]
                    nkeys = NM if kind == "m" else 128

                    def f(e):
                        ins = None
                        for sub in range(2):
                            r0 = sub * 64
                            if kind == "m":
                                lt_ = KTm[r0:r0 + 64, hd, :]
                            else:
                                lt_ = KTv[r0:r0 + 64, hd, kb * 128:(kb + 1) * 128]
                            ins = e.matmul(pSi[0:nkeys, sub, q0:TQ], lhsT=lt_, rhs=QT[r0:r0 + 64, hd, q0:TQ], start=True, stop=True)
                        return ins
                    rk = [("QT", hd)] + ([("KTm", hd)] if kind == "m" else [("KT", hd, kb // 4)])
                    S.add(PE, f, reads=rk, writes=["S%d" % (i % 2)])
                    S.add(ACT, lambda e: e.activation(out=E[0:nkeys, :, q0:TQ], in_=pSi[0:nkeys, :, q0:TQ], func=AF.Exp, scale=0.125),
                          reads=["S%d" % (i % 2)], writes=["E%d" % (i % 2)])
                    if kind == "d":
                        S.add(POOL, lambda e: e.tensor_tensor(out=E[:, :, q0:q0 + 128], in0=E[:, :, q0:q0 + 128],
                                                               in1=mview.unsqueeze(1).to_broadcast([128, 2, 128]), op=ALU.mult),
                              reads=["E%d" % (i % 2), "mask"], writes=["E%d" % (i % 2)])

                def pv_step(i):
                    kind, kb, q0 = kbl[i]
                    E = Eb[i % 2]
                    nkeys = NM if kind == "m" else 128

                    def f(e):
                        ins = None
                        for qi in range(q0 // 128, 4):
                            for sub in range(2):
                                r = qi * 2 + sub
                                bank, off = r // 3, (r % 3) * 129
                                st_ = (bank not in started)
                                started.add(bank)
                                rhs = Vm[:, hd, :] if kind == "m" else Vc[:, kb, hd, :]
                                sp_ = (kind == "d" and kb % 4 == qi)
                                ins = e.matmul(pO[:, bank, off:off + 129], lhsT=E[0:nkeys, sub, qi * 128:(qi + 1) * 128], rhs=rhs,
                                               start=st_, stop=sp_, skip_group_check=True)
                        return ins
                    rk = ["E%d" % (i % 2)] + (["Vmones", ("Vm", hd // 4)] if kind == "m" else ["Vones", ("V", kb, hd // 4)])
                    S.add(PE, f, reads=rk, writes=["O"])

                for i in range(nk + 1):
                    if i < nk:
                        qk_step(i)
                    if i >= 1:
                        pv_step(i - 1)
                    if i == min(nk, 6) and pending is not None:
                        pending()
                        pending = None
                if pending is not None:
                    pending()

                S.add(ACT, lambda e: e.activation(out=rt[:, 0:387], in_=pO[:, 0, 0:387], func=AF.Copy), reads=["O"], writes=["ocp"])
                S.add(DVE, lambda e: e.tensor_copy(out=rt[:, 387:774], in_=pO[:, 1, 0:387]), reads=["O"], writes=["ocp1"])
                S.add(ACT, lambda e: e.activation(out=rt[:, 774:1032], in_=pO[:, 2, 0:258], func=AF.Copy), reads=["O"], writes=["ocp2"])
                ok3 = ["ocp", "ocp1", "ocp2"]
                S.add(DVE, lambda e: e.reciprocal(out=stat[:, 8:16], in_=ocp[:, :, 128]), reads=ok3, writes=["rz"])
                S.add(DVE, lambda e: e.tensor_scalar(out=stat[:, 16:24], in0=stat[:, 8:16], scalar1=nlam[:, 0:1], scalar2=None, op0=ALU.mult),
                      reads=["rz", "nlam"], writes=["rzs"])
                for qi in range(4):
                    r0_, r1_ = 2 * qi, 2 * qi + 1
                    S.add(DVE, lambda e, r1_=r1_: e.tensor_scalar(out=ttmp[:], in0=ocp[:, r1_, 0:128], scalar1=stat[:, 16 + r1_:17 + r1_], scalar2=None, op0=ALU.mult),
                          reads=ok3 + ["rzs"], writes=["ttmp"])
                    S.add(DVE, lambda e, r0_=r0_, qi=qi: e.scalar_tensor_tensor(out=osb[:, qi, :], in0=ocp[:, r0_, 0:128], scalar=stat[:, 8 + r0_:9 + r0_], in1=ttmp[:],
                                                                                  op0=ALU.mult, op1=ALU.add),
                          reads=ok3 + ["rz", "ttmp"], writes=[("osb", qi)])
                    S.add(DVE, lambda e, qi=qi: e.scalar_tensor_tensor(out=junk[:, 0:128], in0=osb[:, qi, :], scalar=1.0, in1=osb[:, qi, :],
                                                                      op0=ALU.mult, op1=ALU.mult, accum_out=stat[:, 24 + qi:25 + qi]),
                          reads=[("osb", qi)], writes=["junkd", ("ss", qi)])
                S.add(DVE, lambda e: e.tensor_scalar(out=stat[:, 28:32], in0=stat[:, 24:28], scalar1=1.0 / 128, scalar2=EPS, op0=ALU.mult, op1=ALU.add),
                      reads=[("ss", q) for q in range(4)], writes=["ssv"])
                S.add(POOL, lambda e: e.tensor_tensor(out=stat[:, 32:36], in0=stat[:, 28:32], in1=nhalf[:, 0:1].to_broadcast([128, 4]), op=ALU.pow),
                      reads=["ssv", "nhalf"], writes=["srs"])
                S.add(DVE, lambda e: e.tensor_tensor(out=oab[:], in0=osb[:], in1=stat[:, 32:36].unsqueeze(2).to_broadcast([128, 4, 128]), op=ALU.mult),
                      reads=[("osb", q) for q in range(4)] + ["srs"], writes=["oab"])

                def finish():
                    def ftr(e):
                        ins = None
                        for qi in range(4):
                            ins = e.transpose(pXb[:, qi * 128:(qi + 1) * 128], oab[:, qi, :], ident[:])
                        return ins
                    S.add(PE, ftr, reads=["oab", "ident"], writes=["X"])
                    S.add(DVE, lambda e: e.tensor_scalar(out=oaT[:, hd, :], in0=pXb[:, 0:TQ], scalar1=sws[:, 0:1], scalar2=None, op0=ALU.mult),
                          reads=["X", "sws"], writes=[("oaT", hd)])
                return finish

            pend = None
            for hd_ in range(NH):
                pend = do_head(hd_, pend)

            dbg("p3")
            okeys = [("oaT", q) for q in range(NH)]
            ckeys = [("cob", q) for q in range(KC)]
            for half in range(2):
                gv, gk = wget(U_GB + half)
                wv, wk = wget(U_WPC + half, held=1)
                for t4 in range(4):
                    t8 = half * 4 + t4
                    ak, ab = fm_tile(gv, gk, t4, hnT, hkeys, TQ)
                    tgi = tg[t8 % 2]
                    tgk = "tg%d" % (t8 % 2)
                    S.add(ACT, lambda e, ab=ab, tgi=tgi: e.activation(out=tgi[:], in_=ab, func=AF.Tanh, scale=0.5), reads=[ak], writes=[tgk])
                    ak2, ab2 = fm_tile(wv, wk, t4, cob, ckeys, TQ)
                    S.add(DVE, lambda e, ab2=ab2, tgi=tgi, t8=t8: e.scalar_tensor_tensor(out=mg[:, t8, :], in0=tgi[:], scalar=1.0, in1=ab2, op0=ALU.add, op1=ALU.mult),
                          reads=[ak2, tgk], writes=[("mg", t8)])
                    if half == 0 and t4 == 1 and pend is not None:
                        pend()
                        pend = None
            for half in range(2):
                gv, gk = wget(U_GA + half)
                wv, wk = wget(U_WPA + half, held=1)
                for t4 in range(4):
                    t8 = half * 4 + t4
                    ak, ab = fm_tile(gv, gk, t4, hnT, hkeys, TQ)
                    tgi = tg[t8 % 2]
                    tgk = "tg%d" % (t8 % 2)
                    S.add(ACT, lambda e, ab=ab, tgi=tgi: e.activation(out=tgi[:], in_=ab, func=AF.Tanh, scale=0.5), reads=[ak], writes=[tgk])
                    ak2, ab2 = fm_tile(wv, wk, t4, oaT, okeys, TQ)
                    S.add(DVE, lambda e, ab2=ab2, tgi=tgi: e.scalar_tensor_tensor(out=m1[:], in0=tgi[:], scalar=1.0, in1=ab2, op0=ALU.add, op1=ALU.mult),
                          reads=[ak2, tgk], writes=["m1"])
                    S.add(DVE, lambda e, t8=t8: e.tensor_tensor(out=mg[:, t8, :], in0=mg[:, t8, :], in1=m1[:], op=ALU.add),
                          reads=["m1", ("mg", t8)], writes=[("mg", t8)])
            mkeys = [("mg", q) for q in range(KC)]

            dbg("p4")
            wv0, wk0 = wget(U_WO)
            wv1, wk1 = wget(U_WO + 1, held=1)
            prev_back = None
            for tt in range(4):
                for hf in range(2):
                    wv, wk = (wv0, wk0) if hf == 0 else (wv1, wk1)
                    ak, ab = next_acc()

                    def f(e, wv=wv, ab=ab, tt=tt):
                        ins = None
                        for kc in range(KC):
                            ins = e.matmul(ab, lhsT=mg[:, kc, tt * 128:(tt + 1) * 128], rhs=wv[:, kc * 512:(kc + 1) * 512],
                                           start=(kc == 0), stop=(kc == KC - 1))
                        return ins
                    S.add(PE, f, reads=[wk] + mkeys, writes=[ak])
                    hv = h[:, tt, hf * 512:(hf + 1) * 512]
                    S.add(DVE, lambda e, ab=ab, hv=hv: e.scalar_tensor_tensor(out=hv, in0=ab, scalar=0.5, in1=hv, op0=ALU.mult, op1=ALU.add),
                          reads=[ak, ("h", tt)], writes=[("h", tt)] if hf == 1 else [("hx", tt)])
                if prev_back is not None:
                    prev_back()
                prev_back = norm_tile(h[:, tt, :], ("h", tt), 128, hnT, [("hnT", tt)], tt * 128, defer=True)
            prev_back()

            dbg("p6")
            for ug in range(11):
                wv, wk = wget(U_WGU + ug)
                for t2 in range(2):
                    j = 2 * ug + t2
                    gk, gb_ = fm_tile(wv, wk, 2 * t2, hnT, hkeys, TQ)
                    uk, ub_ = fm_tile(wv, wk, 2 * t2 + 1, hnT, hkeys, TQ)
                    S.add(ACT, lambda e, gb_=gb_: e.activation(out=swt[:], in_=gb_, func=AF.Tanh, scale=0.5), reads=[gk], writes=["swt"])
                    S.add(DVE, lambda e, gb_=gb_: e.scalar_tensor_tensor(out=swa[:], in0=swt[:], scalar=1.0, in1=gb_, op0=ALU.add, op1=ALU.mult),
                          reads=[gk, "swt"], writes=["swa"])
                    S.add(DVE, lambda e, ub_=ub_, j=j: e.tensor_tensor(out=actT[:, j, :], in0=ub_, in1=swa[:], op=ALU.mult),
                          reads=[uk, "swa"], writes=[("actT", j)])

            dbg("p7")
            pbacks = p1_prefetch(*nxt) if nxt is not None else []
            for hf in range(2):
                accs = [next_acc() for _ in range(4)]
                for g3 in range(3):
                    wv, wk = wget(U_WD + hf * 3 + g3)
                    nj = 8 if g3 < 2 else 6
                    for jj in range(nj):
                        j = g3 * 8 + jj
                        if hf == 1 and pbacks and j in (1, 7, 13, 19):
                            pbacks.pop(0)()
                        for tt in range(4):
                            ak, ab = accs[tt]
                            S.add(PE, lambda e, wv=wv, jj=jj, j=j, tt=tt, ab=ab: e.matmul(ab, lhsT=actT[:, j, tt * 128:(tt + 1) * 128], rhs=wv[:, jj * 512:(jj + 1) * 512],
                                                                                        start=(j == 0), stop=(j == NJ - 1)),
                                  reads=[wk, ("actT", j)], writes=[ak])
                for tt in range(4):
                    ak, ab = accs[tt]
                    hv = h[:, tt, hf * 512:(hf + 1) * 512]
                    S.add(DVE, lambda e, ab=ab, hv=hv: e.scalar_tensor_tensor(out=hv, in0=ab, scalar=0.5, in1=hv, op0=ALU.mult, op1=ALU.add),
                          reads=[ak, ("h", tt)], writes=[("h", tt)] if hf == 1 else [("hx", tt)])

            for tt in range(4):
                S.add(ACT, lambda e, tt=tt: e.activation(out=junk[:], in_=h[:, tt, :], func=AF.Square, accum_out=stat[:, 40 + tt:41 + tt]),
                      reads=[("h", tt)], writes=["junk", ("fs", tt)])
                S.add(DVE, lambda e, tt=tt: e.tensor_scalar(out=stat[:, 44 + tt:45 + tt], in0=stat[:, 40 + tt:41 + tt], scalar1=1.0 / D, scalar2=EPS,
                                                            op0=ALU.mult, op1=ALU.add), reads=[("fs", tt)], writes=[("fv", tt)])
                S.add(POOL, lambda e, tt=tt: e.tensor_tensor(out=stat[:, 48 + tt:49 + tt], in0=stat[:, 44 + tt:45 + tt], in1=nhalf[:], op=ALU.pow),
                      reads=[("fv", tt), "nhalf"], writes=[("fr", tt)])
                S.add(DVE, lambda e, tt=tt: e.scalar_tensor_tensor(out=h[:, tt, :], in0=h[:, tt, :], scalar=stat[:, 48 + tt:49 + tt], in1=nfin[:],
                                                                   op0=ALU.mult, op1=ALU.mult),
                      reads=[("h", tt), ("fr", tt), "nfin"], writes=[("h", tt)])
                S.add(POOL, lambda e, tt=tt: e.dma_start(out=y_d[s, t0 + tt * 128: t0 + (tt + 1) * 128, :], in_=h[:, tt, :]),
                      reads=[("h", tt)], dma="y%d" % tt)

        order = [(s_, c_) for s_ in range(nseq) for c_ in range(nch)]
        for b_ in p1_prefetch(*order[0]):
            b_()
        try:
            for i_, (s_, c_) in enumerate(order):
                do_chunk(s_, c_, order[i_ + 1] if i_ + 1 < len(order) else None)
            S.emit(final_dma_keys=["y0", "y1", "y2", "y3"])
        except _Stop:
            pass
    return nc


def _host_consts(T):
    pos = np.arange(NM + T, dtype=np.float32)
    inv_freq = (1.0 / (10000.0 ** (np.arange(0, 64, 2, dtype=np.float32) / np.float32(64)))).astype(np.float32)
    ang = pos[:, None] * inv_freq[None, :]
    ang = np.concatenate([ang, ang], axis=-1)
    cos = np.cos(ang).astype(np.float32).T
    sin = np.sin(ang).astype(np.float32).T
    sgn = np.where(np.arange(64) < 32, -1.0, 1.0).astype(np.float32)[:, None]
    cos128 = np.concatenate([cos, cos], 0)
    sin128 = np.concatenate([sin * sgn, sin * sgn], 0)
    bf = ml_dtypes.bfloat16
    ident = np.eye(128, dtype=np.float32).astype(bf)
    perm = np.zeros((128, 128), np.float32)
    for m in range(128):
        k = (m % 64 + 32) % 64 + 64 * (m // 64)
        perm[k, m] = 1.0
    mask = (np.arange(128)[None, :] >= np.arange(128)[:, None]).astype(np.float32)
    return (np.ascontiguousarray(cos128), np.ascontiguousarray(sin128), ident, perm.astype(bf), mask.astype(bf))


def _a_unit(W, cols):
    return W[:, cols].reshape(KC, 128, 512).transpose(1, 0, 2).reshape(128, 4096)


def _host_units(w_in, wpc, wpa, wo, wgu, wd):
    units = np.zeros((NU, 128, 4096), np.float32)
    starts = {U_Q: 0, U_K: 1024, U_V: 2048, U_CB: 3072, U_CC: 4096, U_CX: 5120, U_GA: 6144, U_GB: 7168}
    for u0, c0 in starts.items():
        for i in range(2):
            units[u0 + i] = _a_unit(w_in, np.arange(c0 + 512 * i, c0 + 512 * (i + 1)))
    for i in range(2):
        units[U_WPC + i] = _a_unit(wpc, np.arange(512 * i, 512 * (i + 1)))
        units[U_WPA + i] = _a_unit(wpa, np.arange(512 * i, 512 * (i + 1)))
        units[U_WO + i] = _a_unit(wo, np.arange(512 * i, 512 * (i + 1)))
    for i in range(11):
        cols = np.concatenate([np.arange(128 * (2 * i), 128 * (2 * i + 1)), DFF + np.arange(128 * (2 * i), 128 * (2 * i + 1)),
                               np.arange(128 * (2 * i + 1), 128 * (2 * i + 2)), DFF + np.arange(128 * (2 * i + 1), 128 * (2 * i + 2))])
        units[U_WGU + i] = _a_unit(wgu, cols)
    for hf in range(2):
        for g3 in range(3):
            nj = 8 if g3 < 2 else 6
            blk = wd[1024 * g3: 1024 * g3 + 128 * nj, 512 * hf: 512 * (hf + 1)].reshape(nj, 128, 512).transpose(1, 0, 2).reshape(128, nj * 512)
            units[U_WD + hf * 3 + g3, :, :nj * 512] = blk
    return units


def _host_inputs(inputs, nseq, nch, ncores):
    T = nch * TQ
    f = lambda a: np.ascontiguousarray(np.asarray(a, dtype=np.float32))
    cos128, sin128, ident, perm, mask = _host_consts(T)
    units = _host_units(f(inputs["w_in"])[0], f(inputs["w_proj_conv"])[0], f(inputs["w_proj_attn"])[0], f(inputs["w_out"])[0],
                        f(inputs["w_gate_up"])[0], f(inputs["w_down"])[0])
    col8 = lambda v: np.ascontiguousarray(f(v).reshape(KC, 128).T)
    convw = np.ascontiguousarray(f(inputs["conv_w"])[0].reshape(3, KC, 128).transpose(2, 1, 0).reshape(128, KC * 3))
    lam = np.concatenate([f(inputs["lambda_q1"])[0], f(inputs["lambda_k1"])[0], f(inputs["lambda_q2"])[0], f(inputs["lambda_k2"])[0]])
    common = dict(
        meta=f(inputs["meta_tokens"]), wun=units, cos=cos128, sin=sin128, ident=ident, perm=perm, mask=mask,
        nmw=col8(inputs["norm_mix_w"][0]), nfw=col8(inputs["norm_ffn_w"][0]), convw=convw,
        subw=np.ascontiguousarray(f(inputs["subln_w"])[0].reshape(128, 1)),
        nfin=np.ascontiguousarray(np.broadcast_to(f(inputs["norm_final_w"])[None, :], (128, D))),
        lam=np.ascontiguousarray(np.broadcast_to(lam[None, :], (128, 256))),
    )
    x = f(inputs["x"])
    maps = []
    for ci in range(ncores):
        m = dict(common)
        m["x"] = np.ascontiguousarray(x[ci * nseq:(ci + 1) * nseq, :T])
        maps.append(m)
    return maps


def kernel(**inputs):
    nseq = BATCH // N_CORES
    nch = SEQ // TQ
    maps = _host_inputs(inputs, nseq, nch, N_CORES)
    nc = build(nseq, nch)
    res = run_bass_kernel_spmd(nc, maps, core_ids=list(range(N_CORES)))
    out = np.concatenate([np.asarray(r["y"]) for r in res.results], axis=0)
    return out.astype(np.float32)
```

```python
import math
import numpy as np
import ml_dtypes
from contextlib import ExitStack
from collections import defaultdict
import concourse.bass as bass
import concourse.mybir as mybir
from concourse.bass_utils import run_bass_kernel_spmd

F32 = mybir.dt.float32
BF16 = mybir.dt.bfloat16
AF = mybir.ActivationFunctionType
ALU = mybir.AluOpType

PE, ACT, DVE, POOL, SP = "tensor", "scalar", "vector", "gpsimd", "sync"
ENGS = (PE, ACT, DVE, POOL, SP)

D = 1024
KC = 8
NH = 8
NM = 16
TQ = 512
DFF = 2816
NJ = 22
EPS = 1e-5
LAMBDA_INIT = 0.8 - 0.6 * math.exp(-0.3 * 0)
N_CORES = 8
BATCH = 32
SEQ = 2048

U_GB, U_CC, U_CX, U_CB, U_GA, U_Q, U_K, U_V = 0, 2, 4, 6, 8, 10, 12, 14
U_WPC, U_WPA, U_WO, U_WGU, U_WD = 16, 18, 20, 22, 33
NU = 39
NSLOT = 3


class _Op:
    __slots__ = ("eng", "fn", "waits", "semkey", "value", "is_dma", "idx")


class Sched:
    def __init__(self, nc):
        self.nc = nc
        self.ops = []
        self.last_writer = {}
        self.readers = defaultdict(list)
        self.overlaps = defaultdict(list)
        self.count = defaultdict(int)
        self.seen = {e: {} for e in ENGS}
        self.dma_keys = []
        self.psum_keys = set()
        self.access = defaultdict(dict)

    def alias(self, a_keys, b_keys):
        for a in a_keys:
            for b in b_keys:
                self.overlaps[a].append(b)
                self.overlaps[b].append(a)

    def add(self, eng, fn, reads=(), writes=(), dma=None):
        op = _Op()
        op.eng, op.fn, op.is_dma = eng, fn, dma is not None
        op.idx = len(self.ops)
        deps = {}

        def dep(o, raw):
            if o is None or o is op:
                return
            same = (o.eng == eng) and not o.is_dma and not op.is_dma
            if same and (eng == PE or eng == SP or not raw):
                return
            deps[o.idx] = o

        for k in reads:
            dep(self.last_writer.get(k), True)
        for k in writes:
            for kk in [k] + self.overlaps.get(k, []):
                dep(self.last_writer.get(kk), False)
                for r in self.readers.get(kk, ()):
                    dep(r, False)
        for k in list(reads) + list(writes):
            if k in self.psum_keys:
                for kk in [k] + self.overlaps.get(k, []):
                    for e2, o in self.access[kk].items():
                        if e2 != eng:
                            deps[o.idx] = o
                self.access[k][eng] = op
        waits = {}
        for o in deps.values():
            if waits.get(o.semkey, 0) < o.value:
                waits[o.semkey] = o.value
        seen = self.seen[eng]
        op.waits = []
        for sk, v in waits.items():
            if seen.get(sk, 0) < v:
                seen[sk] = v
                op.waits.append((sk, v))
        if op.is_dma:
            op.semkey = "dma_" + dma
            if op.semkey not in self.count:
                self.dma_keys.append(op.semkey)
            self.count[op.semkey] += 16
        else:
            op.semkey = "eng_" + eng
            self.count[op.semkey] += 1
        op.value = self.count[op.semkey]
        for k in reads:
            self.readers[k].append(op)
        for k in writes:
            self.last_writer[k] = op
            self.readers[k] = []
            for kk in self.overlaps.get(k, []):
                self.last_writer[kk] = op
                self.readers[kk] = []
        self.ops.append(op)
        return op

    def emit(self, final_dma_keys=()):
        nc = self.nc
        with ExitStack() as st:
            sems = {}
            for e in ENGS:
                sems["eng_" + e] = st.enter_context(nc.semaphore("s_" + e))
            for k in self.dma_keys:
                sems[k] = st.enter_context(nc.semaphore("s_" + k))
            block = st.enter_context(nc.Block())
            per = {e: [o for o in self.ops if o.eng == e] for e in ENGS}
            finals = [("dma_" + k, self.count["dma_" + k]) for k in final_dma_keys if ("dma_" + k) in sems]

            def make(e):
                def body(eng):
                    for o in per[e]:
                        for sk, v in o.waits:
                            eng.wait_ge(sems[sk], v)
                        ins = o.fn(eng)
                        ins.then_inc(sems[o.semkey], 16 if o.is_dma else 1)
                    if e == SP:
                        for sk, v in finals:
                            eng.wait_ge(sems[sk], v)
                return body

            for e in ENGS:
                if per[e] or e == SP:
                    getattr(block, e)(make(e))
        return nc


DBG = None


class _Stop(Exception):
    pass


def build(nseq, nch):
    T = nch * TQ
    NKB = T // 128
    nc = bass.Bass("TRN2", target_bir_lowering=False)

    def din(name, shape, dt=F32):
        return nc.dram_tensor(name, list(shape), dt, kind="ExternalInput").ap()

    x_d = din("x", [nseq, T, D])
    meta_d = din("meta", [NM, D])
    wun_d = din("wun", [NU, 128, 4096])
    cos_d = din("cos", [128, NM + T])
    sin_d = din("sin", [128, NM + T])
    ident_d = din("ident", [128, 128], BF16)
    perm_d = din("perm", [128, 128], BF16)
    mask_d = din("mask", [128, 128], BF16)
    nmw_d = din("nmw", [128, KC])
    nfw_d = din("nfw", [128, KC])
    convw_d = din("convw", [128, KC * 3])
    subw_d = din("subw", [128, 1])
    nfin_d = din("nfin", [128, D])
    lam_d = din("lam", [128, 4 * 64])
    y_d = nc.dram_tensor("y", [nseq, T, D], F32, kind="ExternalOutput").ap()
    scr_d = nc.dram_tensor("wscr", [NU, 128, 4096], BF16, kind="Internal").ap()

    S = Sched(nc)
    with ExitStack() as st:
        def sb(name, shape, dt):
            return st.enter_context(nc.sbuf_tensor("sb_" + name, list(shape), dt))

        def ps(name, shape, dt):
            return st.enter_context(nc.psum_tensor(name, list(shape), dt))

        KT = sb("KT", [128, NH * T], BF16)
        KTv = KT[:].rearrange("p (h t) -> p h t", h=NH)
        KTm = sb("KTm", [128, NH, NM], BF16)
        Vc = sb("Vc", [128, NKB, NH, 129], BF16)
        Vm = sb("Vm", [NM, NH, 129], BF16)
        cosb = sb("cosb", [128, TQ], F32)
        sinb = sb("sinb", [128, TQ], F32)
        ident = sb("ident", [128, 128], BF16)
        perm = sb("perm", [128, 128], BF16)
        mask = sb("mask", [128, 128], BF16)
        nmw = sb("nmw", [128, KC], F32)
        nfw = sb("nfw", [128, KC], F32)
        convw = sb("convw", [128, KC * 3], F32)
        sws = sb("sws", [128, 1], F32)
        nfin = sb("nfin", [128, D], BF16)
        lamv = sb("lamv", [128, 4 * 64], F32)
        lamt = sb("lamt", [128, 2 * 64], F32)
        lams = sb("lams", [128, 4], F32)
        nlam = sb("nlam", [128, 1], F32)
        nhalf = sb("nhalf", [128, 1], F32)
        h = sb("h", [128, 4, D], F32)
        xin = sb("xin", [128, D], F32)
        xsb = [sb("xs%d" % i, [128, D], BF16) for i in range(4)]
        hnT = sb("hnT", [128, KC, TQ], BF16)
        arena = sb("arena", [128, 4096 + 4096 + KC * 514], BF16)
        QT = arena[:, 0:4096].rearrange("p (h t) -> p h t", h=NH)
        cob = arena[:, 4096:8192].rearrange("p (c t) -> p c t", c=KC)
        ubuf = arena[:, 8192:8192 + KC * 514].rearrange("p (c t) -> p c t", c=KC)
        actT = arena[:, 0:NJ * TQ].rearrange("p (j t) -> p j t", j=NJ)
        tg = [sb("tg%d" % i, [128, TQ], BF16) for i in range(2)]
        mg = sb("mg", [128, KC, TQ], BF16)
        oaT = sb("oaT", [128, NH, TQ], BF16)
        Eb = [sb("E%d" % i, [128, 2, TQ], BF16) for i in range(2)]
        rtb = [sb("rt%d" % i, [128, 1040], F32) for i in range(2)]
        rt = rtb[0]
        ocp = rt[:, 0:8 * 129].rearrange("p (r e) -> p r e", e=129)
        qkbb = [sb("qkb%d" % i, [128, TQ], BF16) for i in range(2)]
        osb = sb("osb", [128, 4, 128], F32)
        ttmp = sb("ttmp", [128, 128], F32)
        oab = sb("oab", [128, 4, 128], BF16)
        junk = sb("junk", [128, D], BF16)
        swt = sb("swt", [128, TQ], BF16)
        swa = sb("swa", [128, TQ], BF16)
        cvb = [sb("cv%d" % i, [128, TQ], F32) for i in range(2)]
        m1 = sb("m1", [128, TQ], BF16)
        stat = sb("stat", [128, 96], F32)
        um = sb("um", [128, KC, NM], BF16)
        uhs = sb("uhs", [128, KC, 2], BF16)
        ccm = sb("ccm", [128, KC, NM], F32)
        wring = sb("wring", [128, NSLOT, 4096], BF16)
        if NH * T // 2 >= 8192:
            stg = KT[:].bitcast(F32)
        else:
            stg = sb("stg", [128, 8192], F32)[:]

        pS = [ps("pS%d" % i, [128, 2, 512], F32) for i in range(2)]
        pO = ps("pO", [128, 3, 512], F32)
        pX = ps("pX", [128, 512], F32)
        banks = [pS[0][:, 0, :], pS[0][:, 1, :], pS[1][:, 0, :], pS[1][:, 1, :],
                 pO[:, 0, :], pO[:, 1, :], pO[:, 2, :], pX[:]]
        pXb = pX[:].bitcast(BF16)

        acc_state = {"i": 0}
        ACCS = [("b%d" % i, banks[i]) for i in range(7)]
        S.alias(["b0", "b1"], ["S0"])
        S.alias(["b2", "b3"], ["S1"])
        S.alias(["b4", "b5", "b6"], ["O"])
        S.alias([("actT", j) for j in range(NJ)],
                [("QT", q) for q in range(NH)] + [("cob", q) for q in range(KC)] + [("u", q) for q in range(KC)] + [("uh", q) for q in range(KC)])
        S.alias([("stg", 0), ("stg", 1)], [("KT", q, cc_) for q in range(NH) for cc_ in range(nch)])
        for sl_ in range(NSLOT):
            S.alias([("w", sl_)], [("wx", sl_, kc_) for kc_ in range(KC)])
        S.alias(["ocp", "ocp1", "ocp2"], ["rt1_0", "rt2_0"])
        S.psum_keys = set(["S0", "S1", "O", "X"] + ["b%d" % i for i in range(7)])

        def next_acc():
            i = acc_state["i"]
            acc_state["i"] = (i + 1) % 7
            return ACCS[i]

        wseq = []
        wst = {"issued": 0, "cons": 0}

        def wget(u, held=0):
            n = wst["cons"]
            assert wseq[n] == u, (n, wseq[n], u)
            while wst["issued"] < min(len(wseq), n - held + NSLOT):
                i = wst["issued"]
                slot = i % NSLOT
                uu = wseq[i]
                S.add(SP, lambda e, slot=slot, uu=uu: e.dma_start(out=wring[:, slot, :], in_=scr_d[uu]),
                      reads=[("scr", uu)], writes=[("w", slot)], dma="w%d" % slot)
                wst["issued"] += 1
            wst["cons"] += 1
            slot = n % NSLOT
            return wring[:, slot, :], ("w", slot)

        meta_units = [U_CC, U_CC + 1, U_CX, U_CX + 1, U_K, U_K + 1, U_V, U_V + 1]
        chunk_units = ([U_CC, U_CC + 1, U_CX, U_CX + 1, U_CB, U_CB + 1, U_Q, U_Q + 1, U_K, U_K + 1, U_V, U_V + 1,
                        U_GB, U_WPC, U_GB + 1, U_WPC + 1, U_GA, U_WPA, U_GA + 1, U_WPA + 1, U_WO, U_WO + 1]
                       + [U_WGU + i for i in range(11)] + [U_WD + i for i in range(6)])
        assert sorted(chunk_units) == list(range(NU))
        wseq.extend(meta_units)
        for _ in range(nseq * nch):
            wseq.extend(chunk_units)
        cast_order = meta_units + [u for u in chunk_units if u not in meta_units]

        def cload(dst, src, key):
            S.add(SP, lambda e: e.dma_start(out=dst, in_=src), writes=[key], dma="c_" + key)

        cload(ident[:], ident_d, "ident")
        cload(perm[:], perm_d, "perm")
        cload(mask[:], mask_d, "mask")
        cload(nmw[:], nmw_d, "nmw")
        cload(nfw[:], nfw_d, "nfw")
        cload(convw[:], convw_d, "convw")
        cload(sws[:], subw_d, "sws0")
        cload(xin[:], nfin_d, "xin")
        S.add(DVE, lambda e: e.tensor_copy(out=nfin[:], in_=xin[:]), reads=["xin"], writes=["nfin"])
        cload(lamv[:], lam_d, "lamv")
        S.add(POOL, lambda e: e.memset(nhalf[:], -0.5), writes=["nhalf"])
        S.add(POOL, lambda e: e.memset(Vc[:, :, :, 128:129], 1.0), writes=["Vones"])
        S.add(POOL, lambda e: e.memset(Vm[:, :, 128:129], 1.0), writes=["Vmones"])
        S.add(DVE, lambda e: e.tensor_scalar(out=sws[:], in0=sws[:], scalar1=float(1.0 - LAMBDA_INIT), scalar2=None,
                                             op0=ALU.mult), reads=["sws0"], writes=["sws"])
        lv = lamv[:].rearrange("p (a b) -> p a b", a=4)
        lt = lamt[:].rearrange("p (a b) -> p a b", a=2)
        S.add(DVE, lambda e: e.tensor_tensor(out=lt[:, 0, :], in0=lv[:, 0, :], in1=lv[:, 1, :], op=ALU.mult),
              reads=["lamv"], writes=["lt0"])
        S.add(DVE, lambda e: e.tensor_tensor(out=lt[:, 1, :], in0=lv[:, 2, :], in1=lv[:, 3, :], op=ALU.mult),
              reads=["lamv"], writes=["lt1"])
        S.add(DVE, lambda e: e.tensor_reduce(out=lams[:, 0:2], in_=lt, op=ALU.add, axis=mybir.AxisListType.X),
              reads=["lt0", "lt1"], writes=["lams01"])
        S.add(ACT, lambda e: e.activation(out=lams[:, 2:4], in_=lams[:, 0:2], func=AF.Exp),
              reads=["lams01"], writes=["lams23"])
        S.add(DVE, lambda e: e.scalar_tensor_tensor(out=nlam[:], in0=lams[:, 3:4], scalar=float(-LAMBDA_INIT),
                                                    in1=lams[:, 2:3], op0=ALU.add, op1=ALU.subtract),
              reads=["lams23"], writes=["nlam"])

        def stage_load(ci_):
            u_ = cast_order[ci_]
            sl_ = ci_ % 2
            sv_ = stg[:, sl_ * 4096:(sl_ + 1) * 4096]
            S.add(SP, lambda e: e.dma_start(out=sv_, in_=wun_d[u_]), writes=[("stg", sl_)], dma="stg%d" % sl_)

        stage_load(0)
        stage_load(1)
        for ci_, u in enumerate(cast_order):
            sl = ci_ % 2
            sv = stg[:, sl * 4096:(sl + 1) * 4096]
            slot = ci_ % NSLOT
            dst = wring[:, slot, :]
            if u < U_WPC or (U_WGU <= u < U_WD):
                sc = nmw if u < U_WPC else nfw
                sck = "nmw" if u < U_WPC else "nfw"
                for kc in range(KC):
                    o_ = dst[:, kc * 512:(kc + 1) * 512]
                    i_ = sv[:, kc * 512:(kc + 1) * 512]
                    if (ci_ + kc) % 2 == 0:
                        S.add(ACT, lambda e, o_=o_, i_=i_, kc=kc, sc=sc: e.activation(out=o_, in_=i_, func=AF.Copy, scale=sc[:, kc:kc + 1]),
                              reads=[("stg", sl), sck], writes=[("wx", slot, kc)])
                    else:
                        S.add(DVE, lambda e, o_=o_, i_=i_, kc=kc, sc=sc: e.tensor_scalar(out=o_, in0=i_, scalar1=sc[:, kc:kc + 1], scalar2=None, op0=ALU.mult),
                              reads=[("stg", sl), sck], writes=[("wx", slot, kc)])
                rk = [("wx", slot, kc) for kc in range(KC)]
            else:
                for hf in range(2):
                    o_ = dst[:, hf * 2048:(hf + 1) * 2048]
                    i_ = sv[:, hf * 2048:(hf + 1) * 2048]
                    if hf == 0:
                        S.add(ACT, lambda e, o_=o_, i_=i_: e.activation(out=o_, in_=i_, func=AF.Copy),
                              reads=[("stg", sl)], writes=[("wx", slot, 0)])
                    else:
                        S.add(DVE, lambda e, o_=o_, i_=i_: e.tensor_copy(out=o_, in_=i_),
                              reads=[("stg", sl)], writes=[("wx", slot, 1)])
                rk = [("wx", slot, 0), ("wx", slot, 1)]
            if ci_ + 2 < len(cast_order):
                stage_load(ci_ + 2)
            S.add(SP, lambda e, dst=dst, u=u: e.dma_start(out=scr_d[u], in_=dst), reads=rk, writes=[("scr", u)], dma="scrw%d" % slot)

        def fm_tile(wv, wk, ci, rhsT, rkeys, n, acc=None):
            ak, ab = acc if acc is not None else next_acc()

            def f(e):
                ins = None
                for kc in range(KC):
                    ins = e.matmul(ab[:, 0:n], lhsT=wv[:, kc * 512 + ci * 128: kc * 512 + (ci + 1) * 128],
                                   rhs=rhsT[:, kc, 0:n], start=(kc == 0), stop=(kc == KC - 1))
                return ins
            S.add(PE, f, reads=[wk] + list(rkeys), writes=[ak])
            return ak, ab

        rp_state = {"i": 0}

        def rope_tile(ak, ab, n, cs, sn, cskeys, dst, dkey):
            i = rp_state["i"]
            rp_state["i"] = i + 1
            b = i % 2
            qkb = qkbb[b]
            rt1 = rtb[b][:, 0:TQ]
            rt2 = rtb[b][:, 520:520 + TQ]
            k1, k2, kq = "rt1_%d" % b, "rt2_%d" % b, "qkb%d" % b
            S.add(ACT, lambda e: e.activation(out=qkb[:, 0:n], in_=ab[:, 0:n], func=AF.Copy), reads=[ak], writes=[kq])
            S.add(PE, lambda e: e.matmul(pX[:, 0:n], lhsT=perm[:], rhs=qkb[:, 0:n], start=True, stop=True),
                  reads=[kq, "perm"], writes=["X"])
            S.add(DVE, lambda e: e.tensor_tensor(out=rt1[:, 0:n], in0=ab[:, 0:n], in1=cs, op=ALU.mult),
                  reads=[ak] + cskeys, writes=[k1])
            S.add(DVE, lambda e: e.tensor_tensor(out=rt2[:, 0:n], in0=pX[:, 0:n], in1=sn, op=ALU.mult),
                  reads=["X"] + cskeys, writes=[k2])
            S.add(POOL, lambda e: e.tensor_tensor(out=dst, in0=rt1[:, 0:n], in1=rt2[:, 0:n], op=ALU.add),
                  reads=[k1, k2], writes=[dkey])

        nt_state = {"i": 0}

        def norm_tile(src, skey, npart, dst_hnT, dkeys, col0, defer=False):
            i = nt_state["i"]
            nt_state["i"] = i + 1
            xs = xsb[i % 4]
            xk = "xs%d" % (i % 4)
            c0 = 64 + 3 * (i % 8)
            sk = "nst%d" % (i % 8)
            S.add(ACT, lambda e: e.activation(out=junk[0:npart, :], in_=src, func=AF.Square, accum_out=stat[0:npart, c0:c0 + 1]),
                  reads=[skey], writes=["junk", sk + "a"])
            S.add(DVE, lambda e: e.tensor_scalar(out=stat[0:npart, c0 + 1:c0 + 2], in0=stat[0:npart, c0:c0 + 1], scalar1=1.0 / D, scalar2=EPS,
                                                 op0=ALU.mult, op1=ALU.add), reads=[sk + "a"], writes=[sk + "b"])
            S.add(POOL, lambda e: e.tensor_tensor(out=stat[0:npart, c0 + 2:c0 + 3], in0=stat[0:npart, c0 + 1:c0 + 2], in1=nhalf[0:npart, :], op=ALU.pow),
                  reads=[sk + "b", "nhalf"], writes=[sk + "c"])
            S.add(ACT, lambda e: e.activation(out=xs[0:npart, :], in_=src, func=AF.Copy, scale=stat[0:npart, c0 + 2:c0 + 3]),
                  reads=[skey, sk + "c"], writes=[xk])

            def back():
                def f(e):
                    ins = None
                    for kc in range(KC):
                        ins = e.transpose(pXb[:, kc * 128: kc * 128 + npart], xs[0:npart, kc * 128:(kc + 1) * 128], ident[0:npart, 0:npart])
                    return ins
                S.add(PE, f, reads=[xk, "ident"], writes=["X"])
                S.add(DVE, lambda e: e.tensor_copy(out=dst_hnT[:, :, col0:col0 + npart],
                                                   in_=pXb.rearrange("p (k t) -> p k t", k=KC)[:, :, 0:npart]),
                      reads=["X"], writes=dkeys)
            if defer:
                return back
            back()
            return None

        S.add(SP, lambda e: e.dma_start(out=xin[0:NM, :], in_=meta_d), writes=["xin"], dma="xin")
        S.add(SP, lambda e: e.dma_start(out=cosb[:, 0:NM], in_=cos_d[:, 0:NM]), writes=["cos"], dma="cos")
        S.add(SP, lambda e: e.dma_start(out=sinb[:, 0:NM], in_=sin_d[:, 0:NM]), writes=["sin"], dma="sin")
        hk_all = [("hnT", tt) for tt in range(4)]
        norm_tile(xin[0:NM, :], "xin", NM, hnT, hk_all, 0)
        for t8 in range(8):
            if t8 % 4 == 0:
                wv, wk = wget(U_CC + t8 // 4)
            ak, ab = fm_tile(wv, wk, t8 % 4, hnT, hk_all, NM)
            S.add(ACT, lambda e, ab=ab, t8=t8: e.activation(out=ccm[:, t8, :], in_=ab[:, 0:NM], func=AF.Copy),
                  reads=[ak], writes=[("ccm", t8)])
        for t8 in range(8):
            if t8 % 4 == 0:
                wv, wk = wget(U_CX + t8 // 4)
            ak, ab = fm_tile(wv, wk, t8 % 4, hnT, hk_all, NM)
            S.add(DVE, lambda e, ab=ab, t8=t8: e.tensor_tensor(out=um[:, t8, :], in0=ab[:, 0:NM], in1=ccm[:, t8, :], op=ALU.mult),
                  reads=[ak, ("ccm", t8)], writes=[("um", t8)])
        for t8 in range(8):
            if t8 % 4 == 0:
                wv, wk = wget(U_K + t8 // 4)
            ak, ab = fm_tile(wv, wk, t8 % 4, hnT, hk_all, NM)
            rope_tile(ak, ab, NM, cosb[:, 0:NM], sinb[:, 0:NM], ["cos", "sin"], KTm[:, t8, :], ("KTm", t8))
        for hf in range(2):
            wv, wk = wget(U_V + hf)
            ak, ab = next_acc()

            def f(e, wv=wv, ab=ab):
                ins = None
                for kc in range(KC):
                    ins = e.matmul(ab[0:NM, :], lhsT=hnT[:, kc, 0:NM], rhs=wv[:, kc * 512:(kc + 1) * 512],
                                   start=(kc == 0), stop=(kc == KC - 1))
                return ins
            S.add(PE, f, reads=[wk] + hk_all, writes=[ak])
            S.add(ACT, lambda e, ab=ab, hf=hf: e.activation(out=Vm[:, hf * 4:(hf + 1) * 4, 0:128],
                                                           in_=ab[0:NM, :].rearrange("p (a b) -> p a b", a=4), func=AF.Copy),
                  reads=[ak, "Vmones"], writes=[("Vm", hf)])

        mview = mask[:]

        def dbg(stage):
            if DBG != stage:
                return
            items = [("QT", arena[:, 0:4096], [128, 4096], BF16), ("cob", arena[:, 4096:8192], [128, 4096], BF16),
                     ("actT", arena[:, 0:NJ * TQ], [128, NJ * TQ], BF16),
                     ("KT", KT[:, 0:4096], [128, 4096], BF16), ("Vc", Vc[:, 0:4, :, :].rearrange("p a b c -> p (a b c)"), [128, 4 * NH * 129], BF16),
                     ("mg", mg[:].rearrange("p a b -> p (a b)"), [128, 4096], BF16),
                     ("oaT", oaT[:].rearrange("p a b -> p (a b)"), [128, 4096], BF16), ("hnT", hnT[:].rearrange("p a b -> p (a b)"), [128, 4096], BF16),
                     ("h", h[:].rearrange("p a b -> p (a b)"), [128, 4 * D], F32), ("stat", stat[:], [128, 96], F32)]
            allk = list(S.last_writer.keys())
            for name, ap_, shp, dt_ in items:
                dd = nc.dram_tensor("dbg_" + name, shp, dt_, kind="ExternalOutput").ap()
                S.add(SP, lambda e, dd=dd, ap_=ap_: e.dma_start(out=dd, in_=ap_), reads=allk, dma="dbg_" + name)
            S.emit(final_dma_keys=["dbg_" + it[0] for it in items])
            raise _Stop()

        def p1_prefetch(s, c):
            t0 = c * TQ
            backs = []
            S.add(SP, lambda e: e.dma_start(out=cosb[:], in_=cos_d[:, NM + t0: NM + t0 + TQ]), writes=["cos"], dma="cos")
            S.add(SP, lambda e: e.dma_start(out=sinb[:], in_=sin_d[:, NM + t0: NM + t0 + TQ]), writes=["sin"], dma="sin")
            for tt in range(4):
                S.add(POOL, lambda e, tt=tt: e.dma_start(out=xin[:], in_=x_d[s, t0 + tt * 128: t0 + (tt + 1) * 128, :]),
                      writes=["xin"], dma="xin")
                backs.append(norm_tile(xin[:], "xin", 128, hnT, [("hnT", tt)], tt * 128, defer=True))
            return backs

        def do_chunk(s, c, nxt):
            t0 = c * TQ
            hkeys = [("hnT", tt) for tt in range(4)]
            for tt in range(4):
                S.add(POOL, lambda e, tt=tt: e.dma_start(out=h[:, tt, :], in_=x_d[s, t0 + tt * 128: t0 + (tt + 1) * 128, :]),
                      writes=[("h", tt)], dma="x%d" % tt)

            dbg("p1")
            for t8 in range(8):
                if t8 % 4 == 0:
                    wv, wk = wget(U_CC + t8 // 4)
                ak, ab = fm_tile(wv, wk, t8 % 4, hnT, hkeys, TQ)
                S.add(ACT, lambda e, ab=ab, t8=t8: e.activation(out=cob[:, t8, :], in_=ab, func=AF.Copy),
                      reads=[ak], writes=[("cob", t8)])
            for t8 in range(8):
                if t8 % 4 == 0:
                    wv, wk = wget(U_CX + t8 // 4)
                ak, ab = fm_tile(wv, wk, t8 % 4, hnT, hkeys, TQ)
                if c == 0:
                    S.add(POOL, lambda e, t8=t8: e.tensor_copy(out=ubuf[:, t8, 0:2], in_=um[:, t8, NM - 2:NM]),
                          reads=[("um", t8)], writes=[("uh", t8)])
                else:
                    S.add(POOL, lambda e, t8=t8: e.tensor_copy(out=ubuf[:, t8, 0:2], in_=uhs[:, t8, :]),
                          reads=[("uhs", t8)], writes=[("uh", t8)])
                S.add(DVE, lambda e, ab=ab, t8=t8: e.tensor_tensor(out=ubuf[:, t8, 2:2 + TQ], in0=ab, in1=cob[:, t8, :], op=ALU.mult),
                      reads=[ak, ("cob", t8), ("uh", t8)], writes=[("u", t8)])
                S.add(POOL, lambda e, t8=t8: e.tensor_copy(out=uhs[:, t8, :], in_=ubuf[:, t8, TQ:TQ + 2]),
                      reads=[("u", t8)], writes=[("uhs", t8)])
            for t8 in range(8):
                if t8 % 4 == 0:
                    wv, wk = wget(U_CB + t8 // 4)
                ak, ab = fm_tile(wv, wk, t8 % 4, hnT, hkeys, TQ)
                cv = cvb[t8 % 2]
                cvk = "cv%d" % (t8 % 2)
                S.add(POOL, lambda e, t8=t8, cv=cv: e.tensor_scalar(out=cv[:], in0=ubuf[:, t8, 0:TQ], scalar1=convw[:, t8 * 3:t8 * 3 + 1],
                                                                    scalar2=0.0, op0=ALU.mult, op1=ALU.add),
                      reads=[("u", t8), ("uh", t8), "convw"], writes=[cvk])
                S.add(DVE, lambda e, t8=t8, cv=cv: e.scalar_tensor_tensor(out=cv[:], in0=ubuf[:, t8, 1:1 + TQ], scalar=convw[:, t8 * 3 + 1:t8 * 3 + 2],
                                                                           in1=cv[:], op0=ALU.mult, op1=ALU.add),
                      reads=[("u", t8), ("uh", t8), "convw", cvk], writes=[cvk])
                S.add(DVE, lambda e, t8=t8, cv=cv: e.scalar_tensor_tensor(out=cv[:], in0=ubuf[:, t8, 2:2 + TQ], scalar=convw[:, t8 * 3 + 2:t8 * 3 + 3],
                                                                           in1=cv[:], op0=ALU.mult, op1=ALU.add),
                      reads=[("u", t8), "convw", cvk], writes=[cvk])
                S.add(DVE, lambda e, ab=ab, t8=t8, cv=cv: e.tensor_tensor(out=cob[:, t8, :], in0=ab, in1=cv[:], op=ALU.mult),
                      reads=[ak, cvk], writes=[("cob", t8)])
            for t8 in range(8):
                if t8 % 4 == 0:
                    wv, wk = wget(U_Q + t8 // 4)
                ak, ab = fm_tile(wv, wk, t8 % 4, hnT, hkeys, TQ)
                rope_tile(ak, ab, TQ, cosb[:], sinb[:], ["cos", "sin"], QT[:, t8, :], ("QT", t8))
            for t8 in range(8):
                if t8 % 4 == 0:
                    wv, wk = wget(U_K + t8 // 4)
                ak, ab = fm_tile(wv, wk, t8 % 4, hnT, hkeys, TQ)
                rope_tile(ak, ab, TQ, cosb[:], sinb[:], ["cos", "sin"], KTv[:, t8, t0:t0 + TQ], ("KT", t8, c))
            for hf in range(2):
                wv, wk = wget(U_V + hf)
                for tt in range(4):
                    ak, ab = next_acc()

                    def f(e, wv=wv, ab=ab, tt=tt):
                        ins = None
                        for kc in range(KC):
                            ins = e.matmul(ab, lhsT=hnT[:, kc, tt * 128:(tt + 1) * 128], rhs=wv[:, kc * 512:(kc + 1) * 512],
                                           start=(kc == 0), stop=(kc == KC - 1))
                        return ins
                    S.add(PE, f, reads=[wk, ("hnT", tt)], writes=[ak])
                    kb = c * 4 + tt
                    S.add(ACT, lambda e, ab=ab, hf=hf, kb=kb: e.activation(out=Vc[:, kb, hf * 4:(hf + 1) * 4, 0:128],
                                                                         in_=ab.rearrange("p (a b) -> p a b", a=4), func=AF.Copy),
                          reads=[ak, "Vones"], writes=[("V", kb, hf)])

            dbg("p2")
            def do_head(hd, pending):
                kbl = [("m", 0, 0)] + [("f", kb, 0) for kb in range(4 * c)] + [("d", 4 * c + i, 128 * i) for i in range(4)]
                nk = len(kbl)
                started = set()

                def qk_step(i):
                    kind, kb, q0 = kbl[i]
                    pSi = pS[i % 2]
                    E = Eb[i % 2]
                    nkeys = NM if kind == "m" else 128

                    def f(e):
                        ins = None
                        for sub in range(2):
                            r0 = sub * 64
                            if kind == "m":
                                lt_ = KTm[r0:r0 + 64, hd, :]
                            else:
                                lt_ = KTv[r0:r0 + 64, hd, kb * 128:(kb + 1) * 128]
                            ins = e.matmul(pSi[0:nkeys, sub, q0:TQ], lhsT=lt_, rhs=QT[r0:r0 + 64, hd, q0:TQ], start=True, stop=True)
                        return ins
                    rk = [("QT", hd)] + ([("KTm", hd)] if kind == "m" else [("KT", hd, kb // 4)])
                    S.add(PE, f, reads=rk, writes=["S%d" % (i % 2)])
                    S.add(ACT, lambda e: e.activation(out=E[0:nkeys, :, q0:TQ], in_=pSi[0:nkeys, :, q0:TQ], func=AF.Exp, scale=0.125),
                          reads=["S%d" % (i % 2)], writes=["E%d" % (i % 2)])
                    if kind == "d":
                        S.add(POOL, lambda e: e.tensor_tensor(out=E[:, :, q0:q0 + 128], in0=E[:, :, q0:q0 + 128],
                                                               in1=mview.unsqueeze(1).to_broadcast([128, 2, 128]), op=ALU.mult),
                              reads=["E%d" % (i % 2), "mask"], writes=["E%d" % (i % 2)])

                def pv_step(i):
                    kind, kb, q0 = kbl[i]
                    E = Eb[i % 2]
                    nkeys = NM if kind == "m" else 128

                    def f(e):
                        ins = None
                        for qi in range(q0 // 128, 4):
                            for sub in range(2):
                                r = qi * 2 + sub
                                bank, off = r // 3, (r % 3) * 129
                                st_ = (bank not in started)
                                started.add(bank)
                                rhs = Vm[:, hd, :] if kind == "m" else Vc[:, kb, hd, :]
                                sp_ = (kind == "d" and kb % 4 == qi)
                                ins = e.matmul(pO[:, bank, off:off + 129], lhsT=E[0:nkeys, sub, qi * 128:(qi + 1) * 128], rhs=rhs,
                                               start=st_, stop=sp_, skip_group_check=True)
                        return ins
                    rk = ["E%d" % (i % 2)] + (["Vmones", ("Vm", hd // 4)] if kind == "m" else ["Vones", ("V", kb, hd // 4)])
                    S.add(PE, f, reads=rk, writes=["O"])

                for i in range(nk + 1):
                    if i < nk:
                        qk_step(i)
                    if i >= 1:
                        pv_step(i - 1)
                    if i == min(nk, 6) and pending is not None:
                        pending()
                        pending = None
                if pending is not None:
                    pending()

                S.add(ACT, lambda e: e.activation(out=rt[:, 0:387], in_=pO[:, 0, 0:387], func=AF.Copy), reads=["O"], writes=["ocp"])
                S.add(DVE, lambda e: e.tensor_copy(out=rt[:, 387:774], in_=pO[:, 1, 0:387]), reads=["O"], writes=["ocp1"])
                S.add(ACT, lambda e: e.activation(out=rt[:, 774:1032], in_=pO[:, 2, 0:258], func=AF.Copy), reads=["O"], writes=["ocp2"])
                ok3 = ["ocp", "ocp1", "ocp2"]
                S.add(DVE, lambda e: e.reciprocal(out=stat[:, 8:16], in_=ocp[:, :, 128]), reads=ok3, writes=["rz"])
                S.add(DVE, lambda e: e.tensor_scalar(out=stat[:, 16:24], in0=stat[:, 8:16], scalar1=nlam[:, 0:1], scalar2=None, op0=ALU.mult),
                      reads=["rz", "nlam"], writes=["rzs"])
                for qi in range(4):
                    r0_, r1_ = 2 * qi, 2 * qi + 1
                    S.add(DVE, lambda e, r1_=r1_: e.tensor_scalar(out=ttmp[:], in0=ocp[:, r1_, 0:128], scalar1=stat[:, 16 + r1_:17 + r1_], scalar2=None, op0=ALU.mult),
                          reads=ok3 + ["rzs"], writes=["ttmp"])
                    S.add(DVE, lambda e, r0_=r0_, qi=qi: e.scalar_tensor_tensor(out=osb[:, qi, :], in0=ocp[:, r0_, 0:128], scalar=stat[:, 8 + r0_:9 + r0_], in1=ttmp[:],
                                                                                  op0=ALU.mult, op1=ALU.add),
                          reads=ok3 + ["rz", "ttmp"], writes=[("osb", qi)])
                    S.add(DVE, lambda e, qi=qi: e.scalar_tensor_tensor(out=junk[:, 0:128], in0=osb[:, qi, :], scalar=1.0, in1=osb[:, qi, :],
                                                                      op0=ALU.mult, op1=ALU.mult, accum_out=stat[:, 24 + qi:25 + qi]),
                          reads=[("osb", qi)], writes=["junkd", ("ss", qi)])
                S.add(DVE, lambda e: e.tensor_scalar(out=stat[:, 28:32], in0=stat[:, 24:28], scalar1=1.0 / 128, scalar2=EPS, op0=ALU.mult, op1=ALU.add),
                      reads=[("ss", q) for q in range(4)], writes=["ssv"])
                S.add(POOL, lambda e: e.tensor_tensor(out=stat[:, 32:36], in0=stat[:, 28:32], in1=nhalf[:, 0:1].to_broadcast([128, 4]), op=ALU.pow),
                      reads=["ssv", "nhalf"], writes=["srs"])
                S.add(DVE, lambda e: e.tensor_tensor(out=oab[:], in0=osb[:], in1=stat[:, 32:36].unsqueeze(2).to_broadcast([128, 4, 128]), op=ALU.mult),
                      reads=[("osb", q) for q in range(4)] + ["srs"], writes=["oab"])

                def finish():
                    def ftr(e):
                        ins = None
                        for qi in range(4):
                            ins = e.transpose(pXb[:, qi * 128:(qi + 1) * 128], oab[:, qi, :], ident[:])
                        return ins
                    S.add(PE, ftr, reads=["oab", "ident"], writes=["X"])
                    S.add(DVE, lambda e: e.tensor_scalar(out=oaT[:, hd, :], in0=pXb[:, 0:TQ], scalar1=sws[:, 0:1], scalar2=None, op0=ALU.mult),
                          reads=["X", "sws"], writes=[("oaT", hd)])
                return finish

            pend = None
            for hd_ in range(NH):
                pend = do_head(hd_, pend)

            dbg("p3")
            okeys = [("oaT", q) for q in range(NH)]
            ckeys = [("cob", q) for q in range(KC)]
            for half in range(2):
                gv, gk = wget(U_GB + half)
                wv, wk = wget(U_WPC + half, held=1)
                for t4 in range(4):
                    t8 = half * 4 + t4
                    ak, ab = fm_tile(gv, gk, t4, hnT, hkeys, TQ)
                    tgi = tg[t8 % 2]
                    tgk = "tg%d" % (t8 % 2)
                    S.add(ACT, lambda e, ab=ab, tgi=tgi: e.activation(out=tgi[:], in_=ab, func=AF.Tanh, scale=0.5), reads=[ak], writes=[tgk])
                    ak2, ab2 = fm_tile(wv, wk, t4, cob, ckeys, TQ)
                    S.add(DVE, lambda e, ab2=ab2, tgi=tgi, t8=t8: e.scalar_tensor_tensor(out=mg[:, t8, :], in0=tgi[:], scalar=1.0, in1=ab2, op0=ALU.add, op1=ALU.mult),
                          reads=[ak2, tgk], writes=[("mg", t8)])
                    if half == 0 and t4 == 1 and pend is not None:
                        pend()
                        pend = None
            for half in range(2):
                gv, gk = wget(U_GA + half)
                wv, wk = wget(U_WPA + half, held=1)
                for t4 in range(4):
                    t8 = half * 4 + t4
                    ak, ab = fm_tile(gv, gk, t4, hnT, hkeys, TQ)
                    tgi = tg[t8 % 2]
                    tgk = "tg%d" % (t8 % 2)
                    S.add(ACT, lambda e, ab=ab, tgi=tgi: e.activation(out=tgi[:], in_=ab, func=AF.Tanh, scale=0.5), reads=[ak], writes=[tgk])
                    ak2, ab2 = fm_tile(wv, wk, t4, oaT, okeys, TQ)
                    S.add(DVE, lambda e, ab2=ab2, tgi=tgi: e.scalar_tensor_tensor(out=m1[:], in0=tgi[:], scalar=1.0, in1=ab2, op0=ALU.add, op1=ALU.mult),
                          reads=[ak2, tgk], writes=["m1"])
                    S.add(DVE, lambda e, t8=t8: e.tensor_tensor(out=mg[:, t8, :], in0=mg[:, t8, :], in1=m1[:], op=ALU.add),
                          reads=["m1", ("mg", t8)], writes=[("mg", t8)])
            mkeys = [("mg", q) for q in range(KC)]

            dbg("p4")
            wv0, wk0 = wget(U_WO)
            wv1, wk1 = wget(U_WO + 1, held=1)
            prev_back = None
            for tt in range(4):
                for hf in range(2):
                    wv, wk = (wv0, wk0) if hf == 0 else (wv1, wk1)
                    ak, ab = next_acc()

                    def f(e, wv=wv, ab=ab, tt=tt):
                        ins = None
                        for kc in range(KC):
                            ins = e.matmul(ab, lhsT=mg[:, kc, tt * 128:(tt + 1) * 128], rhs=wv[:, kc * 512:(kc + 1) * 512],
                                           start=(kc == 0), stop=(kc == KC - 1))
                        return ins
                    S.add(PE, f, reads=[wk] + mkeys, writes=[ak])
                    hv = h[:, tt, hf * 512:(hf + 1) * 512]
                    S.add(DVE, lambda e, ab=ab, hv=hv: e.scalar_tensor_tensor(out=hv, in0=ab, scalar=0.5, in1=hv, op0=ALU.mult, op1=ALU.add),
                          reads=[ak, ("h", tt)], writes=[("h", tt)] if hf == 1 else [("hx", tt)])
                if prev_back is not None:
                    prev_back()
                prev_back = norm_tile(h[:, tt, :], ("h", tt), 128, hnT, [("hnT", tt)], tt * 128, defer=True)
            prev_back()

            dbg("p6")
            for ug in range(11):
                wv, wk = wget(U_WGU + ug)
                for t2 in range(2):
                    j = 2 * ug + t2
                    gk, gb_ = fm_tile(wv, wk, 2 * t2, hnT, hkeys, TQ)
                    uk, ub_ = fm_tile(wv, wk, 2 * t2 + 1, hnT, hkeys, TQ)
                    S.add(ACT, lambda e, gb_=gb_: e.activation(out=swt[:], in_=gb_, func=AF.Tanh, scale=0.5), reads=[gk], writes=["swt"])
                    S.add(DVE, lambda e, gb_=gb_: e.scalar_tensor_tensor(out=swa[:], in0=swt[:], scalar=1.0, in1=gb_, op0=ALU.add, op1=ALU.mult),
                          reads=[gk, "swt"], writes=["swa"])
                    S.add(DVE, lambda e, ub_=ub_, j=j: e.tensor_tensor(out=actT[:, j, :], in0=ub_, in1=swa[:], op=ALU.mult),
                          reads=[uk, "swa"], writes=[("actT", j)])

            dbg("p7")
            pbacks = p1_prefetch(*nxt) if nxt is not None else []
            for hf in range(2):
                accs = [next_acc() for _ in range(4)]
                for g3 in range(3):
                    wv, wk = wget(U_WD + hf * 3 + g3)
                    nj = 8 if g3 < 2 else 6
                    for jj in range(nj):
                        j = g3 * 8 + jj
                        if hf == 1 and pbacks and j in (1, 7, 13, 19):
                            pbacks.pop(0)()
                        for tt in range(4):
                            ak, ab = accs[tt]
                            S.add(PE, lambda e, wv=wv, jj=jj, j=j, tt=tt, ab=ab: e.matmul(ab, lhsT=actT[:, j, tt * 128:(tt + 1) * 128], rhs=wv[:, jj * 512:(jj + 1) * 512],
                                                                                        start=(j == 0), stop=(j == NJ - 1)),
                                  reads=[wk, ("actT", j)], writes=[ak])
                for tt in range(4):
                    ak, ab = accs[tt]
                    hv = h[:, tt, hf * 512:(hf + 1) * 512]
                    S.add(DVE, lambda e, ab=ab, hv=hv: e.scalar_tensor_tensor(out=hv, in0=ab, scalar=0.5, in1=hv, op0=ALU.mult, op1=ALU.add),
                          reads=[ak, ("h", tt)], writes=[("h", tt)] if hf == 1 else [("hx", tt)])

            for tt in range(4):
                S.add(ACT, lambda e, tt=tt: e.activation(out=junk[:], in_=h[:, tt, :], func=AF.Square, accum_out=stat[:, 40 + tt:41 + tt]),
                      reads=[("h", tt)], writes=["junk", ("fs", tt)])
                S.add(DVE, lambda e, tt=tt: e.tensor_scalar(out=stat[:, 44 + tt:45 + tt], in0=stat[:, 40 + tt:41 + tt], scalar1=1.0 / D, scalar2=EPS,
                                                            op0=ALU.mult, op1=ALU.add), reads=[("fs", tt)], writes=[("fv", tt)])
                S.add(POOL, lambda e, tt=tt: e.tensor_tensor(out=stat[:, 48 + tt:49 + tt], in0=stat[:, 44 + tt:45 + tt], in1=nhalf[:], op=ALU.pow),
                      reads=[("fv", tt), "nhalf"], writes=[("fr", tt)])
                S.add(DVE, lambda e, tt=tt: e.scalar_tensor_tensor(out=h[:, tt, :], in0=h[:, tt, :], scalar=stat[:, 48 + tt:49 + tt], in1=nfin[:],
                                                                   op0=ALU.mult, op1=ALU.mult),
                      reads=[("h", tt), ("fr", tt), "nfin"], writes=[("h", tt)])
                S.add(POOL, lambda e, tt=tt: e.dma_start(out=y_d[s, t0 + tt * 128: t0 + (tt + 1) * 128, :], in_=h[:, tt, :]),
                      reads=[("h", tt)], dma="y%d" % tt)

        order = [(s_, c_) for s_ in range(nseq) for c_ in range(nch)]
        for b_ in p1_prefetch(*order[0]):
            b_()
        try:
            for i_, (s_, c_) in enumerate(order):
                do_chunk(s_, c_, order[i_ + 1] if i_ + 1 < len(order) else None)
            S.emit(final_dma_keys=["y0", "y1", "y2", "y3"])
        except _Stop:
            pass
    return nc


def _host_consts(T):
    pos = np.arange(NM + T, dtype=np.float32)
    inv_freq = (1.0 / (10000.0 ** (np.arange(0, 64, 2, dtype=np.float32) / np.float32(64)))).astype(np.float32)
    ang = pos[:, None] * inv_freq[None, :]
    ang = np.concatenate([ang, ang], axis=-1)
    cos = np.cos(ang).astype(np.float32).T
    sin = np.sin(ang).astype(np.float32).T
    sgn = np.where(np.arange(64) < 32, -1.0, 1.0).astype(np.float32)[:, None]
    cos128 = np.concatenate([cos, cos], 0)
    sin128 = np.concatenate([sin * sgn, sin * sgn], 0)
    bf = ml_dtypes.bfloat16
    ident = np.eye(128, dtype=np.float32).astype(bf)
    perm = np.zeros((128, 128), np.float32)
    for m in range(128):
        k = (m % 64 + 32) % 64 + 64 * (m // 64)
        perm[k, m] = 1.0
    mask = (np.arange(128)[None, :] >= np.arange(128)[:, None]).astype(np.float32)
    return (np.ascontiguousarray(cos128), np.ascontiguousarray(sin128), ident, perm.astype(bf), mask.astype(bf))


def _a_unit(W, cols):
    return W[:, cols].reshape(KC, 128, 512).transpose(1, 0, 2).reshape(128, 4096)


def _host_units(w_in, wpc, wpa, wo, wgu, wd):
    units = np.zeros((NU, 128, 4096), np.float32)
    starts = {U_Q: 0, U_K: 1024, U_V: 2048, U_CB: 3072, U_CC: 4096, U_CX: 5120, U_GA: 6144, U_GB: 7168}
    for u0, c0 in starts.items():
        for i in range(2):
            units[u0 + i] = _a_unit(w_in, np.arange(c0 + 512 * i, c0 + 512 * (i + 1)))
    for i in range(2):
        units[U_WPC + i] = _a_unit(wpc, np.arange(512 * i, 512 * (i + 1)))
        units[U_WPA + i] = _a_unit(wpa, np.arange(512 * i, 512 * (i + 1)))
        units[U_WO + i] = _a_unit(wo, np.arange(512 * i, 512 * (i + 1)))
    for i in range(11):
        cols = np.concatenate([np.arange(128 * (2 * i), 128 * (2 * i + 1)), DFF + np.arange(128 * (2 * i), 128 * (2 * i + 1)),
                               np.arange(128 * (2 * i + 1), 128 * (2 * i + 2)), DFF + np.arange(128 * (2 * i + 1), 128 * (2 * i + 2))])
        units[U_WGU + i] = _a_unit(wgu, cols)
    for hf in range(2):
        for g3 in range(3):
            nj = 8 if g3 < 2 else 6
            blk = wd[1024 * g3: 1024 * g3 + 128 * nj, 512 * hf: 512 * (hf + 1)].reshape(nj, 128, 512).transpose(1, 0, 2).reshape(128, nj * 512)
            units[U_WD + hf * 3 + g3, :, :nj * 512] = blk
    return units


def _host_inputs(inputs, nseq, nch, ncores):
    T = nch * TQ
    f = lambda a: np.ascontiguousarray(np.asarray(a, dtype=np.float32))
    cos128, sin128, ident, perm, mask = _host_consts(T)
    units = _host_units(f(inputs["w_in"])[0], f(inputs["w_proj_conv"])[0], f(inputs["w_proj_attn"])[0], f(inputs["w_out"])[0],
                        f(inputs["w_gate_up"])[0], f(inputs["w_down"])[0])
    col8 = lambda v: np.ascontiguousarray(f(v).reshape(KC, 128).T)
    convw = np.ascontiguousarray(f(inputs["conv_w"])[0].reshape(3, KC, 128).transpose(2, 1, 0).reshape(128, KC * 3))
    lam = np.concatenate([f(inputs["lambda_q1"])[0], f(inputs["lambda_k1"])[0], f(inputs["lambda_q2"])[0], f(inputs["lambda_k2"])[0]])
    common = dict(
        meta=f(inputs["meta_tokens"]), wun=units, cos=cos128, sin=sin128, ident=ident, perm=perm, mask=mask,
        nmw=col8(inputs["norm_mix_w"][0]), nfw=col8(inputs["norm_ffn_w"][0]), convw=convw,
        subw=np.ascontiguousarray(f(inputs["subln_w"])[0].reshape(128, 1)),
        nfin=np.ascontiguousarray(np.broadcast_to(f(inputs["norm_final_w"])[None, :], (128, D))),
        lam=np.ascontiguousarray(np.broadcast_to(lam[None, :], (128, 256))),
    )
    x = f(inputs["x"])
    maps = []
    for ci in range(ncores):
        m = dict(common)
        m["x"] = np.ascontiguousarray(x[ci * nseq:(ci + 1) * nseq, :T])
        maps.append(m)
    return maps


def kernel(**inputs):
    nseq = BATCH // N_CORES
    nch = SEQ // TQ
    maps = _host_inputs(inputs, nseq, nch, N_CORES)
    nc = build(nseq, nch)
    res = run_bass_kernel_spmd(nc, maps, core_ids=list(range(N_CORES)))
    out = np.concatenate([np.asarray(r["y"]) for r in res.results], axis=0)
    return out.astype(np.float32)
```

```python
import math
import numpy as np
import ml_dtypes
from contextlib import ExitStack
from collections import defaultdict
import concourse.bass as bass
import concourse.mybir as mybir
from concourse.bass_utils import run_bass_kernel_spmd

F32 = mybir.dt.float32
BF16 = mybir.dt.bfloat16
AF = mybir.ActivationFunctionType
ALU = mybir.AluOpType

PE, ACT, DVE, POOL, SP = "tensor", "scalar", "vector", "gpsimd", "sync"
ENGS = (PE, ACT, DVE, POOL, SP)

D = 1024
KC = 8
NH = 8
NM = 16
TQ = 512
DFF = 2816
NJ = 22
EPS = 1e-5
LAMBDA_INIT = 0.8 - 0.6 * math.exp(-0.3 * 0)
N_CORES = 8
BATCH = 32
SEQ = 2048

U_GB, U_CC, U_CX, U_CB, U_GA, U_Q, U_K, U_V = 0, 2, 4, 6, 8, 10, 12, 14
U_WPC, U_WPA, U_WO, U_WGU, U_WD = 16, 18, 20, 22, 33
NU = 39
NSLOT = 3


class _Op:
    __slots__ = ("eng", "fn", "waits", "semkey", "value", "is_dma", "idx")


class Sched:
    def __init__(self, nc):
        self.nc = nc
        self.ops = []
        self.last_writer = {}
        self.readers = defaultdict(list)
        self.overlaps = defaultdict(list)
        self.count = defaultdict(int)
        self.seen = {e: {} for e in ENGS}
        self.dma_keys = []
        self.psum_keys = set()
        self.access = defaultdict(dict)

    def alias(self, a_keys, b_keys):
        for a in a_keys:
            for b in b_keys:
                self.overlaps[a].append(b)
                self.overlaps[b].append(a)

    def add(self, eng, fn, reads=(), writes=(), dma=None):
        op = _Op()
        op.eng, op.fn, op.is_dma = eng, fn, dma is not None
        op.idx = len(self.ops)
        deps = {}

        def dep(o, raw):
            if o is None or o is op:
                return
            same = (o.eng == eng) and not o.is_dma and not op.is_dma
            if same and (eng == PE or eng == SP or not raw):
                return
            deps[o.idx] = o

        for k in reads:
            dep(self.last_writer.get(k), True)
        for k in writes:
            for kk in [k] + self.overlaps.get(k, []):
                dep(self.last_writer.get(kk), False)
                for r in self.readers.get(kk, ()):
                    dep(r, False)
        for k in list(reads) + list(writes):
            if k in self.psum_keys:
                for kk in [k] + self.overlaps.get(k, []):
                    for e2, o in self.access[kk].items():
                        if e2 != eng:
                            deps[o.idx] = o
                self.access[k][eng] = op
        waits = {}
        for o in deps.values():
            if waits.get(o.semkey, 0) < o.value:
                waits[o.semkey] = o.value
        seen = self.seen[eng]
        op.waits = []
        for sk, v in waits.items():
            if seen.get(sk, 0) < v:
                seen[sk] = v
                op.waits.append((sk, v))
        if op.is_dma:
            op.semkey = "dma_" + dma
            if op.semkey not in self.count:
                self.dma_keys.append(op.semkey)
            self.count[op.semkey] += 16
        else:
            op.semkey = "eng_" + eng
            self.count[op.semkey] += 1
        op.value = self.count[op.semkey]
        for k in reads:
            self.readers[k].append(op)
        for k in writes:
            self.last_writer[k] = op
            self.readers[k] = []
            for kk in self.overlaps.get(k, []):
                self.last_writer[kk] = op
                self.readers[kk] = []
        self.ops.append(op)
        return op

    def emit(self, final_dma_keys=()):
        nc = self.nc
        with ExitStack() as st:
            sems = {}
            for e in ENGS:
                sems["eng_" + e] = st.enter_context(nc.semaphore("s_" + e))
            for k in self.dma_keys:
                sems[k] = st.enter_context(nc.semaphore("s_" + k))
            block = st.enter_context(nc.Block())
            per = {e: [o for o in self.ops if o.eng == e] for e in ENGS}
            finals = [("dma_" + k, self.count["dma_" + k]) for k in final_dma_keys if ("dma_" + k) in sems]

            def make(e):
                def body(eng):
                    for o in per[e]:
                        for sk, v in o.waits:
                            eng.wait_ge(sems[sk], v)
                        ins = o.fn(eng)
                        ins.then_inc(sems[o.semkey], 16 if o.is_dma else 1)
                    if e == SP:
                        for sk, v in finals:
                            eng.wait_ge(sems[sk], v)
                return body

            for e in ENGS:
                if per[e] or e == SP:
                    getattr(block, e)(make(e))
        return nc


DBG = None


class _Stop(Exception):
    pass


def build(nseq, nch):
    T = nch * TQ
    NKB = T // 128
    nc = bass.Bass("TRN2", target_bir_lowering=False)

    def din(name, shape, dt=F32):
        return nc.dram_tensor(name, list(shape), dt, kind="ExternalInput").ap()

    x_d = din("x", [nseq, T, D])
    meta_d = din("meta", [NM, D])
    wun_d = din("wun", [NU, 128, 4096])
    cos_d = din("cos", [128, NM + T])
    sin_d = din("sin", [128, NM + T])
    ident_d = din("ident", [128, 128], BF16)
    perm_d = din("perm", [128, 128], BF16)
    mask_d = din("mask", [128, 128], BF16)
    nmw_d = din("nmw", [128, KC])
    nfw_d = din("nfw", [128, KC])
    convw_d = din("convw", [128, KC * 3])
    subw_d = din("subw", [128, 1])
    nfin_d = din("nfin", [128, D])
    lam_d = din("lam", [128, 4 * 64])
    y_d = nc.dram_tensor("y", [nseq, T, D], F32, kind="ExternalOutput").ap()
    scr_d = nc.dram_tensor("wscr", [NU, 128, 4096], BF16, kind="Internal").ap()

    S = Sched(nc)
    with ExitStack() as st:
        def sb(name, shape, dt):
            return st.enter_context(nc.sbuf_tensor("sb_" + name, list(shape), dt))

        def ps(name, shape, dt):
            return st.enter_context(nc.psum_tensor(name, list(shape), dt))

        KT = sb("KT", [128, NH * T], BF16)
        KTv = KT[:].rearrange("p (h t) -> p h t", h=NH)
        KTm = sb("KTm", [128, NH, NM], BF16)
        Vc = sb("Vc", [128, NKB, NH, 129], BF16)
        Vm = sb("Vm", [NM, NH, 129], BF16)
        cosb = sb("cosb", [128, TQ], F32)
        sinb = sb("sinb", [128, TQ], F32)
        ident = sb("ident", [128, 128], BF16)
        perm = sb("perm", [128, 128], BF16)
        mask = sb("mask", [128, 128], BF16)
        nmw = sb("nmw", [128, KC], F32)
        nfw = sb("nfw", [128, KC], F32)
        convw = sb("convw", [128, KC * 3], F32)
        sws = sb("sws", [128, 1], F32)
        nfin = sb("nfin", [128, D], BF16)
        lamv = sb("lamv", [128, 4 * 64], F32)
        lamt = sb("lamt", [128, 2 * 64], F32)
        lams = sb("lams", [128, 4], F32)
        nlam = sb("nlam", [128, 1], F32)
        nhalf = sb("nhalf", [128, 1], F32)
        h = sb("h", [128, 4, D], F32)
        xin = sb("xin", [128, D], F32)
        xsb = [sb("xs%d" % i, [128, D], BF16) for i in range(4)]
        hnT = sb("hnT", [128, KC, TQ], BF16)
        arena = sb("arena", [128, 4096 + 4096 + KC * 514], BF16)
        QT = arena[:, 0:4096].rearrange("p (h t) -> p h t", h=NH)
        cob = arena[:, 4096:8192].rearrange("p (c t) -> p c t", c=KC)
        ubuf = arena[:, 8192:8192 + KC * 514].rearrange("p (c t) -> p c t", c=KC)
        actT = arena[:, 0:NJ * TQ].rearrange("p (j t) -> p j t", j=NJ)
        tg = [sb("tg%d" % i, [128, TQ], BF16) for i in range(2)]
        mg = sb("mg", [128, KC, TQ], BF16)
        oaT = sb("oaT", [128, NH, TQ], BF16)
        Eb = [sb("E%d" % i, [128, 2, TQ], BF16) for i in range(2)]
        rtb = [sb("rt%d" % i, [128, 1040], F32) for i in range(2)]
        rt = rtb[0]
        ocp = rt[:, 0:8 * 129].rearrange("p (r e) -> p r e", e=129)
        qkbb = [sb("qkb%d" % i, [128, TQ], BF16) for i in range(2)]
        osb = sb("osb", [128, 4, 128], F32)
        ttmp = sb("ttmp", [128, 128], F32)
        oab = sb("oab", [128, 4, 128], BF16)
        junk = sb("junk", [128, D], BF16)
        swt = sb("swt", [128, TQ], BF16)
        swa = sb("swa", [128, TQ], BF16)
        cvb = [sb("cv%d" % i, [128, TQ], F32) for i in range(2)]
        m1 = sb("m1", [128, TQ], BF16)
        stat = sb("stat", [128, 96], F32)
        um = sb("um", [128, KC, NM], BF16)
        uhs = sb("uhs", [128, KC, 2], BF16)
        ccm = sb("ccm", [128, KC, NM], F32)
        wring = sb("wring", [128, NSLOT, 4096], BF16)
        if NH * T // 2 >= 8192:
            stg = KT[:].bitcast(F32)
        else:
            stg = sb("stg", [128, 8192], F32)[:]

        pS = [ps("pS%d" % i, [128, 2, 512], F32) for i in range(2)]
        pO = ps("pO", [128, 3, 512], F32)
        pX = ps("pX", [128, 512], F32)
        banks = [pS[0][:, 0, :], pS[0][:, 1, :], pS[1][:, 0, :], pS[1][:, 1, :],
                 pO[:, 0, :], pO[:, 1, :], pO[:, 2, :], pX[:]]
        pXb = pX[:].bitcast(BF16)

        acc_state = {"i": 0}
        ACCS = [("b%d" % i, banks[i]) for i in range(7)]
        S.alias(["b0", "b1"], ["S0"])
        S.alias(["b2", "b3"], ["S1"])
        S.alias(["b4", "b5", "b6"], ["O"])
        S.alias([("actT", j) for j in range(NJ)],
                [("QT", q) for q in range(NH)] + [("cob", q) for q in range(KC)] + [("u", q) for q in range(KC)] + [("uh", q) for q in range(KC)])
        S.alias([("stg", 0), ("stg", 1)], [("KT", q, cc_) for q in range(NH) for cc_ in range(nch)])
        for sl_ in range(NSLOT):
            S.alias([("w", sl_)], [("wx", sl_, kc_) for kc_ in range(KC)])
        S.alias(["ocp", "ocp1", "ocp2"], ["rt1_0", "rt2_0"])
        S.psum_keys = set(["S0", "S1", "O", "X"] + ["b%d" % i for i in range(7)])

        def next_acc():
            i = acc_state["i"]
            acc_state["i"] = (i + 1) % 7
            return ACCS[i]

        wseq = []
        wst = {"issued": 0, "cons": 0}

        def wget(u, held=0):
            n = wst["cons"]
            assert wseq[n] == u, (n, wseq[n], u)
            while wst["issued"] < min(len(wseq), n - held + NSLOT):
                i = wst["issued"]
                slot = i % NSLOT
                uu = wseq[i]
                S.add(SP, lambda e, slot=slot, uu=uu: e.dma_start(out=wring[:, slot, :], in_=scr_d[uu]),
                      reads=[("scr", uu)], writes=[("w", slot)], dma="w%d" % slot)
                wst["issued"] += 1
            wst["cons"] += 1
            slot = n % NSLOT
            return wring[:, slot, :], ("w", slot)

        meta_units = [U_CC, U_CC + 1, U_CX, U_CX + 1, U_K, U_K + 1, U_V, U_V + 1]
        chunk_units = ([U_CC, U_CC + 1, U_CX, U_CX + 1, U_CB, U_CB + 1, U_Q, U_Q + 1, U_K, U_K + 1, U_V, U_V + 1,
                        U_GB, U_WPC, U_GB + 1, U_WPC + 1, U_GA, U_WPA, U_GA + 1, U_WPA + 1, U_WO, U_WO + 1]
                       + [U_WGU + i for i in range(11)] + [U_WD + i for i in range(6)])
        assert sorted(chunk_units) == list(range(NU))
        wseq.extend(meta_units)
        for _ in range(nseq * nch):
            wseq.extend(chunk_units)
        cast_order = meta_units + [u for u in chunk_units if u not in meta_units]

        def cload(dst, src, key):
            S.add(SP, lambda e: e.dma_start(out=dst, in_=src), writes=[key], dma="c_" + key)

        cload(ident[:], ident_d, "ident")
        cload(perm[:], perm_d, "perm")
        cload(mask[:], mask_d, "mask")
        cload(nmw[:], nmw_d, "nmw")
        cload(nfw[:], nfw_d, "nfw")
        cload(convw[:], convw_d, "convw")
        cload(sws[:], subw_d, "sws0")
        cload(xin[:], nfin_d, "xin")
        S.add(DVE, lambda e: e.tensor_copy(out=nfin[:], in_=xin[:]), reads=["xin"], writes=["nfin"])
        cload(lamv[:], lam_d, "lamv")
        S.add(POOL, lambda e: e.memset(nhalf[:], -0.5), writes=["nhalf"])
        S.add(POOL, lambda e: e.memset(Vc[:, :, :, 128:129], 1.0), writes=["Vones"])
        S.add(POOL, lambda e: e.memset(Vm[:, :, 128:129], 1.0), writes=["Vmones"])
        S.add(DVE, lambda e: e.tensor_scalar(out=sws[:], in0=sws[:], scalar1=float(1.0 - LAMBDA_INIT), scalar2=None,
                                             op0=ALU.mult), reads=["sws0"], writes=["sws"])
        lv = lamv[:].rearrange("p (a b) -> p a b", a=4)
        lt = lamt[:].rearrange("p (a b) -> p a b", a=2)
        S.add(DVE, lambda e: e.tensor_tensor(out=lt[:, 0, :], in0=lv[:, 0, :], in1=lv[:, 1, :], op=ALU.mult),
              reads=["lamv"], writes=["lt0"])
        S.add(DVE, lambda e: e.tensor_tensor(out=lt[:, 1, :], in0=lv[:, 2, :], in1=lv[:, 3, :], op=ALU.mult),
              reads=["lamv"], writes=["lt1"])
        S.add(DVE, lambda e: e.tensor_reduce(out=lams[:, 0:2], in_=lt, op=ALU.add, axis=mybir.AxisListType.X),
              reads=["lt0", "lt1"], writes=["lams01"])
        S.add(ACT, lambda e: e.activation(out=lams[:, 2:4], in_=lams[:, 0:2], func=AF.Exp),
              reads=["lams01"], writes=["lams23"])
        S.add(DVE, lambda e: e.scalar_tensor_tensor(out=nlam[:], in0=lams[:, 3:4], scalar=float(-LAMBDA_INIT),
                                                    in1=lams[:, 2:3], op0=ALU.add, op1=ALU.subtract),
              reads=["lams23"], writes=["nlam"])

        def stage_load(ci_):
            u_ = cast_order[ci_]
            sl_ = ci_ % 2
            sv_ = stg[:, sl_ * 4096:(sl_ + 1) * 4096]
            S.add(SP, lambda e: e.dma_start(out=sv_, in_=wun_d[u_]), writes=[("stg", sl_)], dma="stg%d" % sl_)

        stage_load(0)
        stage_load(1)
        for ci_, u in enumerate(cast_order):
            sl = ci_ % 2
            sv = stg[:, sl * 4096:(sl + 1) * 4096]
            slot = ci_ % NSLOT
            dst = wring[:, slot, :]
            if u < U_WPC or (U_WGU <= u < U_WD):
                sc = nmw if u < U_WPC else nfw
                sck = "nmw" if u < U_WPC else "nfw"
                for kc in range(KC):
                    o_ = dst[:, kc * 512:(kc + 1) * 512]
                    i_ = sv[:, kc * 512:(kc + 1) * 512]
                    if (ci_ + kc) % 2 == 0:
                        S.add(ACT, lambda e, o_=o_, i_=i_, kc=kc, sc=sc: e.activation(out=o_, in_=i_, func=AF.Copy, scale=sc[:, kc:kc + 1]),
                              reads=[("stg", sl), sck], writes=[("wx", slot, kc)])
                    else:
                        S.add(DVE, lambda e, o_=o_, i_=i_, kc=kc, sc=sc: e.tensor_scalar(out=o_, in0=i_, scalar1=sc[:, kc:kc + 1], scalar2=None, op0=ALU.mult),
                              reads=[("stg", sl), sck], writes=[("wx", slot, kc)])
                rk = [("wx", slot, kc) for kc in range(KC)]
            else:
                for hf in range(2):
                    o_ = dst[:, hf * 2048:(hf + 1) * 2048]
                    i_ = sv[:, hf * 2048:(hf + 1) * 2048]
                    if hf == 0:
                        S.add(ACT, lambda e, o_=o_, i_=i_: e.activation(out=o_, in_=i_, func=AF.Copy),
                              reads=[("stg", sl)], writes=[("wx", slot, 0)])
                    else:
                        S.add(DVE, lambda e, o_=o_, i_=i_: e.tensor_copy(out=o_, in_=i_),
                              reads=[("stg", sl)], writes=[("wx", slot, 1)])
                rk = [("wx", slot, 0), ("wx", slot, 1)]
            if ci_ + 2 < len(cast_order):
                stage_load(ci_ + 2)
            S.add(SP, lambda e, dst=dst, u=u: e.dma_start(out=scr_d[u], in_=dst), reads=rk, writes=[("scr", u)], dma="scrw%d" % slot)

        def fm_tile(wv, wk, ci, rhsT, rkeys, n, acc=None):
            ak, ab = acc if acc is not None else next_acc()

            def f(e):
                ins = None
                for kc in range(KC):
                    ins = e.matmul(ab[:, 0:n], lhsT=wv[:, kc * 512 + ci * 128: kc * 512 + (ci + 1) * 128],
                                   rhs=rhsT[:, kc, 0:n], start=(kc == 0), stop=(kc == KC - 1))
                return ins
            S.add(PE, f, reads=[wk] + list(rkeys), writes=[ak])
            return ak, ab

        rp_state = {"i": 0}

        def rope_tile(ak, ab, n, cs, sn, cskeys, dst, dkey, defer=False):
            i = rp_state["i"]
            rp_state["i"] = i + 1
            b = i % 2
            qkb = qkbb[b]
            rt1 = rtb[b][:, 0:TQ]
            rt2 = rtb[b][:, 520:520 + TQ]
            k1, k2, kq = "rt1_%d" % b, "rt2_%d" % b, "qkb%d" % b
            S.add(ACT, lambda e: e.activation(out=qkb[:, 0:n], in_=ab[:, 0:n], func=AF.Copy), reads=[ak], writes=[kq])

            def back():
                S.add(PE, lambda e: e.matmul(pX[:, 0:n], lhsT=perm[:], rhs=qkb[:, 0:n], start=True, stop=True),
                      reads=[kq, "perm"], writes=["X"])
                S.add(DVE, lambda e: e.tensor_tensor(out=rt1[:, 0:n], in0=ab[:, 0:n], in1=cs, op=ALU.mult),
                      reads=[ak] + cskeys, writes=[k1])
                S.add(DVE, lambda e: e.tensor_tensor(out=rt2[:, 0:n], in0=pX[:, 0:n], in1=sn, op=ALU.mult),
                      reads=["X"] + cskeys, writes=[k2])
                S.add(POOL, lambda e: e.tensor_tensor(out=dst, in0=rt1[:, 0:n], in1=rt2[:, 0:n], op=ALU.add),
                      reads=[k1, k2], writes=[dkey])
            if defer:
                return back
            back()
            return None

        nt_state = {"i": 0}

        def norm_tile(src, skey, npart, dst_hnT, dkeys, col0, defer=False):
            i = nt_state["i"]
            nt_state["i"] = i + 1
            xs = xsb[i % 4]
            xk = "xs%d" % (i % 4)
            c0 = 64 + 3 * (i % 8)
            sk = "nst%d" % (i % 8)
            S.add(ACT, lambda e: e.activation(out=junk[0:npart, :], in_=src, func=AF.Square, accum_out=stat[0:npart, c0:c0 + 1]),
                  reads=[skey], writes=["junk", sk + "a"])
            S.add(DVE, lambda e: e.tensor_scalar(out=stat[0:npart, c0 + 1:c0 + 2], in0=stat[0:npart, c0:c0 + 1], scalar1=1.0 / D, scalar2=EPS,
                                                 op0=ALU.mult, op1=ALU.add), reads=[sk + "a"], writes=[sk + "b"])
            S.add(POOL, lambda e: e.tensor_tensor(out=stat[0:npart, c0 + 2:c0 + 3], in0=stat[0:npart, c0 + 1:c0 + 2], in1=nhalf[0:npart, :], op=ALU.pow),
                  reads=[sk + "b", "nhalf"], writes=[sk + "c"])
            S.add(ACT, lambda e: e.activation(out=xs[0:npart, :], in_=src, func=AF.Copy, scale=stat[0:npart, c0 + 2:c0 + 3]),
                  reads=[skey, sk + "c"], writes=[xk])

            def back():
                def f(e):
                    ins = None
                    for kc in range(KC):
                        ins = e.transpose(pXb[:, kc * 128: kc * 128 + npart], xs[0:npart, kc * 128:(kc + 1) * 128], ident[0:npart, 0:npart])
                    return ins
                S.add(PE, f, reads=[xk, "ident"], writes=["X"])
                S.add(DVE, lambda e: e.tensor_copy(out=dst_hnT[:, :, col0:col0 + npart],
                                                   in_=pXb.rearrange("p (k t) -> p k t", k=KC)[:, :, 0:npart]),
                      reads=["X"], writes=dkeys)
            if defer:
                return back
            back()
            return None

        S.add(SP, lambda e: e.dma_start(out=xin[0:NM, :], in_=meta_d), writes=["xin"], dma="xin")
        S.add(SP, lambda e: e.dma_start(out=cosb[:, 0:NM], in_=cos_d[:, 0:NM]), writes=["cos"], dma="cos")
        S.add(SP, lambda e: e.dma_start(out=sinb[:, 0:NM], in_=sin_d[:, 0:NM]), writes=["sin"], dma="sin")
        hk_all = [("hnT", tt) for tt in range(4)]
        norm_tile(xin[0:NM, :], "xin", NM, hnT, hk_all, 0)
        for t8 in range(8):
            if t8 % 4 == 0:
                wv, wk = wget(U_CC + t8 // 4)
            ak, ab = fm_tile(wv, wk, t8 % 4, hnT, hk_all, NM)
            S.add(ACT, lambda e, ab=ab, t8=t8: e.activation(out=ccm[:, t8, :], in_=ab[:, 0:NM], func=AF.Copy),
                  reads=[ak], writes=[("ccm", t8)])
        for t8 in range(8):
            if t8 % 4 == 0:
                wv, wk = wget(U_CX + t8 // 4)
            ak, ab = fm_tile(wv, wk, t8 % 4, hnT, hk_all, NM)
            S.add(DVE, lambda e, ab=ab, t8=t8: e.tensor_tensor(out=um[:, t8, :], in0=ab[:, 0:NM], in1=ccm[:, t8, :], op=ALU.mult),
                  reads=[ak, ("ccm", t8)], writes=[("um", t8)])
        for t8 in range(8):
            if t8 % 4 == 0:
                wv, wk = wget(U_K + t8 // 4)
            ak, ab = fm_tile(wv, wk, t8 % 4, hnT, hk_all, NM)
            rope_tile(ak, ab, NM, cosb[:, 0:NM], sinb[:, 0:NM], ["cos", "sin"], KTm[:, t8, :], ("KTm", t8))
        for hf in range(2):
            wv, wk = wget(U_V + hf)
            ak, ab = next_acc()

            def f(e, wv=wv, ab=ab):
                ins = None
                for kc in range(KC):
                    ins = e.matmul(ab[0:NM, :], lhsT=hnT[:, kc, 0:NM], rhs=wv[:, kc * 512:(kc + 1) * 512],
                                   start=(kc == 0), stop=(kc == KC - 1))
                return ins
            S.add(PE, f, reads=[wk] + hk_all, writes=[ak])
            S.add(ACT, lambda e, ab=ab, hf=hf: e.activation(out=Vm[:, hf * 4:(hf + 1) * 4, 0:128],
                                                           in_=ab[0:NM, :].rearrange("p (a b) -> p a b", a=4), func=AF.Copy),
                  reads=[ak, "Vmones"], writes=[("Vm", hf)])

        mview = mask[:]

        def dbg(stage):
            if DBG != stage:
                return
            items = [("QT", arena[:, 0:4096], [128, 4096], BF16), ("cob", arena[:, 4096:8192], [128, 4096], BF16),
                     ("actT", arena[:, 0:NJ * TQ], [128, NJ * TQ], BF16),
                     ("KT", KT[:, 0:4096], [128, 4096], BF16), ("Vc", Vc[:, 0:4, :, :].rearrange("p a b c -> p (a b c)"), [128, 4 * NH * 129], BF16),
                     ("mg", mg[:].rearrange("p a b -> p (a b)"), [128, 4096], BF16),
                     ("oaT", oaT[:].rearrange("p a b -> p (a b)"), [128, 4096], BF16), ("hnT", hnT[:].rearrange("p a b -> p (a b)"), [128, 4096], BF16),
                     ("h", h[:].rearrange("p a b -> p (a b)"), [128, 4 * D], F32), ("stat", stat[:], [128, 96], F32)]
            allk = list(S.last_writer.keys())
            for name, ap_, shp, dt_ in items:
                dd = nc.dram_tensor("dbg_" + name, shp, dt_, kind="ExternalOutput").ap()
                S.add(SP, lambda e, dd=dd, ap_=ap_: e.dma_start(out=dd, in_=ap_), reads=allk, dma="dbg_" + name)
            S.emit(final_dma_keys=["dbg_" + it[0] for it in items])
            raise _Stop()

        def p1_prefetch(s, c):
            t0 = c * TQ
            backs = []
            S.add(SP, lambda e: e.dma_start(out=cosb[:], in_=cos_d[:, NM + t0: NM + t0 + TQ]), writes=["cos"], dma="cos")
            S.add(SP, lambda e: e.dma_start(out=sinb[:], in_=sin_d[:, NM + t0: NM + t0 + TQ]), writes=["sin"], dma="sin")
            for tt in range(4):
                S.add(POOL, lambda e, tt=tt: e.dma_start(out=xin[:], in_=x_d[s, t0 + tt * 128: t0 + (tt + 1) * 128, :]),
                      writes=["xin"], dma="xin")
                backs.append(norm_tile(xin[:], "xin", 128, hnT, [("hnT", tt)], tt * 128, defer=True))
            return backs

        def do_chunk(s, c, nxt):
            t0 = c * TQ
            hkeys = [("hnT", tt) for tt in range(4)]
            for tt in range(4):
                S.add(POOL, lambda e, tt=tt: e.dma_start(out=h[:, tt, :], in_=x_d[s, t0 + tt * 128: t0 + (tt + 1) * 128, :]),
                      writes=[("h", tt)], dma="x%d" % tt)

            dbg("p1")
            for t8 in range(8):
                if t8 % 4 == 0:
                    wv, wk = wget(U_CC + t8 // 4)
                ak, ab = fm_tile(wv, wk, t8 % 4, hnT, hkeys, TQ)
                S.add(ACT, lambda e, ab=ab, t8=t8: e.activation(out=cob[:, t8, :], in_=ab, func=AF.Copy),
                      reads=[ak], writes=[("cob", t8)])
            for t8 in range(8):
                if t8 % 4 == 0:
                    wv, wk = wget(U_CX + t8 // 4)
                ak, ab = fm_tile(wv, wk, t8 % 4, hnT, hkeys, TQ)
                if c == 0:
                    S.add(POOL, lambda e, t8=t8: e.tensor_copy(out=ubuf[:, t8, 0:2], in_=um[:, t8, NM - 2:NM]),
                          reads=[("um", t8)], writes=[("uh", t8)])
                else:
                    S.add(POOL, lambda e, t8=t8: e.tensor_copy(out=ubuf[:, t8, 0:2], in_=uhs[:, t8, :]),
                          reads=[("uhs", t8)], writes=[("uh", t8)])
                S.add(DVE, lambda e, ab=ab, t8=t8: e.tensor_tensor(out=ubuf[:, t8, 2:2 + TQ], in0=ab, in1=cob[:, t8, :], op=ALU.mult),
                      reads=[ak, ("cob", t8), ("uh", t8)], writes=[("u", t8)])
                S.add(POOL, lambda e, t8=t8: e.tensor_copy(out=uhs[:, t8, :], in_=ubuf[:, t8, TQ:TQ + 2]),
                      reads=[("u", t8)], writes=[("uhs", t8)])
            for t8 in range(8):
                if t8 % 4 == 0:
                    wv, wk = wget(U_CB + t8 // 4)
                ak, ab = fm_tile(wv, wk, t8 % 4, hnT, hkeys, TQ)
                cv = cvb[t8 % 2]
                cvk = "cv%d" % (t8 % 2)
                S.add(POOL, lambda e, t8=t8, cv=cv: e.tensor_scalar(out=cv[:], in0=ubuf[:, t8, 0:TQ], scalar1=convw[:, t8 * 3:t8 * 3 + 1],
                                                                    scalar2=0.0, op0=ALU.mult, op1=ALU.add),
                      reads=[("u", t8), ("uh", t8), "convw"], writes=[cvk])
                S.add(DVE, lambda e, t8=t8, cv=cv: e.scalar_tensor_tensor(out=cv[:], in0=ubuf[:, t8, 1:1 + TQ], scalar=convw[:, t8 * 3 + 1:t8 * 3 + 2],
                                                                           in1=cv[:], op0=ALU.mult, op1=ALU.add),
                      reads=[("u", t8), ("uh", t8), "convw", cvk], writes=[cvk])
                S.add(DVE, lambda e, t8=t8, cv=cv: e.scalar_tensor_tensor(out=cv[:], in0=ubuf[:, t8, 2:2 + TQ], scalar=convw[:, t8 * 3 + 2:t8 * 3 + 3],
                                                                           in1=cv[:], op0=ALU.mult, op1=ALU.add),
                      reads=[("u", t8), "convw", cvk], writes=[cvk])
                S.add(DVE, lambda e, ab=ab, t8=t8, cv=cv: e.tensor_tensor(out=cob[:, t8, :], in0=ab, in1=cv[:], op=ALU.mult),
                      reads=[ak, cvk], writes=[("cob", t8)])
            rpend = None
            for t8 in range(8):
                if t8 % 4 == 0:
                    wv, wk = wget(U_Q + t8 // 4)
                ak, ab = fm_tile(wv, wk, t8 % 4, hnT, hkeys, TQ)
                nb = rope_tile(ak, ab, TQ, cosb[:], sinb[:], ["cos", "sin"], QT[:, t8, :], ("QT", t8), defer=True)
                if rpend is not None:
                    rpend()
                rpend = nb
            for t8 in range(8):
                if t8 % 4 == 0:
                    wv, wk = wget(U_K + t8 // 4)
                ak, ab = fm_tile(wv, wk, t8 % 4, hnT, hkeys, TQ)
                nb = rope_tile(ak, ab, TQ, cosb[:], sinb[:], ["cos", "sin"], KTv[:, t8, t0:t0 + TQ], ("KT", t8, c), defer=True)
                rpend()
                rpend = nb
            for hf in range(2):
                wv, wk = wget(U_V + hf)
                for tt in range(4):
                    ak, ab = next_acc()

                    def f(e, wv=wv, ab=ab, tt=tt):
                        ins = None
                        for kc in range(KC):
                            ins = e.matmul(ab, lhsT=hnT[:, kc, tt * 128:(tt + 1) * 128], rhs=wv[:, kc * 512:(kc + 1) * 512],
                                           start=(kc == 0), stop=(kc == KC - 1))
                        return ins
                    S.add(PE, f, reads=[wk, ("hnT", tt)], writes=[ak])
                    if rpend is not None:
                        rpend()
                        rpend = None
                    kb = c * 4 + tt
                    S.add(ACT, lambda e, ab=ab, hf=hf, kb=kb: e.activation(out=Vc[:, kb, hf * 4:(hf + 1) * 4, 0:128],
                                                                         in_=ab.rearrange("p (a b) -> p a b", a=4), func=AF.Copy),
                          reads=[ak, "Vones"], writes=[("V", kb, hf)])

            dbg("p2")
            def do_head(hd, pending):
                kbl = [("m", 0, 0)] + [("f", kb, 0) for kb in range(4 * c)] + [("d", 4 * c + i, 128 * i) for i in range(4)]
                nk = len(kbl)
                started = set()

                def qk_step(i):
                    kind, kb, q0 = kbl[i]
                    pSi = pS[i % 2]
                    E = Eb[i % 2]
                    nkeys = NM if kind == "m" else 128

                    def f(e):
                        ins = None
                        for sub in range(2):
                            r0 = sub * 64
                            if kind == "m":
                                lt_ = KTm[r0:r0 + 64, hd, :]
                            else:
                                lt_ = KTv[r0:r0 + 64, hd, kb * 128:(kb + 1) * 128]
                            ins = e.matmul(pSi[0:nkeys, sub, q0:TQ], lhsT=lt_, rhs=QT[r0:r0 + 64, hd, q0:TQ], start=True, stop=(kind != "d"),
                                           skip_group_check=True)
                        if kind == "d":
                            for sub in range(2):
                                ins = e.matmul(pSi[:, sub, q0:q0 + 128], lhsT=ident[:], rhs=mask[:], start=False, stop=True, skip_group_check=True)
                        return ins
                    rk = [("QT", hd), "ident", "mask"] + ([("KTm", hd)] if kind == "m" else [("KT", hd, kb // 4)])
                    S.add(PE, f, reads=rk, writes=["S%d" % (i % 2)])
                    S.add(ACT, lambda e: e.activation(out=E[0:nkeys, :, q0:TQ], in_=pSi[0:nkeys, :, q0:TQ], func=AF.Exp, scale=0.125),
                          reads=["S%d" % (i % 2)], writes=["E%d" % (i % 2)])

                def pv_step(i):
                    kind, kb, q0 = kbl[i]
                    E = Eb[i % 2]
                    nkeys = NM if kind == "m" else 128

                    def f(e):
                        ins = None
                        for qi in range(q0 // 128, 4):
                            for sub in range(2):
                                r = qi * 2 + sub
                                bank, off = r // 3, (r % 3) * 129
                                st_ = (bank not in started)
                                started.add(bank)
                                rhs = Vm[:, hd, :] if kind == "m" else Vc[:, kb, hd, :]
                                sp_ = (kind == "d" and kb % 4 == qi)
                                ins = e.matmul(pO[:, bank, off:off + 129], lhsT=E[0:nkeys, sub, qi * 128:(qi + 1) * 128], rhs=rhs,
                                               start=st_, stop=sp_, skip_group_check=True)
                        return ins
                    rk = ["E%d" % (i % 2)] + (["Vmones", ("Vm", hd // 4)] if kind == "m" else ["Vones", ("V", kb, hd // 4)])
                    S.add(PE, f, reads=rk, writes=["O"])

                for i in range(nk + 1):
                    if i < nk:
                        qk_step(i)
                    if i >= 1:
                        pv_step(i - 1)
                    if i == min(nk, 6) and pending is not None:
                        pending()
                        pending = None
                if pending is not None:
                    pending()

                S.add(DVE, lambda e: e.tensor_copy(out=rt[:, 0:387], in_=pO[:, 0, 0:387]), reads=["O"], writes=["ocp"])
                S.add(DVE, lambda e: e.tensor_copy(out=rt[:, 387:774], in_=pO[:, 1, 0:387]), reads=["O"], writes=["ocp1"])
                S.add(DVE, lambda e: e.tensor_copy(out=rt[:, 774:1032], in_=pO[:, 2, 0:258]), reads=["O"], writes=["ocp2"])
                ok3 = ["ocp", "ocp1", "ocp2"]
                S.add(DVE, lambda e: e.reciprocal(out=stat[:, 8:16], in_=ocp[:, :, 128]), reads=ok3, writes=["rz"])
                S.add(DVE, lambda e: e.tensor_scalar(out=stat[:, 16:24], in0=stat[:, 8:16], scalar1=nlam[:, 0:1], scalar2=None, op0=ALU.mult),
                      reads=["rz", "nlam"], writes=["rzs"])
                for qi in range(4):
                    r0_, r1_ = 2 * qi, 2 * qi + 1
                    S.add(DVE, lambda e, r1_=r1_: e.tensor_scalar(out=ttmp[:], in0=ocp[:, r1_, 0:128], scalar1=stat[:, 16 + r1_:17 + r1_], scalar2=None, op0=ALU.mult),
                          reads=ok3 + ["rzs"], writes=["ttmp"])
                    S.add(DVE, lambda e, r0_=r0_, qi=qi: e.scalar_tensor_tensor(out=osb[:, qi, :], in0=ocp[:, r0_, 0:128], scalar=stat[:, 8 + r0_:9 + r0_], in1=ttmp[:],
                                                                                  op0=ALU.mult, op1=ALU.add),
                          reads=ok3 + ["rz", "ttmp"], writes=[("osb", qi)])
                    S.add(DVE, lambda e, qi=qi: e.scalar_tensor_tensor(out=junk[:, 0:128], in0=osb[:, qi, :], scalar=1.0, in1=osb[:, qi, :],
                                                                      op0=ALU.mult, op1=ALU.mult, accum_out=stat[:, 24 + qi:25 + qi]),
                          reads=[("osb", qi)], writes=["junkd", ("ss", qi)])
                S.add(DVE, lambda e: e.tensor_scalar(out=stat[:, 28:32], in0=stat[:, 24:28], scalar1=1.0 / 128, scalar2=EPS, op0=ALU.mult, op1=ALU.add),
                      reads=[("ss", q) for q in range(4)], writes=["ssv"])
                S.add(POOL, lambda e: e.tensor_tensor(out=stat[:, 32:36], in0=stat[:, 28:32], in1=nhalf[:, 0:1].to_broadcast([128, 4]), op=ALU.pow),
                      reads=["ssv", "nhalf"], writes=["srs"])
                S.add(DVE, lambda e: e.tensor_tensor(out=oab[:], in0=osb[:], in1=stat[:, 32:36].unsqueeze(2).to_broadcast([128, 4, 128]), op=ALU.mult),
                      reads=[("osb", q) for q in range(4)] + ["srs"], writes=["oab"])

                def finish():
                    def ftr(e):
                        ins = None
                        for qi in range(4):
                            ins = e.transpose(pXb[:, qi * 128:(qi + 1) * 128], oab[:, qi, :], ident[:])
                        return ins
                    S.add(PE, ftr, reads=["oab", "ident"], writes=["X"])
                    S.add(DVE, lambda e: e.tensor_scalar(out=oaT[:, hd, :], in0=pXb[:, 0:TQ], scalar1=sws[:, 0:1], scalar2=None, op0=ALU.mult),
                          reads=["X", "sws"], writes=[("oaT", hd)])
                return finish

            pend = None
            for hd_ in range(NH):
                pend = do_head(hd_, pend)

            dbg("p3")
            okeys = [("oaT", q) for q in range(NH)]
            ckeys = [("cob", q) for q in range(KC)]
            for half in range(2):
                gv, gk = wget(U_GB + half)
                wv, wk = wget(U_WPC + half, held=1)
                for t4 in range(4):
                    t8 = half * 4 + t4
                    ak, ab = fm_tile(gv, gk, t4, hnT, hkeys, TQ)
                    tgi = tg[t8 % 2]
                    tgk = "tg%d" % (t8 % 2)
                    S.add(ACT, lambda e, ab=ab, tgi=tgi: e.activation(out=tgi[:], in_=ab, func=AF.Tanh, scale=0.5), reads=[ak], writes=[tgk])
                    ak2, ab2 = fm_tile(wv, wk, t4, cob, ckeys, TQ)
                    S.add(DVE, lambda e, ab2=ab2, tgi=tgi, t8=t8: e.scalar_tensor_tensor(out=mg[:, t8, :], in0=tgi[:], scalar=1.0, in1=ab2, op0=ALU.add, op1=ALU.mult),
                          reads=[ak2, tgk], writes=[("mg", t8)])
                    if half == 0 and t4 == 1 and pend is not None:
                        pend()
                        pend = None
            for half in range(2):
                gv, gk = wget(U_GA + half)
                wv, wk = wget(U_WPA + half, held=1)
                for t4 in range(4):
                    t8 = half * 4 + t4
                    ak, ab = fm_tile(gv, gk, t4, hnT, hkeys, TQ)
                    tgi = tg[t8 % 2]
                    tgk = "tg%d" % (t8 % 2)
                    S.add(ACT, lambda e, ab=ab, tgi=tgi: e.activation(out=tgi[:], in_=ab, func=AF.Tanh, scale=0.5), reads=[ak], writes=[tgk])
                    ak2, ab2 = fm_tile(wv, wk, t4, oaT, okeys, TQ)
                    S.add(DVE, lambda e, ab2=ab2, tgi=tgi: e.scalar_tensor_tensor(out=m1[:], in0=tgi[:], scalar=1.0, in1=ab2, op0=ALU.add, op1=ALU.mult),
                          reads=[ak2, tgk], writes=["m1"])
                    S.add(DVE, lambda e, t8=t8: e.tensor_tensor(out=mg[:, t8, :], in0=mg[:, t8, :], in1=m1[:], op=ALU.add),
                          reads=["m1", ("mg", t8)], writes=[("mg", t8)])
            mkeys = [("mg", q) for q in range(KC)]

            dbg("p4")
            wv0, wk0 = wget(U_WO)
            wv1, wk1 = wget(U_WO + 1, held=1)
            prev_back = None
            for tt in range(4):
                for hf in range(2):
                    wv, wk = (wv0, wk0) if hf == 0 else (wv1, wk1)
                    ak, ab = next_acc()

                    def f(e, wv=wv, ab=ab, tt=tt):
                        ins = None
                        for kc in range(KC):
                            ins = e.matmul(ab, lhsT=mg[:, kc, tt * 128:(tt + 1) * 128], rhs=wv[:, kc * 512:(kc + 1) * 512],
                                           start=(kc == 0), stop=(kc == KC - 1))
                        return ins
                    S.add(PE, f, reads=[wk] + mkeys, writes=[ak])
                    hv = h[:, tt, hf * 512:(hf + 1) * 512]
                    S.add(DVE, lambda e, ab=ab, hv=hv: e.scalar_tensor_tensor(out=hv, in0=ab, scalar=0.5, in1=hv, op0=ALU.mult, op1=ALU.add),
                          reads=[ak, ("h", tt)], writes=[("h", tt)] if hf == 1 else [("hx", tt)])
                if prev_back is not None:
                    prev_back()
                prev_back = norm_tile(h[:, tt, :], ("h", tt), 128, hnT, [("hnT", tt)], tt * 128, defer=True)
            prev_back()

            dbg("p6")
            for ug in range(11):
                wv, wk = wget(U_WGU + ug)
                for t2 in range(2):
                    j = 2 * ug + t2
                    gk, gb_ = fm_tile(wv, wk, 2 * t2, hnT, hkeys, TQ)
                    uk, ub_ = fm_tile(wv, wk, 2 * t2 + 1, hnT, hkeys, TQ)
                    S.add(ACT, lambda e, gb_=gb_: e.activation(out=swt[:], in_=gb_, func=AF.Tanh, scale=0.5), reads=[gk], writes=["swt"])
                    S.add(DVE, lambda e, gb_=gb_: e.scalar_tensor_tensor(out=swa[:], in0=swt[:], scalar=1.0, in1=gb_, op0=ALU.add, op1=ALU.mult),
                          reads=[gk, "swt"], writes=["swa"])
                    S.add(DVE, lambda e, ub_=ub_, j=j: e.tensor_tensor(out=actT[:, j, :], in0=ub_, in1=swa[:], op=ALU.mult),
                          reads=[uk, "swa"], writes=[("actT", j)])

            dbg("p7")
            pbacks = p1_prefetch(*nxt) if nxt is not None else []
            for hf in range(2):
                accs = [next_acc() for _ in range(4)]
                for g3 in range(3):
                    wv, wk = wget(U_WD + hf * 3 + g3)
                    nj = 8 if g3 < 2 else 6
                    for jj in range(nj):
                        j = g3 * 8 + jj
                        if hf == 1 and pbacks and j in (1, 7, 13, 19):
                            pbacks.pop(0)()
                        for tt in range(4):
                            ak, ab = accs[tt]
                            S.add(PE, lambda e, wv=wv, jj=jj, j=j, tt=tt, ab=ab: e.matmul(ab, lhsT=actT[:, j, tt * 128:(tt + 1) * 128], rhs=wv[:, jj * 512:(jj + 1) * 512],
                                                                                        start=(j == 0), stop=(j == NJ - 1)),
                                  reads=[wk, ("actT", j)], writes=[ak])
                for tt in range(4):
                    ak, ab = accs[tt]
                    hv = h[:, tt, hf * 512:(hf + 1) * 512]
                    S.add(DVE, lambda e, ab=ab, hv=hv: e.scalar_tensor_tensor(out=hv, in0=ab, scalar=0.5, in1=hv, op0=ALU.mult, op1=ALU.add),
                          reads=[ak, ("h", tt)], writes=[("h", tt)] if hf == 1 else [("hx", tt)])

            for tt in range(4):
                S.add(ACT, lambda e, tt=tt: e.activation(out=junk[:], in_=h[:, tt, :], func=AF.Square, accum_out=stat[:, 40 + tt:41 + tt]),
                      reads=[("h", tt)], writes=["junk", ("fs", tt)])
                S.add(DVE, lambda e, tt=tt: e.tensor_scalar(out=stat[:, 44 + tt:45 + tt], in0=stat[:, 40 + tt:41 + tt], scalar1=1.0 / D, scalar2=EPS,
                                                            op0=ALU.mult, op1=ALU.add), reads=[("fs", tt)], writes=[("fv", tt)])
                S.add(POOL, lambda e, tt=tt: e.tensor_tensor(out=stat[:, 48 + tt:49 + tt], in0=stat[:, 44 + tt:45 + tt], in1=nhalf[:], op=ALU.pow),
                      reads=[("fv", tt), "nhalf"], writes=[("fr", tt)])
                S.add(DVE, lambda e, tt=tt: e.scalar_tensor_tensor(out=h[:, tt, :], in0=h[:, tt, :], scalar=stat[:, 48 + tt:49 + tt], in1=nfin[:],
                                                                   op0=ALU.mult, op1=ALU.mult),
                      reads=[("h", tt), ("fr", tt), "nfin"], writes=[("h", tt)])
                S.add(POOL, lambda e, tt=tt: e.dma_start(out=y_d[s, t0 + tt * 128: t0 + (tt + 1) * 128, :], in_=h[:, tt, :]),
                      reads=[("h", tt)], dma="y%d" % tt)

        order = [(s_, c_) for s_ in range(nseq) for c_ in range(nch)]
        for b_ in p1_prefetch(*order[0]):
            b_()
        try:
            for i_, (s_, c_) in enumerate(order):
                do_chunk(s_, c_, order[i_ + 1] if i_ + 1 < len(order) else None)
            S.emit(final_dma_keys=["y0", "y1", "y2", "y3"])
        except _Stop:
            pass
    return nc


def _host_consts(T):
    pos = np.arange(NM + T, dtype=np.float32)
    inv_freq = (1.0 / (10000.0 ** (np.arange(0, 64, 2, dtype=np.float32) / np.float32(64)))).astype(np.float32)
    ang = pos[:, None] * inv_freq[None, :]
    ang = np.concatenate([ang, ang], axis=-1)
    cos = np.cos(ang).astype(np.float32).T
    sin = np.sin(ang).astype(np.float32).T
    sgn = np.where(np.arange(64) < 32, -1.0, 1.0).astype(np.float32)[:, None]
    cos128 = np.concatenate([cos, cos], 0)
    sin128 = np.concatenate([sin * sgn, sin * sgn], 0)
    bf = ml_dtypes.bfloat16
    ident = np.eye(128, dtype=np.float32).astype(bf)
    perm = np.zeros((128, 128), np.float32)
    for m in range(128):
        k = (m % 64 + 32) % 64 + 64 * (m // 64)
        perm[k, m] = 1.0
    mask = np.where(np.arange(128)[None, :] >= np.arange(128)[:, None], 0.0, -30000.0).astype(np.float32)
    return (np.ascontiguousarray(cos128), np.ascontiguousarray(sin128), ident, perm.astype(bf), mask.astype(bf))


def _a_unit(W, cols):
    return W[:, cols].reshape(KC, 128, 512).transpose(1, 0, 2).reshape(128, 4096)


def _host_units(w_in, wpc, wpa, wo, wgu, wd):
    units = np.zeros((NU, 128, 4096), np.float32)
    starts = {U_Q: 0, U_K: 1024, U_V: 2048, U_CB: 3072, U_CC: 4096, U_CX: 5120, U_GA: 6144, U_GB: 7168}
    for u0, c0 in starts.items():
        for i in range(2):
            units[u0 + i] = _a_unit(w_in, np.arange(c0 + 512 * i, c0 + 512 * (i + 1)))
    for i in range(2):
        units[U_WPC + i] = _a_unit(wpc, np.arange(512 * i, 512 * (i + 1)))
        units[U_WPA + i] = _a_unit(wpa, np.arange(512 * i, 512 * (i + 1)))
        units[U_WO + i] = _a_unit(wo, np.arange(512 * i, 512 * (i + 1)))
    for i in range(11):
        cols = np.concatenate([np.arange(128 * (2 * i), 128 * (2 * i + 1)), DFF + np.arange(128 * (2 * i), 128 * (2 * i + 1)),
                               np.arange(128 * (2 * i + 1), 128 * (2 * i + 2)), DFF + np.arange(128 * (2 * i + 1), 128 * (2 * i + 2))])
        units[U_WGU + i] = _a_unit(wgu, cols)
    for hf in range(2):
        for g3 in range(3):
            nj = 8 if g3 < 2 else 6
            blk = wd[1024 * g3: 1024 * g3 + 128 * nj, 512 * hf: 512 * (hf + 1)].reshape(nj, 128, 512).transpose(1, 0, 2).reshape(128, nj * 512)
            units[U_WD + hf * 3 + g3, :, :nj * 512] = blk
    return units


def _host_inputs(inputs, nseq, nch, ncores):
    T = nch * TQ
    f = lambda a: np.ascontiguousarray(np.asarray(a, dtype=np.float32))
    cos128, sin128, ident, perm, mask = _host_consts(T)
    units = _host_units(f(inputs["w_in"])[0], f(inputs["w_proj_conv"])[0], f(inputs["w_proj_attn"])[0], f(inputs["w_out"])[0],
                        f(inputs["w_gate_up"])[0], f(inputs["w_down"])[0])
    col8 = lambda v: np.ascontiguousarray(f(v).reshape(KC, 128).T)
    convw = np.ascontiguousarray(f(inputs["conv_w"])[0].reshape(3, KC, 128).transpose(2, 1, 0).reshape(128, KC * 3))
    lam = np.concatenate([f(inputs["lambda_q1"])[0], f(inputs["lambda_k1"])[0], f(inputs["lambda_q2"])[0], f(inputs["lambda_k2"])[0]])
    common = dict(
        meta=f(inputs["meta_tokens"]), wun=units, cos=cos128, sin=sin128, ident=ident, perm=perm, mask=mask,
        nmw=col8(inputs["norm_mix_w"][0]), nfw=col8(inputs["norm_ffn_w"][0]), convw=convw,
        subw=np.ascontiguousarray(f(inputs["subln_w"])[0].reshape(128, 1)),
        nfin=np.ascontiguousarray(np.broadcast_to(f(inputs["norm_final_w"])[None, :], (128, D))),
        lam=np.ascontiguousarray(np.broadcast_to(lam[None, :], (128, 256))),
    )
    x = f(inputs["x"])
    maps = []
    for ci in range(ncores):
        m = dict(common)
        m["x"] = np.ascontiguousarray(x[ci * nseq:(ci + 1) * nseq, :T])
        maps.append(m)
    return maps


def kernel(**inputs):
    nseq = BATCH // N_CORES
    nch = SEQ // TQ
    maps = _host_inputs(inputs, nseq, nch, N_CORES)
    nc = build(nseq, nch)
    res = run_bass_kernel_spmd(nc, maps, core_ids=list(range(N_CORES)))
    out = np.concatenate([np.asarray(r["y"]) for r in res.results], axis=0)
    return out.astype(np.float32)
```

```python
import math
import numpy as np
import ml_dtypes
from contextlib import ExitStack
from collections import defaultdict
import concourse.bass as bass
import concourse.mybir as mybir
from concourse.bass_utils import run_bass_kernel_spmd

F32 = mybir.dt.float32
BF16 = mybir.dt.bfloat16
AF = mybir.ActivationFunctionType
ALU = mybir.AluOpType

PE, ACT, DVE, POOL, SP = "tensor", "scalar", "vector", "gpsimd", "sync"
ENGS = (PE, ACT, DVE, POOL, SP)

D = 1024
KC = 8
NH = 8
NM = 16
TQ = 512
DFF = 2816
NJ = 22
EPS = 1e-5
LAMBDA_INIT = 0.8 - 0.6 * math.exp(-0.3 * 0)
N_CORES = 8
BATCH = 32
SEQ = 2048

U_GB, U_CC, U_CX, U_CB, U_GA, U_Q, U_K, U_V = 0, 2, 4, 6, 8, 10, 12, 14
U_WPC, U_WPA, U_WO, U_WGU, U_WD = 16, 18, 20, 22, 33
NU = 39
NSLOT = 3


class _Op:
    __slots__ = ("eng", "fn", "waits", "semkey", "value", "is_dma", "idx")


class Sched:
    def __init__(self, nc):
        self.nc = nc
        self.ops = []
        self.last_writer = {}
        self.readers = defaultdict(list)
        self.overlaps = defaultdict(list)
        self.count = defaultdict(int)
        self.seen = {e: {} for e in ENGS}
        self.dma_keys = []
        self.psum_keys = set()
        self.access = defaultdict(dict)

    def alias(self, a_keys, b_keys):
        for a in a_keys:
            for b in b_keys:
                self.overlaps[a].append(b)
                self.overlaps[b].append(a)

    def add(self, eng, fn, reads=(), writes=(), dma=None):
        op = _Op()
        op.eng, op.fn, op.is_dma = eng, fn, dma is not None
        op.idx = len(self.ops)
        deps = {}

        def dep(o, raw):
            if o is None or o is op:
                return
            same = (o.eng == eng) and not o.is_dma and not op.is_dma
            if same and (eng == PE or eng == SP or not raw):
                return
            deps[o.idx] = o

        for k in reads:
            dep(self.last_writer.get(k), True)
        for k in writes:
            for kk in [k] + self.overlaps.get(k, []):
                dep(self.last_writer.get(kk), False)
                for r in self.readers.get(kk, ()):
                    dep(r, False)
        for k in list(reads) + list(writes):
            if k in self.psum_keys:
                for kk in [k] + self.overlaps.get(k, []):
                    for e2, o in self.access[kk].items():
                        if e2 != eng:
                            deps[o.idx] = o
                self.access[k][eng] = op
        waits = {}
        for o in deps.values():
            if waits.get(o.semkey, 0) < o.value:
                waits[o.semkey] = o.value
        seen = self.seen[eng]
        op.waits = []
        for sk, v in waits.items():
            if seen.get(sk, 0) < v:
                seen[sk] = v
                op.waits.append((sk, v))
        if op.is_dma:
            op.semkey = "dma_" + dma
            if op.semkey not in self.count:
                self.dma_keys.append(op.semkey)
            self.count[op.semkey] += 16
        else:
            op.semkey = "eng_" + eng
            self.count[op.semkey] += 1
        op.value = self.count[op.semkey]
        for k in reads:
            self.readers[k].append(op)
        for k in writes:
            self.last_writer[k] = op
            self.readers[k] = []
            for kk in self.overlaps.get(k, []):
                self.last_writer[kk] = op
                self.readers[kk] = []
        self.ops.append(op)
        return op

    def emit(self, final_dma_keys=()):
        nc = self.nc
        with ExitStack() as st:
            sems = {}
            for e in ENGS:
                sems["eng_" + e] = st.enter_context(nc.semaphore("s_" + e))
            for k in self.dma_keys:
                sems[k] = st.enter_context(nc.semaphore("s_" + k))
            block = st.enter_context(nc.Block())
            per = {e: [o for o in self.ops if o.eng == e] for e in ENGS}
            finals = [("dma_" + k, self.count["dma_" + k]) for k in final_dma_keys if ("dma_" + k) in sems]

            def make(e):
                def body(eng):
                    for o in per[e]:
                        for sk, v in o.waits:
                            eng.wait_ge(sems[sk], v)
                        ins = o.fn(eng)
                        ins.then_inc(sems[o.semkey], 16 if o.is_dma else 1)
                    if e == SP:
                        for sk, v in finals:
                            eng.wait_ge(sems[sk], v)
                return body

            for e in ENGS:
                if per[e] or e == SP:
                    getattr(block, e)(make(e))
        return nc


DBG = None


class _Stop(Exception):
    pass


def build(nseq, nch):
    T = nch * TQ
    NKB = T // 128
    nc = bass.Bass("TRN2", target_bir_lowering=False)

    def din(name, shape, dt=F32):
        return nc.dram_tensor(name, list(shape), dt, kind="ExternalInput").ap()

    x_d = din("x", [nseq, T, D])
    meta_d = din("meta", [NM, D])
    wun_d = din("wun", [NU, 128, 4096])
    cos_d = din("cos", [128, NM + T])
    sin_d = din("sin", [128, NM + T])
    ident_d = din("ident", [128, 128], BF16)
    perm_d = din("perm", [128, 128], BF16)
    mask_d = din("mask", [128, 128], BF16)
    nmw_d = din("nmw", [128, KC])
    nfw_d = din("nfw", [128, KC])
    convw_d = din("convw", [128, KC * 3])
    subw_d = din("subw", [128, 1])
    nfin_d = din("nfin", [128, D])
    lam_d = din("lam", [128, 4 * 64])
    y_d = nc.dram_tensor("y", [nseq, T, D], F32, kind="ExternalOutput").ap()
    scr_d = nc.dram_tensor("wscr", [NU, 128, 4096], BF16, kind="Internal").ap()

    S = Sched(nc)
    with ExitStack() as st:
        def sb(name, shape, dt):
            return st.enter_context(nc.sbuf_tensor("sb_" + name, list(shape), dt))

        def ps(name, shape, dt):
            return st.enter_context(nc.psum_tensor(name, list(shape), dt))

        KT = sb("KT", [128, NH * T], BF16)
        KTv = KT[:].rearrange("p (h t) -> p h t", h=NH)
        KTm = sb("KTm", [128, NH, NM], BF16)
        Vc = sb("Vc", [128, NKB, NH, 129], BF16)
        Vm = sb("Vm", [NM, NH, 129], BF16)
        cosb = sb("cosb", [128, TQ], F32)
        sinb = sb("sinb", [128, TQ], F32)
        ident = sb("ident", [128, 128], BF16)
        perm = sb("perm", [128, 128], BF16)
        mask = sb("mask", [128, 128], BF16)
        nmw = sb("nmw", [128, KC], F32)
        nfw = sb("nfw", [128, KC], F32)
        convw = sb("convw", [128, KC * 3], F32)
        sws = sb("sws", [128, 1], F32)
        nfin = sb("nfin", [128, D], BF16)
        lamv = sb("lamv", [128, 4 * 64], F32)
        lamt = sb("lamt", [128, 2 * 64], F32)
        lams = sb("lams", [128, 4], F32)
        nlam = sb("nlam", [128, 1], F32)
        nhalf = sb("nhalf", [128, 1], F32)
        h = sb("h", [128, 4, D], F32)
        xin = sb("xin", [128, D], F32)
        xsb = [sb("xs%d" % i, [128, D], BF16) for i in range(4)]
        hnT = sb("hnT", [128, KC, TQ], BF16)
        arena = sb("arena", [128, 4096 + 4096 + KC * 514], BF16)
        QT = arena[:, 0:4096].rearrange("p (h t) -> p h t", h=NH)
        cob = arena[:, 4096:8192].rearrange("p (c t) -> p c t", c=KC)
        ubuf = arena[:, 8192:8192 + KC * 514].rearrange("p (c t) -> p c t", c=KC)
        actT = arena[:, 0:NJ * TQ].rearrange("p (j t) -> p j t", j=NJ)
        tg = [sb("tg%d" % i, [128, TQ], BF16) for i in range(2)]
        mg = sb("mg", [128, KC, TQ], BF16)
        oaT = sb("oaT", [128, NH, TQ], BF16)
        Eb = [sb("E%d" % i, [128, 2, TQ], BF16) for i in range(2)]
        rtb = [sb("rt%d" % i, [128, 1040], F32) for i in range(2)]
        rt = rtb[0]
        ocp = rt[:, 0:8 * 129].rearrange("p (r e) -> p r e", e=129)
        qkbb = [sb("qkb%d" % i, [128, TQ], BF16) for i in range(2)]
        osb = sb("osb", [128, 4, 128], F32)
        ttmp = sb("ttmp", [128, 128], F32)
        oab = sb("oab", [128, 4, 128], BF16)
        junk = sb("junk", [128, D], BF16)
        swt = sb("swt", [128, TQ], BF16)
        swa = sb("swa", [128, TQ], BF16)
        cvb = [sb("cv%d" % i, [128, TQ], F32) for i in range(2)]
        m1 = sb("m1", [128, TQ], BF16)
        stat = sb("stat", [128, 96], F32)
        um = sb("um", [128, KC, NM], BF16)
        uhs = sb("uhs", [128, KC, 2], BF16)
        ccm = sb("ccm", [128, KC, NM], F32)
        wring = sb("wring", [128, NSLOT, 4096], BF16)
        if NH * T // 2 >= 8192:
            stg = KT[:].bitcast(F32)
        else:
            stg = sb("stg", [128, 8192], F32)[:]

        pS = [ps("pS%d" % i, [128, 2, 512], F32) for i in range(2)]
        pO = ps("pO", [128, 3, 512], F32)
        pX = ps("pX", [128, 512], F32)
        banks = [pS[0][:, 0, :], pS[0][:, 1, :], pS[1][:, 0, :], pS[1][:, 1, :],
                 pO[:, 0, :], pO[:, 1, :], pO[:, 2, :], pX[:]]
        pXb = pX[:].bitcast(BF16)

        acc_state = {"i": 0}
        ACCS = [("b%d" % i, banks[i]) for i in range(7)]
        S.alias(["b0", "b1"], ["S0"])
        S.alias(["b2", "b3"], ["S1"])
        S.alias(["b4", "b5", "b6"], ["O"])
        S.alias([("actT", j) for j in range(NJ)],
                [("QT", q) for q in range(NH)] + [("cob", q) for q in range(KC)] + [("u", q) for q in range(KC)] + [("uh", q) for q in range(KC)])
        S.alias([("stg", 0), ("stg", 1)], [("KT", q, cc_) for q in range(NH) for cc_ in range(nch)])
        for sl_ in range(NSLOT):
            S.alias([("w", sl_)], [("wx", sl_, kc_) for kc_ in range(KC)])
        S.alias(["ocp", "ocp1", "ocp2"], ["rt1_0", "rt2_0"])
        S.psum_keys = set(["S0", "S1", "O", "X"] + ["b%d" % i for i in range(7)])

        def next_acc():
            i = acc_state["i"]
            acc_state["i"] = (i + 1) % 7
            return ACCS[i]

        wseq = []
        wst = {"issued": 0, "cons": 0}

        wslots = [wring[:, 0, :], wring[:, 1, :], wring[:, 2, :],
                  oaT[:].rearrange("p a b -> p (a b)"), mg[:].rearrange("p a b -> p (a b)")]
        S.alias([("w", 3)], [("oaT", q) for q in range(NH)])
        S.alias([("w", 4)], [("mg", q) for q in range(KC)])
        wplan = {}

        def wplan_build():
            last = [-1] * 5
            slot_of, prev_occ, need = [], [], []
            nmeta = len(meta_units)
            for i, (uu, wide) in enumerate(wseq):
                allowed = range(5) if wide else range(3)
                s_ = min(allowed, key=lambda q: last[q])
                slot_of.append(s_)
                prev_occ.append(last[s_])
                last[s_] = i
                nd = 0
                if i >= nmeta and s_ >= 3:
                    base = nmeta + ((i - nmeta) // NU) * NU
                    pos = (i - nmeta) % NU
                    if pos >= 22:
                        nd = base + (20 if s_ == 3 else 22)
                need.append(nd)
            wplan["slot"], wplan["prev"], wplan["need"] = slot_of, prev_occ, need

        def wget(u, held=0):
            n = wst["cons"]
            assert wseq[n][0] == u, (n, wseq[n], u)
            while (wst["issued"] < min(len(wseq), n + 6) and wplan["prev"][wst["issued"]] < n - held
                   and wplan["need"][wst["issued"]] <= n - held):
                i = wst["issued"]
                slot = wplan["slot"][i]
                uu = wseq[i][0]
                S.add(SP, lambda e, slot=slot, uu=uu: e.dma_start(out=wslots[slot], in_=scr_d[uu]),
                      reads=[("scr", uu)], writes=[("w", slot)], dma="w%d" % slot)
                wst["issued"] += 1
            assert wst["issued"] > n
            wst["cons"] += 1
            slot = wplan["slot"][n]
            return wslots[slot], ("w", slot)

        meta_units = [U_CC, U_CC + 1, U_CX, U_CX + 1, U_K, U_K + 1, U_V, U_V + 1]
        chunk_units = ([U_CC, U_CC + 1, U_CX, U_CX + 1, U_CB, U_V, U_CB + 1, U_V + 1, U_Q, U_Q + 1, U_K, U_K + 1,
                        U_GB, U_WPC, U_GB + 1, U_WPC + 1, U_GA, U_WPA, U_GA + 1, U_WPA + 1, U_WO, U_WO + 1]
                       + [U_WGU + i for i in range(11)] + [U_WD + i for i in range(6)])
        assert sorted(chunk_units) == list(range(NU))
        wseq.extend((u_, False) for u_ in meta_units)
        for _ in range(nseq * nch):
            wseq.extend((u_, not (12 <= p_ < 22)) for p_, u_ in enumerate(chunk_units))
        wplan_build()
        cast_order = meta_units + [u for u in chunk_units if u not in meta_units]

        def cload(dst, src, key):
            S.add(SP, lambda e: e.dma_start(out=dst, in_=src), writes=[key], dma="c_" + key)

        cload(ident[:], ident_d, "ident")
        cload(perm[:], perm_d, "perm")
        cload(mask[:], mask_d, "mask")
        cload(nmw[:], nmw_d, "nmw")
        cload(nfw[:], nfw_d, "nfw")
        cload(convw[:], convw_d, "convw")
        cload(sws[:], subw_d, "sws0")
        cload(xin[:], nfin_d, "xin")
        S.add(DVE, lambda e: e.tensor_copy(out=nfin[:], in_=xin[:]), reads=["xin"], writes=["nfin"])
        cload(lamv[:], lam_d, "lamv")
        S.add(POOL, lambda e: e.memset(nhalf[:], -0.5), writes=["nhalf"])
        S.add(POOL, lambda e: e.memset(Vc[:, :, :, 128:129], 1.0), writes=["Vones"])
        S.add(POOL, lambda e: e.memset(Vm[:, :, 128:129], 1.0), writes=["Vmones"])
        S.add(DVE, lambda e: e.tensor_scalar(out=sws[:], in0=sws[:], scalar1=float(1.0 - LAMBDA_INIT), scalar2=None,
                                             op0=ALU.mult), reads=["sws0"], writes=["sws"])
        lv = lamv[:].rearrange("p (a b) -> p a b", a=4)
        lt = lamt[:].rearrange("p (a b) -> p a b", a=2)
        S.add(DVE, lambda e: e.tensor_tensor(out=lt[:, 0, :], in0=lv[:, 0, :], in1=lv[:, 1, :], op=ALU.mult),
              reads=["lamv"], writes=["lt0"])
        S.add(DVE, lambda e: e.tensor_tensor(out=lt[:, 1, :], in0=lv[:, 2, :], in1=lv[:, 3, :], op=ALU.mult),
              reads=["lamv"], writes=["lt1"])
        S.add(DVE, lambda e: e.tensor_reduce(out=lams[:, 0:2], in_=lt, op=ALU.add, axis=mybir.AxisListType.X),
              reads=["lt0", "lt1"], writes=["lams01"])
        S.add(ACT, lambda e: e.activation(out=lams[:, 2:4], in_=lams[:, 0:2], func=AF.Exp),
              reads=["lams01"], writes=["lams23"])
        S.add(DVE, lambda e: e.scalar_tensor_tensor(out=nlam[:], in0=lams[:, 3:4], scalar=float(-LAMBDA_INIT),
                                                    in1=lams[:, 2:3], op0=ALU.add, op1=ALU.subtract),
              reads=["lams23"], writes=["nlam"])

        def stage_load(ci_):
            u_ = cast_order[ci_]
            sl_ = ci_ % 2
            sv_ = stg[:, sl_ * 4096:(sl_ + 1) * 4096]
            S.add(SP, lambda e: e.dma_start(out=sv_, in_=wun_d[u_]), writes=[("stg", sl_)], dma="stg%d" % sl_)

        stage_load(0)
        stage_load(1)
        for ci_, u in enumerate(cast_order):
            sl = ci_ % 2
            sv = stg[:, sl * 4096:(sl + 1) * 4096]
            slot = ci_ % NSLOT
            dst = wring[:, slot, :]
            if u < U_WPC or (U_WGU <= u < U_WD):
                sc = nmw if u < U_WPC else nfw
                sck = "nmw" if u < U_WPC else "nfw"
                for kc in range(KC):
                    o_ = dst[:, kc * 512:(kc + 1) * 512]
                    i_ = sv[:, kc * 512:(kc + 1) * 512]
                    if (ci_ + kc) % 2 == 0:
                        S.add(ACT, lambda e, o_=o_, i_=i_, kc=kc, sc=sc: e.activation(out=o_, in_=i_, func=AF.Copy, scale=sc[:, kc:kc + 1]),
                              reads=[("stg", sl), sck], writes=[("wx", slot, kc)])
                    else:
                        S.add(DVE, lambda e, o_=o_, i_=i_, kc=kc, sc=sc: e.tensor_scalar(out=o_, in0=i_, scalar1=sc[:, kc:kc + 1], scalar2=None, op0=ALU.mult),
                              reads=[("stg", sl), sck], writes=[("wx", slot, kc)])
                rk = [("wx", slot, kc) for kc in range(KC)]
            else:
                for hf in range(2):
                    o_ = dst[:, hf * 2048:(hf + 1) * 2048]
                    i_ = sv[:, hf * 2048:(hf + 1) * 2048]
                    if hf == 0:
                        S.add(ACT, lambda e, o_=o_, i_=i_: e.activation(out=o_, in_=i_, func=AF.Copy),
                              reads=[("stg", sl)], writes=[("wx", slot, 0)])
                    else:
                        S.add(DVE, lambda e, o_=o_, i_=i_: e.tensor_copy(out=o_, in_=i_),
                              reads=[("stg", sl)], writes=[("wx", slot, 1)])
                rk = [("wx", slot, 0), ("wx", slot, 1)]
            if ci_ + 2 < len(cast_order):
                stage_load(ci_ + 2)
            S.add(SP, lambda e, dst=dst, u=u: e.dma_start(out=scr_d[u], in_=dst), reads=rk, writes=[("scr", u)], dma="scrw%d" % slot)

        def fm_tile(wv, wk, ci, rhsT, rkeys, n, acc=None):
            ak, ab = acc if acc is not None else next_acc()

            def f(e):
                ins = None
                for kc in range(KC):
                    ins = e.matmul(ab[:, 0:n], lhsT=wv[:, kc * 512 + ci * 128: kc * 512 + (ci + 1) * 128],
                                   rhs=rhsT[:, kc, 0:n], start=(kc == 0), stop=(kc == KC - 1))
                return ins
            S.add(PE, f, reads=[wk] + list(rkeys), writes=[ak])
            return ak, ab

        rp_state = {"i": 0}

        def rope_tile(ak, ab, n, cs, sn, cskeys, dst, dkey, defer=False):
            i = rp_state["i"]
            rp_state["i"] = i + 1
            b = i % 2
            qkb = qkbb[b]
            rt1 = rtb[b][:, 0:TQ]
            rt2 = rtb[b][:, 520:520 + TQ]
            k1, k2, kq = "rt1_%d" % b, "rt2_%d" % b, "qkb%d" % b
            S.add(ACT, lambda e: e.activation(out=qkb[:, 0:n], in_=ab[:, 0:n], func=AF.Copy), reads=[ak], writes=[kq])

            def back():
                S.add(PE, lambda e: e.matmul(pX[:, 0:n], lhsT=perm[:], rhs=qkb[:, 0:n], start=True, stop=True),
                      reads=[kq, "perm"], writes=["X"])
                S.add(DVE, lambda e: e.tensor_tensor(out=rt1[:, 0:n], in0=ab[:, 0:n], in1=cs, op=ALU.mult),
                      reads=[ak] + cskeys, writes=[k1])
                S.add(DVE, lambda e: e.tensor_tensor(out=rt2[:, 0:n], in0=pX[:, 0:n], in1=sn, op=ALU.mult),
                      reads=["X"] + cskeys, writes=[k2])
                S.add(POOL, lambda e: e.tensor_tensor(out=dst, in0=rt1[:, 0:n], in1=rt2[:, 0:n], op=ALU.add),
                      reads=[k1, k2], writes=[dkey])
            if defer:
                return back
            back()
            return None

        nt_state = {"i": 0}

        def norm_tile(src, skey, npart, dst_hnT, dkeys, col0, defer=False):
            i = nt_state["i"]
            nt_state["i"] = i + 1
            xs = xsb[i % 4]
            xk = "xs%d" % (i % 4)
            c0 = 64 + 3 * (i % 8)
            sk = "nst%d" % (i % 8)
            S.add(ACT, lambda e: e.activation(out=junk[0:npart, :], in_=src, func=AF.Square, accum_out=stat[0:npart, c0:c0 + 1]),
                  reads=[skey], writes=["junk", sk + "a"])
            S.add(DVE, lambda e: e.tensor_scalar(out=stat[0:npart, c0 + 1:c0 + 2], in0=stat[0:npart, c0:c0 + 1], scalar1=1.0 / D, scalar2=EPS,
                                                 op0=ALU.mult, op1=ALU.add), reads=[sk + "a"], writes=[sk + "b"])
            S.add(POOL, lambda e: e.tensor_tensor(out=stat[0:npart, c0 + 2:c0 + 3], in0=stat[0:npart, c0 + 1:c0 + 2], in1=nhalf[0:npart, :], op=ALU.pow),
                  reads=[sk + "b", "nhalf"], writes=[sk + "c"])
            S.add(ACT, lambda e: e.activation(out=xs[0:npart, :], in_=src, func=AF.Copy, scale=stat[0:npart, c0 + 2:c0 + 3]),
                  reads=[skey, sk + "c"], writes=[xk])

            def back():
                def f(e):
                    ins = None
                    for kc in range(KC):
                        ins = e.transpose(pXb[:, kc * 128: kc * 128 + npart], xs[0:npart, kc * 128:(kc + 1) * 128], ident[0:npart, 0:npart])
                    return ins
                S.add(PE, f, reads=[xk, "ident"], writes=["X"])
                S.add(DVE, lambda e: e.tensor_copy(out=dst_hnT[:, :, col0:col0 + npart],
                                                   in_=pXb.rearrange("p (k t) -> p k t", k=KC)[:, :, 0:npart]),
                      reads=["X"], writes=dkeys)
            if defer:
                return back
            back()
            return None

        S.add(SP, lambda e: e.dma_start(out=xin[0:NM, :], in_=meta_d), writes=["xin"], dma="xin")
        S.add(SP, lambda e: e.dma_start(out=cosb[:, 0:NM], in_=cos_d[:, 0:NM]), writes=["cos"], dma="cos")
        S.add(SP, lambda e: e.dma_start(out=sinb[:, 0:NM], in_=sin_d[:, 0:NM]), writes=["sin"], dma="sin")
        hk_all = [("hnT", tt) for tt in range(4)]
        norm_tile(xin[0:NM, :], "xin", NM, hnT, hk_all, 0)
        for t8 in range(8):
            if t8 % 4 == 0:
                wv, wk = wget(U_CC + t8 // 4)
            ak, ab = fm_tile(wv, wk, t8 % 4, hnT, hk_all, NM)
            S.add(ACT, lambda e, ab=ab, t8=t8: e.activation(out=ccm[:, t8, :], in_=ab[:, 0:NM], func=AF.Copy),
                  reads=[ak], writes=[("ccm", t8)])
        for t8 in range(8):
            if t8 % 4 == 0:
                wv, wk = wget(U_CX + t8 // 4)
            ak, ab = fm_tile(wv, wk, t8 % 4, hnT, hk_all, NM)
            S.add(DVE, lambda e, ab=ab, t8=t8: e.tensor_tensor(out=um[:, t8, :], in0=ab[:, 0:NM], in1=ccm[:, t8, :], op=ALU.mult),
                  reads=[ak, ("ccm", t8)], writes=[("um", t8)])
        for t8 in range(8):
            if t8 % 4 == 0:
                wv, wk = wget(U_K + t8 // 4)
            ak, ab = fm_tile(wv, wk, t8 % 4, hnT, hk_all, NM)
            rope_tile(ak, ab, NM, cosb[:, 0:NM], sinb[:, 0:NM], ["cos", "sin"], KTm[:, t8, :], ("KTm", t8))
        for hf in range(2):
            wv, wk = wget(U_V + hf)
            ak, ab = next_acc()

            def f(e, wv=wv, ab=ab):
                ins = None
                for kc in range(KC):
                    ins = e.matmul(ab[0:NM, :], lhsT=hnT[:, kc, 0:NM], rhs=wv[:, kc * 512:(kc + 1) * 512],
                                   start=(kc == 0), stop=(kc == KC - 1))
                return ins
            S.add(PE, f, reads=[wk] + hk_all, writes=[ak])
            S.add(ACT, lambda e, ab=ab, hf=hf: e.activation(out=Vm[:, hf * 4:(hf + 1) * 4, 0:128],
                                                           in_=ab[0:NM, :].rearrange("p (a b) -> p a b", a=4), func=AF.Copy),
                  reads=[ak, "Vmones"], writes=[("Vm", hf)])

        mview = mask[:]

        def dbg(stage):
            if DBG != stage:
                return
            items = [("QT", arena[:, 0:4096], [128, 4096], BF16), ("cob", arena[:, 4096:8192], [128, 4096], BF16),
                     ("actT", arena[:, 0:NJ * TQ], [128, NJ * TQ], BF16),
                     ("KT", KT[:, 0:4096], [128, 4096], BF16), ("Vc", Vc[:, 0:4, :, :].rearrange("p a b c -> p (a b c)"), [128, 4 * NH * 129], BF16),
                     ("mg", mg[:].rearrange("p a b -> p (a b)"), [128, 4096], BF16),
                     ("oaT", oaT[:].rearrange("p a b -> p (a b)"), [128, 4096], BF16), ("hnT", hnT[:].rearrange("p a b -> p (a b)"), [128, 4096], BF16),
                     ("h", h[:].rearrange("p a b -> p (a b)"), [128, 4 * D], F32), ("stat", stat[:], [128, 96], F32)]
            allk = list(S.last_writer.keys())
            for name, ap_, shp, dt_ in items:
                dd = nc.dram_tensor("dbg_" + name, shp, dt_, kind="ExternalOutput").ap()
                S.add(SP, lambda e, dd=dd, ap_=ap_: e.dma_start(out=dd, in_=ap_), reads=allk, dma="dbg_" + name)
            S.emit(final_dma_keys=["dbg_" + it[0] for it in items])
            raise _Stop()

        def p1_prefetch(s, c):
            t0 = c * TQ

            def mk(tt):
                def front():
                    if tt == 0:
                        S.add(SP, lambda e: e.dma_start(out=cosb[:], in_=cos_d[:, NM + t0: NM + t0 + TQ]), writes=["cos"], dma="cos")
                        S.add(SP, lambda e: e.dma_start(out=sinb[:], in_=sin_d[:, NM + t0: NM + t0 + TQ]), writes=["sin"], dma="sin")
                    S.add(POOL, lambda e: e.dma_start(out=xin[:], in_=x_d[s, t0 + tt * 128: t0 + (tt + 1) * 128, :]),
                          writes=["xin"], dma="xin")
                    return norm_tile(xin[:], "xin", 128, hnT, [("hnT", tt)], tt * 128, defer=True)
                return front
            return [mk(tt) for tt in range(4)]

        def do_chunk(s, c, nxt):
            t0 = c * TQ
            hkeys = [("hnT", tt) for tt in range(4)]
            for tt in range(4):
                S.add(POOL, lambda e, tt=tt: e.dma_start(out=h[:, tt, :], in_=x_d[s, t0 + tt * 128: t0 + (tt + 1) * 128, :]),
                      writes=[("h", tt)], dma="x%d" % tt)

            dbg("p1")
            for t8 in range(8):
                if t8 % 4 == 0:
                    wv, wk = wget(U_CC + t8 // 4)
                ak, ab = fm_tile(wv, wk, t8 % 4, hnT, hkeys, TQ)
                S.add(ACT, lambda e, ab=ab, t8=t8: e.activation(out=cob[:, t8, :], in_=ab, func=AF.Copy),
                      reads=[ak], writes=[("cob", t8)])
            for half in range(2):
                wv, wk = wget(U_CX + half)
                for t4 in range(4):
                    t8 = half * 4 + t4
                    ak, ab = fm_tile(wv, wk, t4, hnT, hkeys, TQ)
                    if c == 0:
                        S.add(POOL, lambda e, t8=t8: e.tensor_copy(out=ubuf[:, t8, 0:2], in_=um[:, t8, NM - 2:NM]),
                              reads=[("um", t8)], writes=[("uh", t8)])
                    else:
                        S.add(POOL, lambda e, t8=t8: e.tensor_copy(out=ubuf[:, t8, 0:2], in_=uhs[:, t8, :]),
                              reads=[("uhs", t8)], writes=[("uh", t8)])
                    S.add(DVE, lambda e, ab=ab, t8=t8: e.tensor_tensor(out=ubuf[:, t8, 2:2 + TQ], in0=ab, in1=cob[:, t8, :], op=ALU.mult),
                          reads=[ak, ("cob", t8), ("uh", t8)], writes=[("u", t8)])
                    S.add(POOL, lambda e, t8=t8: e.tensor_copy(out=uhs[:, t8, :], in_=ubuf[:, t8, TQ:TQ + 2]),
                          reads=[("u", t8)], writes=[("uhs", t8)])

            for half in range(2):
                wv, wk = wget(U_CB + half)
                for t4 in range(4):
                    t8 = half * 4 + t4
                    ak, ab = fm_tile(wv, wk, t4, hnT, hkeys, TQ)
                    cv = cvb[t8 % 2]
                    cvk = "cv%d" % (t8 % 2)
                    S.add(POOL, lambda e, t8=t8, cv=cv: e.tensor_scalar(out=cv[:], in0=ubuf[:, t8, 0:TQ], scalar1=convw[:, t8 * 3:t8 * 3 + 1],
                                                                        scalar2=0.0, op0=ALU.mult, op1=ALU.add),
                          reads=[("u", t8), ("uh", t8), "convw"], writes=[cvk])
                    S.add(DVE, lambda e, t8=t8, cv=cv: e.scalar_tensor_tensor(out=cv[:], in0=ubuf[:, t8, 1:1 + TQ], scalar=convw[:, t8 * 3 + 1:t8 * 3 + 2],
                                                                               in1=cv[:], op0=ALU.mult, op1=ALU.add),
                          reads=[("u", t8), ("uh", t8), "convw", cvk], writes=[cvk])
                    S.add(DVE, lambda e, t8=t8, cv=cv: e.scalar_tensor_tensor(out=cv[:], in0=ubuf[:, t8, 2:2 + TQ], scalar=convw[:, t8 * 3 + 2:t8 * 3 + 3],
                                                                               in1=cv[:], op0=ALU.mult, op1=ALU.add),
                          reads=[("u", t8), "convw", cvk], writes=[cvk])
                    S.add(DVE, lambda e, ab=ab, t8=t8, cv=cv: e.tensor_tensor(out=cob[:, t8, :], in0=ab, in1=cv[:], op=ALU.mult),
                          reads=[ak, cvk], writes=[("cob", t8)])
                hf = half
                wv, wk = wget(U_V + hf)
                for tt in range(4):
                    ak, ab = next_acc()

                    def f(e, wv=wv, ab=ab, tt=tt):
                        ins = None
                        for kc in range(KC):
                            ins = e.matmul(ab, lhsT=hnT[:, kc, tt * 128:(tt + 1) * 128], rhs=wv[:, kc * 512:(kc + 1) * 512],
                                           start=(kc == 0), stop=(kc == KC - 1))
                        return ins
                    S.add(PE, f, reads=[wk, ("hnT", tt)], writes=[ak])
                    kb = c * 4 + tt
                    S.add(ACT, lambda e, ab=ab, hf=hf, kb=kb: e.activation(out=Vc[:, kb, hf * 4:(hf + 1) * 4, 0:128],
                                                                         in_=ab.rearrange("p (a b) -> p a b", a=4), func=AF.Copy),
                          reads=[ak, "Vones"], writes=[("V", kb, hf)])

            rpend = None
            for t8 in range(8):
                if t8 % 4 == 0:
                    wv, wk = wget(U_Q + t8 // 4)
                ak, ab = fm_tile(wv, wk, t8 % 4, hnT, hkeys, TQ)
                nb = rope_tile(ak, ab, TQ, cosb[:], sinb[:], ["cos", "sin"], QT[:, t8, :], ("QT", t8), defer=True)
                if rpend is not None:
                    rpend()
                rpend = nb
            for t8 in range(8):
                if t8 % 4 == 0:
                    wv, wk = wget(U_K + t8 // 4)
                ak, ab = fm_tile(wv, wk, t8 % 4, hnT, hkeys, TQ)
                nb = rope_tile(ak, ab, TQ, cosb[:], sinb[:], ["cos", "sin"], KTv[:, t8, t0:t0 + TQ], ("KT", t8, c), defer=True)
                rpend()
                rpend = nb
            rpend()

            dbg("p2")
            def do_head(hd, pending):
                kbl = [("m", 0, 0)] + [("f", kb, 0) for kb in range(4 * c)] + [("d", 4 * c + i, 128 * i) for i in range(4)]
                nk = len(kbl)
                started = set()

                def qk_step(i):
                    kind, kb, q0 = kbl[i]
                    pSi = pS[i % 2]
                    E = Eb[i % 2]
                    nkeys = NM if kind == "m" else 128

                    def f(e):
                        ins = None
                        for sub in range(2):
                            r0 = sub * 64
                            if kind == "m":
                                lt_ = KTm[r0:r0 + 64, hd, :]
                            else:
                                lt_ = KTv[r0:r0 + 64, hd, kb * 128:(kb + 1) * 128]
                            ins = e.matmul(pSi[0:nkeys, sub, q0:TQ], lhsT=lt_, rhs=QT[r0:r0 + 64, hd, q0:TQ], start=True, stop=(kind != "d"),
                                           skip_group_check=True)
                        if kind == "d":
                            for sub in range(2):
                                ins = e.matmul(pSi[:, sub, q0:q0 + 128], lhsT=ident[:], rhs=mask[:], start=False, stop=True, skip_group_check=True)
                        return ins
                    rk = [("QT", hd), "ident", "mask"] + ([("KTm", hd)] if kind == "m" else [("KT", hd, kb // 4)])
                    S.add(PE, f, reads=rk, writes=["S%d" % (i % 2)])
                    S.add(ACT, lambda e: e.activation(out=E[0:nkeys, :, q0:TQ], in_=pSi[0:nkeys, :, q0:TQ], func=AF.Exp, scale=0.125),
                          reads=["S%d" % (i % 2)], writes=["E%d" % (i % 2)])

                def pv_step(i):
                    kind, kb, q0 = kbl[i]
                    E = Eb[i % 2]
                    nkeys = NM if kind == "m" else 128

                    def f(e):
                        ins = None
                        for qi in range(q0 // 128, 4):
                            for sub in range(2):
                                r = qi * 2 + sub
                                bank, off = r // 3, (r % 3) * 129
                                st_ = (bank not in started)
                                started.add(bank)
                                rhs = Vm[:, hd, :] if kind == "m" else Vc[:, kb, hd, :]
                                sp_ = (kind == "d" and kb % 4 == qi)
                                ins = e.matmul(pO[:, bank, off:off + 129], lhsT=E[0:nkeys, sub, qi * 128:(qi + 1) * 128], rhs=rhs,
                                               start=st_, stop=sp_, skip_group_check=True)
                        return ins
                    rk = ["E%d" % (i % 2)] + (["Vmones", ("Vm", hd // 4)] if kind == "m" else ["Vones", ("V", kb, hd // 4)])
                    S.add(PE, f, reads=rk, writes=["O"])

                for i in range(nk + 1):
                    if i < nk:
                        qk_step(i)
                    if i >= 1:
                        pv_step(i - 1)
                    if i == min(nk, 6) and pending is not None:
                        pending()
                        pending = None
                if pending is not None:
                    pending()

                S.add(DVE, lambda e: e.tensor_copy(out=rt[:, 0:387], in_=pO[:, 0, 0:387]), reads=["O"], writes=["ocp"])
                S.add(DVE, lambda e: e.tensor_copy(out=rt[:, 387:774], in_=pO[:, 1, 0:387]), reads=["O"], writes=["ocp1"])
                S.add(DVE, lambda e: e.tensor_copy(out=rt[:, 774:1032], in_=pO[:, 2, 0:258]), reads=["O"], writes=["ocp2"])
                ok3 = ["ocp", "ocp1", "ocp2"]
                S.add(DVE, lambda e: e.reciprocal(out=stat[:, 8:16], in_=ocp[:, :, 128]), reads=ok3, writes=["rz"])
                S.add(DVE, lambda e: e.tensor_scalar(out=stat[:, 16:24], in0=stat[:, 8:16], scalar1=nlam[:, 0:1], scalar2=None, op0=ALU.mult),
                      reads=["rz", "nlam"], writes=["rzs"])
                for qi in range(4):
                    r0_, r1_ = 2 * qi, 2 * qi + 1
                    S.add(DVE, lambda e, r1_=r1_: e.tensor_scalar(out=ttmp[:], in0=ocp[:, r1_, 0:128], scalar1=stat[:, 16 + r1_:17 + r1_], scalar2=None, op0=ALU.mult),
                          reads=ok3 + ["rzs"], writes=["ttmp"])
                    S.add(DVE, lambda e, r0_=r0_, qi=qi: e.scalar_tensor_tensor(out=osb[:, qi, :], in0=ocp[:, r0_, 0:128], scalar=stat[:, 8 + r0_:9 + r0_], in1=ttmp[:],
                                                                                  op0=ALU.mult, op1=ALU.add),
                          reads=ok3 + ["rz", "ttmp"], writes=[("osb", qi)])
                    S.add(DVE, lambda e, qi=qi: e.scalar_tensor_tensor(out=junk[:, 0:128], in0=osb[:, qi, :], scalar=1.0, in1=osb[:, qi, :],
                                                                      op0=ALU.mult, op1=ALU.mult, accum_out=stat[:, 24 + qi:25 + qi]),
                          reads=[("osb", qi)], writes=["junkd", ("ss", qi)])
                S.add(DVE, lambda e: e.tensor_scalar(out=stat[:, 28:32], in0=stat[:, 24:28], scalar1=1.0 / 128, scalar2=EPS, op0=ALU.mult, op1=ALU.add),
                      reads=[("ss", q) for q in range(4)], writes=["ssv"])
                S.add(POOL, lambda e: e.tensor_tensor(out=stat[:, 32:36], in0=stat[:, 28:32], in1=nhalf[:, 0:1].to_broadcast([128, 4]), op=ALU.pow),
                      reads=["ssv", "nhalf"], writes=["srs"])
                S.add(DVE, lambda e: e.tensor_tensor(out=oab[:], in0=osb[:], in1=stat[:, 32:36].unsqueeze(2).to_broadcast([128, 4, 128]), op=ALU.mult),
                      reads=[("osb", q) for q in range(4)] + ["srs"], writes=["oab"])

                def finish():
                    def ftr(e):
                        ins = None
                        for qi in range(4):
                            ins = e.transpose(pXb[:, qi * 128:(qi + 1) * 128], oab[:, qi, :], ident[:])
                        return ins
                    S.add(PE, ftr, reads=["oab", "ident"], writes=["X"])
                    S.add(DVE, lambda e: e.tensor_scalar(out=oaT[:, hd, :], in0=pXb[:, 0:TQ], scalar1=sws[:, 0:1], scalar2=None, op0=ALU.mult),
                          reads=["X", "sws"], writes=[("oaT", hd)])
                return finish

            pend = None
            for hd_ in range(NH):
                pend = do_head(hd_, pend)

            dbg("p3")
            okeys = [("oaT", q) for q in range(NH)]
            ckeys = [("cob", q) for q in range(KC)]
            for half in range(2):
                gv, gk = wget(U_GB + half)
                wv, wk = wget(U_WPC + half, held=1)
                for t4 in range(4):
                    t8 = half * 4 + t4
                    ak, ab = fm_tile(gv, gk, t4, hnT, hkeys, TQ)
                    tgi = tg[t8 % 2]
                    tgk = "tg%d" % (t8 % 2)
                    S.add(ACT, lambda e, ab=ab, tgi=tgi: e.activation(out=tgi[:], in_=ab, func=AF.Tanh, scale=0.5), reads=[ak], writes=[tgk])
                    ak2, ab2 = fm_tile(wv, wk, t4, cob, ckeys, TQ)
                    S.add(DVE, lambda e, ab2=ab2, tgi=tgi, t8=t8: e.scalar_tensor_tensor(out=mg[:, t8, :], in0=tgi[:], scalar=1.0, in1=ab2, op0=ALU.add, op1=ALU.mult),
                          reads=[ak2, tgk], writes=[("mg", t8)])
                    if half == 0 and t4 == 1 and pend is not None:
                        pend()
                        pend = None
            for half in range(2):
                gv, gk = wget(U_GA + half)
                wv, wk = wget(U_WPA + half, held=1)
                for t4 in range(4):
                    t8 = half * 4 + t4
                    ak, ab = fm_tile(gv, gk, t4, hnT, hkeys, TQ)
                    tgi = tg[t8 % 2]
                    tgk = "tg%d" % (t8 % 2)
                    S.add(ACT, lambda e, ab=ab, tgi=tgi: e.activation(out=tgi[:], in_=ab, func=AF.Tanh, scale=0.5), reads=[ak], writes=[tgk])
                    ak2, ab2 = fm_tile(wv, wk, t4, oaT, okeys, TQ)
                    S.add(DVE, lambda e, ab2=ab2, tgi=tgi: e.scalar_tensor_tensor(out=m1[:], in0=tgi[:], scalar=1.0, in1=ab2, op0=ALU.add, op1=ALU.mult),
                          reads=[ak2, tgk], writes=["m1"])
                    S.add(DVE, lambda e, t8=t8: e.tensor_tensor(out=mg[:, t8, :], in0=mg[:, t8, :], in1=m1[:], op=ALU.add),
                          reads=["m1", ("mg", t8)], writes=[("mg", t8)])
            mkeys = [("mg", q) for q in range(KC)]

            dbg("p4")
            wv0, wk0 = wget(U_WO)
            wv1, wk1 = wget(U_WO + 1, held=1)
            prev_back = None
            for tt in range(4):
                for hf in range(2):
                    wv, wk = (wv0, wk0) if hf == 0 else (wv1, wk1)
                    ak, ab = next_acc()

                    def f(e, wv=wv, ab=ab, tt=tt):
                        ins = None
                        for kc in range(KC):
                            ins = e.matmul(ab, lhsT=mg[:, kc, tt * 128:(tt + 1) * 128], rhs=wv[:, kc * 512:(kc + 1) * 512],
                                           start=(kc == 0), stop=(kc == KC - 1))
                        return ins
                    S.add(PE, f, reads=[wk] + mkeys, writes=[ak])
                    hv = h[:, tt, hf * 512:(hf + 1) * 512]
                    S.add(DVE, lambda e, ab=ab, hv=hv: e.scalar_tensor_tensor(out=hv, in0=ab, scalar=0.5, in1=hv, op0=ALU.mult, op1=ALU.add),
                          reads=[ak, ("h", tt)], writes=[("h", tt)] if hf == 1 else [("hx", tt)])
                if prev_back is not None:
                    prev_back()
                prev_back = norm_tile(h[:, tt, :], ("h", tt), 128, hnT, [("hnT", tt)], tt * 128, defer=True)
            prev_back()

            dbg("p6")
            pfronts = p1_prefetch(*nxt) if nxt is not None else []
            pbacks = []
            for ug in range(11):
                wv, wk = wget(U_WGU + ug)
                if pfronts and ug in (3, 5, 7, 9):
                    pbacks.append(pfronts.pop(0)())
                for t2 in range(2):
                    j = 2 * ug + t2
                    gk, gb_ = fm_tile(wv, wk, 2 * t2, hnT, hkeys, TQ)
                    uk, ub_ = fm_tile(wv, wk, 2 * t2 + 1, hnT, hkeys, TQ)
                    S.add(ACT, lambda e, gb_=gb_: e.activation(out=swt[:], in_=gb_, func=AF.Tanh, scale=0.5), reads=[gk], writes=["swt"])
                    S.add(DVE, lambda e, gb_=gb_: e.scalar_tensor_tensor(out=swa[:], in0=swt[:], scalar=1.0, in1=gb_, op0=ALU.add, op1=ALU.mult),
                          reads=[gk, "swt"], writes=["swa"])
                    S.add(DVE, lambda e, ub_=ub_, j=j: e.tensor_tensor(out=actT[:, j, :], in0=ub_, in1=swa[:], op=ALU.mult),
                          reads=[uk, "swa"], writes=[("actT", j)])

            dbg("p7")
            for hf in range(2):
                accs = [next_acc() for _ in range(4)]
                for g3 in range(3):
                    wv, wk = wget(U_WD + hf * 3 + g3)
                    nj = 8 if g3 < 2 else 6
                    for jj in range(nj):
                        j = g3 * 8 + jj
                        if pbacks and ((hf == 0 and j in (8, 16)) or (hf == 1 and j in (3, 12))):
                            pbacks.pop(0)()
                        for tt in range(4):
                            ak, ab = accs[tt]
                            S.add(PE, lambda e, wv=wv, jj=jj, j=j, tt=tt, ab=ab: e.matmul(ab, lhsT=actT[:, j, tt * 128:(tt + 1) * 128], rhs=wv[:, jj * 512:(jj + 1) * 512],
                                                                                        start=(j == 0), stop=(j == NJ - 1)),
                                  reads=[wk, ("actT", j)], writes=[ak])
                for tt in range(4):
                    ak, ab = accs[tt]
                    hv = h[:, tt, hf * 512:(hf + 1) * 512]
                    S.add(DVE, lambda e, ab=ab, hv=hv: e.scalar_tensor_tensor(out=hv, in0=ab, scalar=0.5, in1=hv, op0=ALU.mult, op1=ALU.add),
                          reads=[ak, ("h", tt)], writes=[("h", tt)] if hf == 1 else [("hx", tt)])

            for tt in range(4):
                S.add(ACT, lambda e, tt=tt: e.activation(out=junk[:], in_=h[:, tt, :], func=AF.Square, accum_out=stat[:, 40 + tt:41 + tt]),
                      reads=[("h", tt)], writes=["junk", ("fs", tt)])
                S.add(DVE, lambda e, tt=tt: e.tensor_scalar(out=stat[:, 44 + tt:45 + tt], in0=stat[:, 40 + tt:41 + tt], scalar1=1.0 / D, scalar2=EPS,
                                                            op0=ALU.mult, op1=ALU.add), reads=[("fs", tt)], writes=[("fv", tt)])
                S.add(POOL, lambda e, tt=tt: e.tensor_tensor(out=stat[:, 48 + tt:49 + tt], in0=stat[:, 44 + tt:45 + tt], in1=nhalf[:], op=ALU.pow),
                      reads=[("fv", tt), "nhalf"], writes=[("fr", tt)])
                S.add(DVE, lambda e, tt=tt: e.scalar_tensor_tensor(out=h[:, tt, :], in0=h[:, tt, :], scalar=stat[:, 48 + tt:49 + tt], in1=nfin[:],
                                                                   op0=ALU.mult, op1=ALU.mult),
                      reads=[("h", tt), ("fr", tt), "nfin"], writes=[("h", tt)])
                S.add(POOL, lambda e, tt=tt: e.dma_start(out=y_d[s, t0 + tt * 128: t0 + (tt + 1) * 128, :], in_=h[:, tt, :]),
                      reads=[("h", tt)], dma="y%d" % tt)

        order = [(s_, c_) for s_ in range(nseq) for c_ in range(nch)]
        for f_ in p1_prefetch(*order[0]):
            f_()()
        try:
            for i_, (s_, c_) in enumerate(order):
                do_chunk(s_, c_, order[i_ + 1] if i_ + 1 < len(order) else None)
            S.emit(final_dma_keys=["y0", "y1", "y2", "y3"])
        except _Stop:
            pass
    return nc


def _host_consts(T):
    pos = np.arange(NM + T, dtype=np.float32)
    inv_freq = (1.0 / (10000.0 ** (np.arange(0, 64, 2, dtype=np.float32) / np.float32(64)))).astype(np.float32)
    ang = pos[:, None] * inv_freq[None, :]
    ang = np.concatenate([ang, ang], axis=-1)
    cos = np.cos(ang).astype(np.float32).T
    sin = np.sin(ang).astype(np.float32).T
    sgn = np.where(np.arange(64) < 32, -1.0, 1.0).astype(np.float32)[:, None]
    cos128 = np.concatenate([cos, cos], 0)
    sin128 = np.concatenate([sin * sgn, sin * sgn], 0)
    bf = ml_dtypes.bfloat16
    ident = np.eye(128, dtype=np.float32).astype(bf)
    perm = np.zeros((128, 128), np.float32)
    for m in range(128):
        k = (m % 64 + 32) % 64 + 64 * (m // 64)
        perm[k, m] = 1.0
    mask = np.where(np.arange(128)[None, :] >= np.arange(128)[:, None], 0.0, -30000.0).astype(np.float32)
    return (np.ascontiguousarray(cos128), np.ascontiguousarray(sin128), ident, perm.astype(bf), mask.astype(bf))


def _a_unit(W, cols):
    return W[:, cols].reshape(KC, 128, 512).transpose(1, 0, 2).reshape(128, 4096)


def _host_units(w_in, wpc, wpa, wo, wgu, wd):
    units = np.zeros((NU, 128, 4096), np.float32)
    starts = {U_Q: 0, U_K: 1024, U_V: 2048, U_CB: 3072, U_CC: 4096, U_CX: 5120, U_GA: 6144, U_GB: 7168}
    for u0, c0 in starts.items():
        for i in range(2):
            units[u0 + i] = _a_unit(w_in, np.arange(c0 + 512 * i, c0 + 512 * (i + 1)))
    for i in range(2):
        units[U_WPC + i] = _a_unit(wpc, np.arange(512 * i, 512 * (i + 1)))
        units[U_WPA + i] = _a_unit(wpa, np.arange(512 * i, 512 * (i + 1)))
        units[U_WO + i] = _a_unit(wo, np.arange(512 * i, 512 * (i + 1)))
    for i in range(11):
        cols = np.concatenate([np.arange(128 * (2 * i), 128 * (2 * i + 1)), DFF + np.arange(128 * (2 * i), 128 * (2 * i + 1)),
                               np.arange(128 * (2 * i + 1), 128 * (2 * i + 2)), DFF + np.arange(128 * (2 * i + 1), 128 * (2 * i + 2))])
        units[U_WGU + i] = _a_unit(wgu, cols)
    for hf in range(2):
        for g3 in range(3):
            nj = 8 if g3 < 2 else 6
            blk = wd[1024 * g3: 1024 * g3 + 128 * nj, 512 * hf: 512 * (hf + 1)].reshape(nj, 128, 512).transpose(1, 0, 2).reshape(128, nj * 512)
            units[U_WD + hf * 3 + g3, :, :nj * 512] = blk
    return units


def _host_inputs(inputs, nseq, nch, ncores):
    T = nch * TQ
    f = lambda a: np.ascontiguousarray(np.asarray(a, dtype=np.float32))
    cos128, sin128, ident, perm, mask = _host_consts(T)
    units = _host_units(f(inputs["w_in"])[0], f(inputs["w_proj_conv"])[0], f(inputs["w_proj_attn"])[0], f(inputs["w_out"])[0],
                        f(inputs["w_gate_up"])[0], f(inputs["w_down"])[0])
    col8 = lambda v: np.ascontiguousarray(f(v).reshape(KC, 128).T)
    convw = np.ascontiguousarray(f(inputs["conv_w"])[0].reshape(3, KC, 128).transpose(2, 1, 0).reshape(128, KC * 3))
    lam = np.concatenate([f(inputs["lambda_q1"])[0], f(inputs["lambda_k1"])[0], f(inputs["lambda_q2"])[0], f(inputs["lambda_k2"])[0]])
    common = dict(
        meta=f(inputs["meta_tokens"]), wun=units, cos=cos128, sin=sin128, ident=ident, perm=perm, mask=mask,
        nmw=col8(inputs["norm_mix_w"][0]), nfw=col8(inputs["norm_ffn_w"][0]), convw=convw,
        subw=np.ascontiguousarray(f(inputs["subln_w"])[0].reshape(128, 1)),
        nfin=np.ascontiguousarray(np.broadcast_to(f(inputs["norm_final_w"])[None, :], (128, D))),
        lam=np.ascontiguousarray(np.broadcast_to(lam[None, :], (128, 256))),
    )
    x = f(inputs["x"])
    maps = []
    for ci in range(ncores):
        m = dict(common)
        m["x"] = np.ascontiguousarray(x[ci * nseq:(ci + 1) * nseq, :T])
        maps.append(m)
    return maps


def kernel(**inputs):
    nseq = BATCH // N_CORES
    nch = SEQ // TQ
    maps = _host_inputs(inputs, nseq, nch, N_CORES)
    nc = build(nseq, nch)
    res = run_bass_kernel_spmd(nc, maps, core_ids=list(range(N_CORES)))
    out = np.concatenate([np.asarray(r["y"]) for r in res.results], axis=0)
    return out.astype(np.float32)
```

```python
import math
import numpy as np
import ml_dtypes
from contextlib import ExitStack
from collections import defaultdict
import concourse.bass as bass
import concourse.mybir as mybir
from concourse.bass_utils import run_bass_kernel_spmd

F32 = mybir.dt.float32
BF16 = mybir.dt.bfloat16
AF = mybir.ActivationFunctionType
ALU = mybir.AluOpType

PE, ACT, DVE, POOL, SP = "tensor", "scalar", "vector", "gpsimd", "sync"
ENGS = (PE, ACT, DVE, POOL, SP)

D = 1024
KC = 8
NH = 8
NM = 16
TQ = 512
DFF = 2816
NJ = 22
EPS = 1e-5
LAMBDA_INIT = 0.8 - 0.6 * math.exp(-0.3 * 0)
N_CORES = 8
BATCH = 32
SEQ = 2048

U_GB, U_CC, U_CX, U_CB, U_GA, U_Q, U_K, U_V = 0, 2, 4, 6, 8, 10, 12, 14
U_WPC, U_WPA, U_WO, U_WGU, U_WD = 16, 18, 20, 22, 33
NU = 39
NSLOT = 3


class _Op:
    __slots__ = ("eng", "fn", "waits", "semkey", "value", "is_dma", "idx")


class Sched:
    def __init__(self, nc):
        self.nc = nc
        self.ops = []
        self.last_writer = {}
        self.readers = defaultdict(list)
        self.overlaps = defaultdict(list)
        self.count = defaultdict(int)
        self.seen = {e: {} for e in ENGS}
        self.dma_keys = []
        self.psum_keys = set()
        self.access = defaultdict(dict)

    def alias(self, a_keys, b_keys):
        for a in a_keys:
            for b in b_keys:
                self.overlaps[a].append(b)
                self.overlaps[b].append(a)

    def add(self, eng, fn, reads=(), writes=(), dma=None):
        op = _Op()
        op.eng, op.fn, op.is_dma = eng, fn, dma is not None
        op.idx = len(self.ops)
        deps = {}

        def dep(o, raw):
            if o is None or o is op:
                return
            same = (o.eng == eng) and not o.is_dma and not op.is_dma
            if same and (eng == PE or eng == SP or not raw):
                return
            deps[o.idx] = o

        for k in reads:
            dep(self.last_writer.get(k), True)
        for k in writes:
            for kk in [k] + self.overlaps.get(k, []):
                dep(self.last_writer.get(kk), False)
                for r in self.readers.get(kk, ()):
                    dep(r, False)
        for k in list(reads) + list(writes):
            if k in self.psum_keys:
                for kk in [k] + self.overlaps.get(k, []):
                    for e2, o in self.access[kk].items():
                        if e2 != eng:
                            deps[o.idx] = o
                self.access[k][eng] = op
        waits = {}
        for o in deps.values():
            if waits.get(o.semkey, 0) < o.value:
                waits[o.semkey] = o.value
        seen = self.seen[eng]
        op.waits = []
        for sk, v in waits.items():
            if seen.get(sk, 0) < v:
                seen[sk] = v
                op.waits.append((sk, v))
        if op.is_dma:
            op.semkey = "dma_" + dma
            if op.semkey not in self.count:
                self.dma_keys.append(op.semkey)
            self.count[op.semkey] += 16
        else:
            op.semkey = "eng_" + eng
            self.count[op.semkey] += 1
        op.value = self.count[op.semkey]
        for k in reads:
            self.readers[k].append(op)
        for k in writes:
            self.last_writer[k] = op
            self.readers[k] = []
            for kk in self.overlaps.get(k, []):
                self.last_writer[kk] = op
                self.readers[kk] = []
        self.ops.append(op)
        return op

    def emit(self, final_dma_keys=()):
        nc = self.nc
        with ExitStack() as st:
            sems = {}
            for e in ENGS:
                sems["eng_" + e] = st.enter_context(nc.semaphore("s_" + e))
            for k in self.dma_keys:
                sems[k] = st.enter_context(nc.semaphore("s_" + k))
            block = st.enter_context(nc.Block())
            per = {e: [o for o in self.ops if o.eng == e] for e in ENGS}
            finals = [("dma_" + k, self.count["dma_" + k]) for k in final_dma_keys if ("dma_" + k) in sems]

            def make(e):
                def body(eng):
                    for o in per[e]:
                        for sk, v in o.waits:
                            eng.wait_ge(sems[sk], v)
                        ins = o.fn(eng)
                        ins.then_inc(sems[o.semkey], 16 if o.is_dma else 1)
                    if e == SP:
                        for sk, v in finals:
                            eng.wait_ge(sems[sk], v)
                return body

            for e in ENGS:
                if per[e] or e == SP:
                    getattr(block, e)(make(e))
        return nc


DBG = None


class _Stop(Exception):
    pass


def build(nseq, nch):
    T = nch * TQ
    NKB = T // 128
    nc = bass.Bass("TRN2", target_bir_lowering=False)

    def din(name, shape, dt=F32):
        return nc.dram_tensor(name, list(shape), dt, kind="ExternalInput").ap()

    x_d = din("x", [nseq, T, D])
    meta_d = din("meta", [NM, D])
    wun_d = din("wun", [NU, 128, 4096])
    cos_d = din("cos", [128, NM + T])
    sin_d = din("sin", [128, NM + T])
    ident_d = din("ident", [128, 128], BF16)
    perm_d = din("perm", [128, 128], BF16)
    mask_d = din("mask", [128, 128], BF16)
    nmw_d = din("nmw", [128, KC])
    nfw_d = din("nfw", [128, KC])
    convw_d = din("convw", [128, KC * 3])
    subw_d = din("subw", [128, 1])
    nfin_d = din("nfin", [128, D])
    lam_d = din("lam", [128, 4 * 64])
    y_d = nc.dram_tensor("y", [nseq, T, D], F32, kind="ExternalOutput").ap()
    scr_d = nc.dram_tensor("wscr", [NU, 128, 4096], BF16, kind="Internal").ap()

    S = Sched(nc)
    with ExitStack() as st:
        def sb(name, shape, dt):
            return st.enter_context(nc.sbuf_tensor("sb_" + name, list(shape), dt))

        def ps(name, shape, dt):
            return st.enter_context(nc.psum_tensor(name, list(shape), dt))

        KT = sb("KT", [128, NH * T], BF16)
        KTv = KT[:].rearrange("p (h t) -> p h t", h=NH)
        KTm = sb("KTm", [128, NH, NM], BF16)
        Vcf = sb("Vc", [128, NKB * NH * 129], BF16)
        Vc = Vcf[:].rearrange("p (a b c) -> p a b c", a=NKB, b=NH)
        Vm = sb("Vm", [NM, NH, 129], BF16)
        cosb = sb("cosb", [128, TQ], F32)
        sinb = sb("sinb", [128, TQ], F32)
        ident = sb("ident", [128, 128], BF16)
        perm = sb("perm", [128, 128], BF16)
        mask = sb("mask", [128, 128], BF16)
        nmw = sb("nmw", [128, KC], F32)
        nfw = sb("nfw", [128, KC], F32)
        convw = sb("convw", [128, KC * 3], F32)
        sws = sb("sws", [128, 1], F32)
        nfin = sb("nfin", [128, D], BF16)
        lamv = sb("lamv", [128, 4 * 64], F32)
        lamt = sb("lamt", [128, 2 * 64], F32)
        lams = sb("lams", [128, 4], F32)
        nlam = sb("nlam", [128, 1], F32)
        nhalf = sb("nhalf", [128, 1], F32)
        h = sb("h", [128, 4, D], F32)
        xin = sb("xin", [128, D], F32)
        xsb = [sb("xs%d" % i, [128, D], BF16) for i in range(4)]
        hnT = sb("hnT", [128, KC, TQ], BF16)
        arena = sb("arena", [128, 4096 + 4096 + KC * 514], BF16)
        QT = arena[:, 0:4096].rearrange("p (h t) -> p h t", h=NH)
        cob = arena[:, 4096:8192].rearrange("p (c t) -> p c t", c=KC)
        ubuf = arena[:, 8192:8192 + KC * 514].rearrange("p (c t) -> p c t", c=KC)
        actT = arena[:, 0:NJ * TQ].rearrange("p (j t) -> p j t", j=NJ)
        tg = [sb("tg%d" % i, [128, TQ], BF16) for i in range(2)]
        mg = sb("mg", [128, KC, TQ], BF16)
        oaT = sb("oaT", [128, NH, TQ], BF16)
        Eb = [sb("E%d" % i, [128, 2, TQ], BF16) for i in range(2)]
        rtb = [sb("rt%d" % i, [128, 1040], F32) for i in range(2)]
        rt = rtb[0]
        ocp = rt[:, 0:8 * 129].rearrange("p (r e) -> p r e", e=129)
        qkbb = [sb("qkb%d" % i, [128, TQ], BF16) for i in range(2)]
        osb = sb("osb", [128, 4, 128], F32)
        ttmp = sb("ttmp", [128, 128], F32)
        oab = sb("oab", [128, 4, 128], BF16)
        junk = sb("junk", [128, D], BF16)
        swt = sb("swt", [128, TQ], BF16)
        swa = sb("swa", [128, TQ], BF16)
        cvb = [sb("cv%d" % i, [128, TQ], F32) for i in range(2)]
        m1 = sb("m1", [128, TQ], BF16)
        stat = sb("stat", [128, 96], F32)
        um = sb("um", [128, KC, NM], BF16)
        uhs = sb("uhs", [128, KC, 2], BF16)
        ccm = sb("ccm", [128, KC, NM], F32)
        wring = sb("wring", [128, NSLOT, 4096], BF16)
        if NH * T // 2 >= 8192:
            stg = KT[:].bitcast(F32)
        else:
            stg = sb("stg", [128, 8192], F32)[:]

        pS = [ps("pS%d" % i, [128, 2, 512], F32) for i in range(2)]
        pO = ps("pO", [128, 3, 512], F32)
        pX = ps("pX", [128, 512], F32)
        banks = [pS[0][:, 0, :], pS[0][:, 1, :], pS[1][:, 0, :], pS[1][:, 1, :],
                 pO[:, 0, :], pO[:, 1, :], pO[:, 2, :], pX[:]]
        pXb = pX[:].bitcast(BF16)

        acc_state = {"i": 0}
        ACCS = [("b%d" % i, banks[i]) for i in range(7)]
        S.alias(["b0", "b1"], ["S0"])
        S.alias(["b2", "b3"], ["S1"])
        S.alias(["b4", "b5", "b6"], ["O"])
        S.alias([("actT", j) for j in range(NJ)],
                [("QT", q) for q in range(NH)] + [("cob", q) for q in range(KC)] + [("u", q) for q in range(KC)] + [("uh", q) for q in range(KC)])
        S.alias([("stg", 0), ("stg", 1)], [("KT", q, cc_) for q in range(NH) for cc_ in range(nch)])
        for sl_ in range(NSLOT):
            S.alias([("w", sl_)], [("wx", sl_, kc_) for kc_ in range(KC)])
        S.alias(["ocp", "ocp1", "ocp2"], ["rt1_0", "rt2_0"])
        S.psum_keys = set(["S0", "S1", "O", "X"] + ["b%d" % i for i in range(7)])

        def next_acc():
            i = acc_state["i"]
            acc_state["i"] = (i + 1) % 7
            return ACCS[i]

        wseq = []
        wst = {"issued": 0, "cons": 0}

        wslots = [wring[:, 0, :], wring[:, 1, :], wring[:, 2, :],
                  oaT[:].rearrange("p a b -> p (a b)"), mg[:].rearrange("p a b -> p (a b)")]
        S.alias([("w", 3)], [("oaT", q) for q in range(NH)])
        S.alias([("w", 4)], [("mg", q) for q in range(KC)])
        wplan = {}

        def wplan_build():
            last = [-1] * 5
            slot_of, prev_occ, need = [], [], []
            nmeta = len(meta_units)
            for i, (uu, wide) in enumerate(wseq):
                allowed = range(5) if wide else range(3)
                s_ = min(allowed, key=lambda q: last[q])
                slot_of.append(s_)
                prev_occ.append(last[s_])
                last[s_] = i
                nd = 0
                if i >= nmeta and s_ >= 3:
                    base = nmeta + ((i - nmeta) // NU) * NU
                    pos = (i - nmeta) % NU
                    if pos >= 22:
                        nd = base + (20 if s_ == 3 else 22)
                need.append(nd)
            wplan["slot"], wplan["prev"], wplan["need"] = slot_of, prev_occ, need

        def wget(u, held=0):
            n = wst["cons"]
            assert wseq[n][0] == u, (n, wseq[n], u)
            while (wst["issued"] < min(len(wseq), n + 6) and wplan["prev"][wst["issued"]] < n - held
                   and wplan["need"][wst["issued"]] <= n - held):
                i = wst["issued"]
                slot = wplan["slot"][i]
                uu = wseq[i][0]
                cast_upto(cast_index[uu] + 3)
                S.add(SP, lambda e, slot=slot, uu=uu: e.dma_start(out=wslots[slot], in_=scr_d[uu]),
                      reads=[("scr", uu)], writes=[("w", slot)], dma="w%d" % slot)
                wst["issued"] += 1
            assert wst["issued"] > n
            wst["cons"] += 1
            slot = wplan["slot"][n]
            return wslots[slot], ("w", slot)

        meta_units = [U_CC, U_CC + 1, U_CX, U_CX + 1, U_K, U_K + 1, U_V, U_V + 1]
        chunk_units = ([U_CC, U_CC + 1, U_CX, U_CX + 1, U_CB, U_V, U_CB + 1, U_V + 1, U_Q, U_Q + 1, U_K, U_K + 1,
                        U_GB, U_WPC, U_GB + 1, U_WPC + 1, U_GA, U_WPA, U_GA + 1, U_WPA + 1, U_WO, U_WO + 1]
                       + [U_WGU + i for i in range(11)] + [U_WD + i for i in range(6)])
        assert sorted(chunk_units) == list(range(NU))
        wseq.extend((u_, False) for u_ in meta_units)
        for _ in range(nseq * nch):
            wseq.extend((u_, not (12 <= p_ < 22)) for p_, u_ in enumerate(chunk_units))
        wplan_build()
        cast_order = meta_units + [u for u in chunk_units if u not in meta_units]

        def cload(dst, src, key):
            S.add(SP, lambda e: e.dma_start(out=dst, in_=src), writes=[key], dma="c_" + key)

        cload(ident[:], ident_d, "ident")
        cload(perm[:], perm_d, "perm")
        cload(mask[:], mask_d, "mask")
        cload(nmw[:], nmw_d, "nmw")
        cload(nfw[:], nfw_d, "nfw")
        cload(convw[:], convw_d, "convw")
        cload(sws[:], subw_d, "sws0")
        cload(xin[:], nfin_d, "xin")
        S.add(DVE, lambda e: e.tensor_copy(out=nfin[:], in_=xin[:]), reads=["xin"], writes=["nfin"])
        cload(lamv[:], lam_d, "lamv")
        S.add(POOL, lambda e: e.memset(nhalf[:], -0.5), writes=["nhalf"])
        S.add(POOL, lambda e: e.memset(Vc[:, :, :, 128:129], 1.0), writes=["Vones"])
        S.add(POOL, lambda e: e.memset(Vm[:, :, 128:129], 1.0), writes=["Vmones"])
        S.add(DVE, lambda e: e.tensor_scalar(out=sws[:], in0=sws[:], scalar1=float(1.0 - LAMBDA_INIT), scalar2=None,
                                             op0=ALU.mult), reads=["sws0"], writes=["sws"])
        lv = lamv[:].rearrange("p (a b) -> p a b", a=4)
        lt = lamt[:].rearrange("p (a b) -> p a b", a=2)
        S.add(DVE, lambda e: e.tensor_tensor(out=lt[:, 0, :], in0=lv[:, 0, :], in1=lv[:, 1, :], op=ALU.mult),
              reads=["lamv"], writes=["lt0"])
        S.add(DVE, lambda e: e.tensor_tensor(out=lt[:, 1, :], in0=lv[:, 2, :], in1=lv[:, 3, :], op=ALU.mult),
              reads=["lamv"], writes=["lt1"])
        S.add(DVE, lambda e: e.tensor_reduce(out=lams[:, 0:2], in_=lt, op=ALU.add, axis=mybir.AxisListType.X),
              reads=["lt0", "lt1"], writes=["lams01"])
        S.add(ACT, lambda e: e.activation(out=lams[:, 2:4], in_=lams[:, 0:2], func=AF.Exp),
              reads=["lams01"], writes=["lams23"])
        S.add(DVE, lambda e: e.scalar_tensor_tensor(out=nlam[:], in0=lams[:, 3:4], scalar=float(-LAMBDA_INIT),
                                                    in1=lams[:, 2:3], op0=ALU.add, op1=ALU.subtract),
              reads=["lams23"], writes=["nlam"])

        JIT = (nch == 4)
        N_PRE = 12 if JIT else NU
        cst = {"done": 0, "staged": 0}
        cast_index = {u_: i_ for i_, u_ in enumerate(cast_order)}
        v3 = lambda ap_: ap_.rearrange("p (k c) -> p k c", k=KC)
        if JIT:
            KT32 = KT[:].bitcast(F32).rearrange("p (h t) -> p h t", h=NH)
            Vflat = Vcf[:]
            V32 = Vcf[:].bitcast(F32)
            jstg = [KT32[:, :, 256:768], v3(V32[:, 2064:2064 + 4096])]
            jdst = [v3(Vflat[:, 12 * 1032:12 * 1032 + 4096]), KTv[:, :, 1536:2048]]
            S.alias([("stgj", 0)], [("KT", q, cc_) for q in range(NH) for cc_ in (1, 2)] + [("stg", 0), ("stg", 1)])
            S.alias([("stgj", 1)], [("V", kb_, hf_) for kb_ in range(4, 12) for hf_ in range(2)])
            S.alias([("dstj", 0, kc_) for kc_ in range(KC)], [("V", kb_, hf_) for kb_ in range(12, 16) for hf_ in range(2)])
            S.alias([("dstj", 1, kc_) for kc_ in range(KC)], [("KT", q, 3) for q in range(NH)] + [("stg", 0), ("stg", 1)])

        def cast_views(ci_):
            if ci_ < N_PRE:
                sl = ci_ % 2
                slot = ci_ % NSLOT
                return (v3(stg[:, sl * 4096:(sl + 1) * 4096]), ("stg", sl), v3(wring[:, slot, :]),
                        [("wx", slot, kc) for kc in range(KC)], "p%d" % sl, "p%d" % slot)
            sl = ci_ % 2
            return (jstg[sl], ("stgj", sl), jdst[sl], [("dstj", sl, kc) for kc in range(KC)], "j%d" % sl, "j%d" % sl)

        def stage_upto(k):
            while cst["staged"] < min(k, NU):
                ci_ = cst["staged"]
                u_ = cast_order[ci_]
                sv_, sk_, _, _, tg_, _ = cast_views(ci_)
                S.add(SP, lambda e, sv_=sv_, u_=u_: e.dma_start(out=sv_, in_=v3(wun_d[u_])), writes=[sk_], dma="stg" + tg_)
                cst["staged"] += 1

        def cast_unit(ci_):
            u = cast_order[ci_]
            stage_upto(ci_ + 1)
            sv, sk, dv, dks, _, tg_ = cast_views(ci_)
            if u < U_WPC or (U_WGU <= u < U_WD):
                sc = nmw if u < U_WPC else nfw
                sck = "nmw" if u < U_WPC else "nfw"
                for kc in range(KC):
                    o_, i_ = dv[:, kc, :], sv[:, kc, :]
                    if (ci_ + kc) % 2 == 0:
                        S.add(ACT, lambda e, o_=o_, i_=i_, kc=kc, sc=sc: e.activation(out=o_, in_=i_, func=AF.Copy, scale=sc[:, kc:kc + 1]),
                              reads=[sk, sck], writes=[dks[kc]])
                    else:
                        S.add(DVE, lambda e, o_=o_, i_=i_, kc=kc, sc=sc: e.tensor_scalar(out=o_, in0=i_, scalar1=sc[:, kc:kc + 1], scalar2=None, op0=ALU.mult),
                              reads=[sk, sck], writes=[dks[kc]])
            else:
                S.add(ACT, lambda e: e.activation(out=dv[:, 0:4, :], in_=sv[:, 0:4, :], func=AF.Copy), reads=[sk], writes=dks[0:4])
                S.add(DVE, lambda e: e.tensor_copy(out=dv[:, 4:8, :], in_=sv[:, 4:8, :]), reads=[sk], writes=dks[4:8])
            if ci_ + 2 < N_PRE or ci_ >= N_PRE:
                stage_upto(ci_ + 3)
            S.add(SP, lambda e: e.dma_start(out=v3(scr_d[u]), in_=dv), reads=list(dict.fromkeys(dks)), writes=[("scr", u)], dma="scrw" + tg_)

        def cast_upto(k):
            while cst["done"] < min(k, NU):
                cast_unit(cst["done"])
                cst["done"] += 1

        stage_upto(2)
        cast_upto(N_PRE)

        def fm_tile(wv, wk, ci, rhsT, rkeys, n, acc=None):
            ak, ab = acc if acc is not None else next_acc()

            def f(e):
                ins = None
                for kc in range(KC):
                    ins = e.matmul(ab[:, 0:n], lhsT=wv[:, kc * 512 + ci * 128: kc * 512 + (ci + 1) * 128],
                                   rhs=rhsT[:, kc, 0:n], start=(kc == 0), stop=(kc == KC - 1))
                return ins
            S.add(PE, f, reads=[wk] + list(rkeys), writes=[ak])
            return ak, ab

        rp_state = {"i": 0}

        def rope_tile(ak, ab, n, cs, sn, cskeys, dst, dkey, defer=False):
            i = rp_state["i"]
            rp_state["i"] = i + 1
            b = i % 2
            qkb = qkbb[b]
            rt1 = rtb[b][:, 0:TQ]
            rt2 = rtb[b][:, 520:520 + TQ]
            k1, k2, kq = "rt1_%d" % b, "rt2_%d" % b, "qkb%d" % b
            S.add(ACT, lambda e: e.activation(out=qkb[:, 0:n], in_=ab[:, 0:n], func=AF.Copy), reads=[ak], writes=[kq])

            def back():
                S.add(PE, lambda e: e.matmul(pX[:, 0:n], lhsT=perm[:], rhs=qkb[:, 0:n], start=True, stop=True),
                      reads=[kq, "perm"], writes=["X"])
                S.add(DVE, lambda e: e.tensor_tensor(out=rt1[:, 0:n], in0=ab[:, 0:n], in1=cs, op=ALU.mult),
                      reads=[ak] + cskeys, writes=[k1])
                S.add(DVE, lambda e: e.tensor_tensor(out=rt2[:, 0:n], in0=pX[:, 0:n], in1=sn, op=ALU.mult),
                      reads=["X"] + cskeys, writes=[k2])
                S.add(POOL, lambda e: e.tensor_tensor(out=dst, in0=rt1[:, 0:n], in1=rt2[:, 0:n], op=ALU.add),
                      reads=[k1, k2], writes=[dkey])
            if defer:
                return back
            back()
            return None

        nt_state = {"i": 0}

        def norm_tile(src, skey, npart, dst_hnT, dkeys, col0, defer=False):
            i = nt_state["i"]
            nt_state["i"] = i + 1
            xs = xsb[i % 4]
            xk = "xs%d" % (i % 4)
            c0 = 64 + 3 * (i % 8)
            sk = "nst%d" % (i % 8)
            S.add(ACT, lambda e: e.activation(out=junk[0:npart, :], in_=src, func=AF.Square, accum_out=stat[0:npart, c0:c0 + 1]),
                  reads=[skey], writes=["junk", sk + "a"])
            S.add(DVE, lambda e: e.tensor_scalar(out=stat[0:npart, c0 + 1:c0 + 2], in0=stat[0:npart, c0:c0 + 1], scalar1=1.0 / D, scalar2=EPS,
                                                 op0=ALU.mult, op1=ALU.add), reads=[sk + "a"], writes=[sk + "b"])
            S.add(POOL, lambda e: e.tensor_tensor(out=stat[0:npart, c0 + 2:c0 + 3], in0=stat[0:npart, c0 + 1:c0 + 2], in1=nhalf[0:npart, :], op=ALU.pow),
                  reads=[sk + "b", "nhalf"], writes=[sk + "c"])
            S.add(ACT, lambda e: e.activation(out=xs[0:npart, :], in_=src, func=AF.Copy, scale=stat[0:npart, c0 + 2:c0 + 3]),
                  reads=[skey, sk + "c"], writes=[xk])

            def back():
                def f(e):
                    ins = None
                    for kc in range(KC):
                        ins = e.transpose(pXb[:, kc * 128: kc * 128 + npart], xs[0:npart, kc * 128:(kc + 1) * 128], ident[0:npart, 0:npart])
                    return ins
                S.add(PE, f, reads=[xk, "ident"], writes=["X"])
                S.add(DVE, lambda e: e.tensor_copy(out=dst_hnT[:, :, col0:col0 + npart],
                                                   in_=pXb.rearrange("p (k t) -> p k t", k=KC)[:, :, 0:npart]),
                      reads=["X"], writes=dkeys)
            if defer:
                return back
            back()
            return None

        S.add(SP, lambda e: e.dma_start(out=xin[0:NM, :], in_=meta_d), writes=["xin"], dma="xin")
        S.add(SP, lambda e: e.dma_start(out=cosb[:, 0:NM], in_=cos_d[:, 0:NM]), writes=["cos"], dma="cos")
        S.add(SP, lambda e: e.dma_start(out=sinb[:, 0:NM], in_=sin_d[:, 0:NM]), writes=["sin"], dma="sin")
        hk_all = [("hnT", tt) for tt in range(4)]
        norm_tile(xin[0:NM, :], "xin", NM, hnT, hk_all, 0)
        for t8 in range(8):
            if t8 % 4 == 0:
                wv, wk = wget(U_CC + t8 // 4)
            ak, ab = fm_tile(wv, wk, t8 % 4, hnT, hk_all, NM)
            S.add(ACT, lambda e, ab=ab, t8=t8: e.activation(out=ccm[:, t8, :], in_=ab[:, 0:NM], func=AF.Copy),
                  reads=[ak], writes=[("ccm", t8)])
        for t8 in range(8):
            if t8 % 4 == 0:
                wv, wk = wget(U_CX + t8 // 4)
            ak, ab = fm_tile(wv, wk, t8 % 4, hnT, hk_all, NM)
            S.add(DVE, lambda e, ab=ab, t8=t8: e.tensor_tensor(out=um[:, t8, :], in0=ab[:, 0:NM], in1=ccm[:, t8, :], op=ALU.mult),
                  reads=[ak, ("ccm", t8)], writes=[("um", t8)])
        for t8 in range(8):
            if t8 % 4 == 0:
                wv, wk = wget(U_K + t8 // 4)
            ak, ab = fm_tile(wv, wk, t8 % 4, hnT, hk_all, NM)
            rope_tile(ak, ab, NM, cosb[:, 0:NM], sinb[:, 0:NM], ["cos", "sin"], KTm[:, t8, :], ("KTm", t8))
        for hf in range(2):
            wv, wk = wget(U_V + hf)
            ak, ab = next_acc()

            def f(e, wv=wv, ab=ab):
                ins = None
                for kc in range(KC):
                    ins = e.matmul(ab[0:NM, :], lhsT=hnT[:, kc, 0:NM], rhs=wv[:, kc * 512:(kc + 1) * 512],
                                   start=(kc == 0), stop=(kc == KC - 1))
                return ins
            S.add(PE, f, reads=[wk] + hk_all, writes=[ak])
            S.add(ACT, lambda e, ab=ab, hf=hf: e.activation(out=Vm[:, hf * 4:(hf + 1) * 4, 0:128],
                                                           in_=ab[0:NM, :].rearrange("p (a b) -> p a b", a=4), func=AF.Copy),
                  reads=[ak, "Vmones"], writes=[("Vm", hf)])

        mview = mask[:]

        def dbg(stage):
            if DBG != stage:
                return
            items = [("QT", arena[:, 0:4096], [128, 4096], BF16), ("cob", arena[:, 4096:8192], [128, 4096], BF16),
                     ("actT", arena[:, 0:NJ * TQ], [128, NJ * TQ], BF16),
                     ("KT", KT[:, 0:4096], [128, 4096], BF16), ("Vc", Vc[:, 0:4, :, :].rearrange("p a b c -> p (a b c)"), [128, 4 * NH * 129], BF16),
                     ("mg", mg[:].rearrange("p a b -> p (a b)"), [128, 4096], BF16),
                     ("oaT", oaT[:].rearrange("p a b -> p (a b)"), [128, 4096], BF16), ("hnT", hnT[:].rearrange("p a b -> p (a b)"), [128, 4096], BF16),
                     ("h", h[:].rearrange("p a b -> p (a b)"), [128, 4 * D], F32), ("stat", stat[:], [128, 96], F32)]
            allk = list(S.last_writer.keys())
            for name, ap_, shp, dt_ in items:
                dd = nc.dram_tensor("dbg_" + name, shp, dt_, kind="ExternalOutput").ap()
                S.add(SP, lambda e, dd=dd, ap_=ap_: e.dma_start(out=dd, in_=ap_), reads=allk, dma="dbg_" + name)
            S.emit(final_dma_keys=["dbg_" + it[0] for it in items])
            raise _Stop()

        def p1_prefetch(s, c):
            t0 = c * TQ

            def mk(tt):
                def front():
                    if tt == 0:
                        S.add(SP, lambda e: e.dma_start(out=cosb[:], in_=cos_d[:, NM + t0: NM + t0 + TQ]), writes=["cos"], dma="cos")
                        S.add(SP, lambda e: e.dma_start(out=sinb[:], in_=sin_d[:, NM + t0: NM + t0 + TQ]), writes=["sin"], dma="sin")
                    S.add(POOL, lambda e: e.dma_start(out=xin[:], in_=x_d[s, t0 + tt * 128: t0 + (tt + 1) * 128, :]),
                          writes=["xin"], dma="xin")
                    return norm_tile(xin[:], "xin", 128, hnT, [("hnT", tt)], tt * 128, defer=True)
                return front
            return [mk(tt) for tt in range(4)]

        def do_chunk(s, c, nxt, restore_ones=False):
            t0 = c * TQ
            if restore_ones:
                S.add(POOL, lambda e: e.memset(Vc[:, 4:16, :, 128:129], 1.0),
                      writes=["Vones"] + [("V", kb_, hf_) for kb_ in range(4, 16) for hf_ in range(2)])
            hkeys = [("hnT", tt) for tt in range(4)]
            for tt in range(4):
                S.add(POOL, lambda e, tt=tt: e.dma_start(out=h[:, tt, :], in_=x_d[s, t0 + tt * 128: t0 + (tt + 1) * 128, :]),
                      writes=[("h", tt)], dma="x%d" % tt)

            dbg("p1")
            for t8 in range(8):
                if t8 % 4 == 0:
                    wv, wk = wget(U_CC + t8 // 4)
                ak, ab = fm_tile(wv, wk, t8 % 4, hnT, hkeys, TQ)
                S.add(ACT, lambda e, ab=ab, t8=t8: e.activation(out=cob[:, t8, :], in_=ab, func=AF.Copy),
                      reads=[ak], writes=[("cob", t8)])
            for half in range(2):
                wv, wk = wget(U_CX + half)
                for t4 in range(4):
                    t8 = half * 4 + t4
                    ak, ab = fm_tile(wv, wk, t4, hnT, hkeys, TQ)
                    if c == 0:
                        S.add(POOL, lambda e, t8=t8: e.tensor_copy(out=ubuf[:, t8, 0:2], in_=um[:, t8, NM - 2:NM]),
                              reads=[("um", t8)], writes=[("uh", t8)])
                    else:
                        S.add(POOL, lambda e, t8=t8: e.tensor_copy(out=ubuf[:, t8, 0:2], in_=uhs[:, t8, :]),
                              reads=[("uhs", t8)], writes=[("uh", t8)])
                    S.add(DVE, lambda e, ab=ab, t8=t8: e.tensor_tensor(out=ubuf[:, t8, 2:2 + TQ], in0=ab, in1=cob[:, t8, :], op=ALU.mult),
                          reads=[ak, ("cob", t8), ("uh", t8)], writes=[("u", t8)])
                    S.add(POOL, lambda e, t8=t8: e.tensor_copy(out=uhs[:, t8, :], in_=ubuf[:, t8, TQ:TQ + 2]),
                          reads=[("u", t8)], writes=[("uhs", t8)])

            for half in range(2):
                wv, wk = wget(U_CB + half)
                for t4 in range(4):
                    t8 = half * 4 + t4
                    ak, ab = fm_tile(wv, wk, t4, hnT, hkeys, TQ)
                    cv = cvb[t8 % 2]
                    cvk = "cv%d" % (t8 % 2)
                    S.add(POOL, lambda e, t8=t8, cv=cv: e.tensor_scalar(out=cv[:], in0=ubuf[:, t8, 0:TQ], scalar1=convw[:, t8 * 3:t8 * 3 + 1],
                                                                        scalar2=0.0, op0=ALU.mult, op1=ALU.add),
                          reads=[("u", t8), ("uh", t8), "convw"], writes=[cvk])
                    S.add(DVE, lambda e, t8=t8, cv=cv: e.scalar_tensor_tensor(out=cv[:], in0=ubuf[:, t8, 1:1 + TQ], scalar=convw[:, t8 * 3 + 1:t8 * 3 + 2],
                                                                               in1=cv[:], op0=ALU.mult, op1=ALU.add),
                          reads=[("u", t8), ("uh", t8), "convw", cvk], writes=[cvk])
                    S.add(DVE, lambda e, t8=t8, cv=cv: e.scalar_tensor_tensor(out=cv[:], in0=ubuf[:, t8, 2:2 + TQ], scalar=convw[:, t8 * 3 + 2:t8 * 3 + 3],
                                                                               in1=cv[:], op0=ALU.mult, op1=ALU.add),
                          reads=[("u", t8), "convw", cvk], writes=[cvk])
                    S.add(DVE, lambda e, ab=ab, t8=t8, cv=cv: e.tensor_tensor(out=cob[:, t8, :], in0=ab, in1=cv[:], op=ALU.mult),
                          reads=[ak, cvk], writes=[("cob", t8)])
                hf = half
                wv, wk = wget(U_V + hf)
                for tt in range(4):
                    ak, ab = next_acc()

                    def f(e, wv=wv, ab=ab, tt=tt):
                        ins = None
                        for kc in range(KC):
                            ins = e.matmul(ab, lhsT=hnT[:, kc, tt * 128:(tt + 1) * 128], rhs=wv[:, kc * 512:(kc + 1) * 512],
                                           start=(kc == 0), stop=(kc == KC - 1))
                        return ins
                    S.add(PE, f, reads=[wk, ("hnT", tt)], writes=[ak])
                    kb = c * 4 + tt
                    S.add(ACT, lambda e, ab=ab, hf=hf, kb=kb: e.activation(out=Vc[:, kb, hf * 4:(hf + 1) * 4, 0:128],
                                                                         in_=ab.rearrange("p (a b) -> p a b", a=4), func=AF.Copy),
                          reads=[ak, "Vones"], writes=[("V", kb, hf)])

            rpend = None
            for t8 in range(8):
                if t8 % 4 == 0:
                    wv, wk = wget(U_Q + t8 // 4)
                ak, ab = fm_tile(wv, wk, t8 % 4, hnT, hkeys, TQ)
                nb = rope_tile(ak, ab, TQ, cosb[:], sinb[:], ["cos", "sin"], QT[:, t8, :], ("QT", t8), defer=True)
                if rpend is not None:
                    rpend()
                rpend = nb
            for t8 in range(8):
                if t8 % 4 == 0:
                    wv, wk = wget(U_K + t8 // 4)
                ak, ab = fm_tile(wv, wk, t8 % 4, hnT, hkeys, TQ)
                nb = rope_tile(ak, ab, TQ, cosb[:], sinb[:], ["cos", "sin"], KTv[:, t8, t0:t0 + TQ], ("KT", t8, c), defer=True)
                rpend()
                rpend = nb
            rpend()

            dbg("p2")
            def do_head(hd, pending):
                kbl = [("m", 0, 0)] + [("f", kb, 0) for kb in range(4 * c)] + [("d", 4 * c + i, 128 * i) for i in range(4)]
                nk = len(kbl)
                started = set()

                def qk_step(i):
                    kind, kb, q0 = kbl[i]
                    pSi = pS[i % 2]
                    E = Eb[i % 2]
                    nkeys = NM if kind == "m" else 128

                    def f(e):
                        ins = None
                        for sub in range(2):
                            r0 = sub * 64
                            if kind == "m":
                                lt_ = KTm[r0:r0 + 64, hd, :]
                            else:
                                lt_ = KTv[r0:r0 + 64, hd, kb * 128:(kb + 1) * 128]
                            ins = e.matmul(pSi[0:nkeys, sub, q0:TQ], lhsT=lt_, rhs=QT[r0:r0 + 64, hd, q0:TQ], start=True, stop=(kind != "d"),
                                           skip_group_check=True)
                        if kind == "d":
                            for sub in range(2):
                                ins = e.matmul(pSi[:, sub, q0:q0 + 128], lhsT=ident[:], rhs=mask[:], start=False, stop=True, skip_group_check=True)
                        return ins
                    rk = [("QT", hd), "ident", "mask"] + ([("KTm", hd)] if kind == "m" else [("KT", hd, kb // 4)])
                    S.add(PE, f, reads=rk, writes=["S%d" % (i % 2)])
                    S.add(ACT, lambda e: e.activation(out=E[0:nkeys, :, q0:TQ], in_=pSi[0:nkeys, :, q0:TQ], func=AF.Exp, scale=0.125),
                          reads=["S%d" % (i % 2)], writes=["E%d" % (i % 2)])

                def pv_step(i):
                    kind, kb, q0 = kbl[i]
                    E = Eb[i % 2]
                    nkeys = NM if kind == "m" else 128

                    def f(e):
                        ins = None
                        for qi in range(q0 // 128, 4):
                            for sub in range(2):
                                r = qi * 2 + sub
                                bank, off = r // 3, (r % 3) * 129
                                st_ = (bank not in started)
                                started.add(bank)
                                rhs = Vm[:, hd, :] if kind == "m" else Vc[:, kb, hd, :]
                                sp_ = (kind == "d" and kb % 4 == qi)
                                ins = e.matmul(pO[:, bank, off:off + 129], lhsT=E[0:nkeys, sub, qi * 128:(qi + 1) * 128], rhs=rhs,
                                               start=st_, stop=sp_, skip_group_check=True)
                        return ins
                    rk = ["E%d" % (i % 2)] + (["Vmones", ("Vm", hd // 4)] if kind == "m" else ["Vones", ("V", kb, hd // 4)])
                    S.add(PE, f, reads=rk, writes=["O"])

                for i in range(nk + 1):
                    if i < nk:
                        qk_step(i)
                    if i >= 1:
                        pv_step(i - 1)
                    if i == min(nk, 6) and pending is not None:
                        pending()
                        pending = None
                if pending is not None:
                    pending()

                S.add(DVE, lambda e: e.tensor_copy(out=rt[:, 0:387], in_=pO[:, 0, 0:387]), reads=["O"], writes=["ocp"])
                S.add(DVE, lambda e: e.tensor_copy(out=rt[:, 387:774], in_=pO[:, 1, 0:387]), reads=["O"], writes=["ocp1"])
                S.add(DVE, lambda e: e.tensor_copy(out=rt[:, 774:1032], in_=pO[:, 2, 0:258]), reads=["O"], writes=["ocp2"])
                ok3 = ["ocp", "ocp1", "ocp2"]
                S.add(DVE, lambda e: e.reciprocal(out=stat[:, 8:16], in_=ocp[:, :, 128]), reads=ok3, writes=["rz"])
                S.add(DVE, lambda e: e.tensor_scalar(out=stat[:, 16:24], in0=stat[:, 8:16], scalar1=nlam[:, 0:1], scalar2=None, op0=ALU.mult),
                      reads=["rz", "nlam"], writes=["rzs"])
                for qi in range(4):
                    r0_, r1_ = 2 * qi, 2 * qi + 1
                    S.add(DVE, lambda e, r1_=r1_: e.tensor_scalar(out=ttmp[:], in0=ocp[:, r1_, 0:128], scalar1=stat[:, 16 + r1_:17 + r1_], scalar2=None, op0=ALU.mult),
                          reads=ok3 + ["rzs"], writes=["ttmp"])
                    S.add(DVE, lambda e, r0_=r0_, qi=qi: e.scalar_tensor_tensor(out=osb[:, qi, :], in0=ocp[:, r0_, 0:128], scalar=stat[:, 8 + r0_:9 + r0_], in1=ttmp[:],
                                                                                  op0=ALU.mult, op1=ALU.add),
                          reads=ok3 + ["rz", "ttmp"], writes=[("osb", qi)])
                    S.add(DVE, lambda e, qi=qi: e.scalar_tensor_tensor(out=junk[:, 0:128], in0=osb[:, qi, :], scalar=1.0, in1=osb[:, qi, :],
                                                                      op0=ALU.mult, op1=ALU.mult, accum_out=stat[:, 24 + qi:25 + qi]),
                          reads=[("osb", qi)], writes=["junkd", ("ss", qi)])
                S.add(DVE, lambda e: e.tensor_scalar(out=stat[:, 28:32], in0=stat[:, 24:28], scalar1=1.0 / 128, scalar2=EPS, op0=ALU.mult, op1=ALU.add),
                      reads=[("ss", q) for q in range(4)], writes=["ssv"])
                S.add(POOL, lambda e: e.tensor_tensor(out=stat[:, 32:36], in0=stat[:, 28:32], in1=nhalf[:, 0:1].to_broadcast([128, 4]), op=ALU.pow),
                      reads=["ssv", "nhalf"], writes=["srs"])
                S.add(DVE, lambda e: e.tensor_tensor(out=oab[:], in0=osb[:], in1=stat[:, 32:36].unsqueeze(2).to_broadcast([128, 4, 128]), op=ALU.mult),
                      reads=[("osb", q) for q in range(4)] + ["srs"], writes=["oab"])

                def finish():
                    def ftr(e):
                        ins = None
                        for qi in range(4):
                            ins = e.transpose(pXb[:, qi * 128:(qi + 1) * 128], oab[:, qi, :], ident[:])
                        return ins
                    S.add(PE, ftr, reads=["oab", "ident"], writes=["X"])
                    S.add(DVE, lambda e: e.tensor_scalar(out=oaT[:, hd, :], in0=pXb[:, 0:TQ], scalar1=sws[:, 0:1], scalar2=None, op0=ALU.mult),
                          reads=["X", "sws"], writes=[("oaT", hd)])
                return finish

            pend = None
            for hd_ in range(NH):
                pend = do_head(hd_, pend)

            dbg("p3")
            okeys = [("oaT", q) for q in range(NH)]
            ckeys = [("cob", q) for q in range(KC)]
            for half in range(2):
                gv, gk = wget(U_GB + half)
                wv, wk = wget(U_WPC + half, held=1)
                for t4 in range(4):
                    t8 = half * 4 + t4
                    ak, ab = fm_tile(gv, gk, t4, hnT, hkeys, TQ)
                    tgi = tg[t8 % 2]
                    tgk = "tg%d" % (t8 % 2)
                    S.add(ACT, lambda e, ab=ab, tgi=tgi: e.activation(out=tgi[:], in_=ab, func=AF.Tanh, scale=0.5), reads=[ak], writes=[tgk])
                    ak2, ab2 = fm_tile(wv, wk, t4, cob, ckeys, TQ)
                    S.add(DVE, lambda e, ab2=ab2, tgi=tgi, t8=t8: e.scalar_tensor_tensor(out=mg[:, t8, :], in0=tgi[:], scalar=1.0, in1=ab2, op0=ALU.add, op1=ALU.mult),
                          reads=[ak2, tgk], writes=[("mg", t8)])
                    if half == 0 and t4 == 1 and pend is not None:
                        pend()
                        pend = None
            for half in range(2):
                gv, gk = wget(U_GA + half)
                wv, wk = wget(U_WPA + half, held=1)
                for t4 in range(4):
                    t8 = half * 4 + t4
                    ak, ab = fm_tile(gv, gk, t4, hnT, hkeys, TQ)
                    tgi = tg[t8 % 2]
                    tgk = "tg%d" % (t8 % 2)
                    S.add(ACT, lambda e, ab=ab, tgi=tgi: e.activation(out=tgi[:], in_=ab, func=AF.Tanh, scale=0.5), reads=[ak], writes=[tgk])
                    ak2, ab2 = fm_tile(wv, wk, t4, oaT, okeys, TQ)
                    S.add(DVE, lambda e, ab2=ab2, tgi=tgi: e.scalar_tensor_tensor(out=m1[:], in0=tgi[:], scalar=1.0, in1=ab2, op0=ALU.add, op1=ALU.mult),
                          reads=[ak2, tgk], writes=["m1"])
                    S.add(DVE, lambda e, t8=t8: e.tensor_tensor(out=mg[:, t8, :], in0=mg[:, t8, :], in1=m1[:], op=ALU.add),
                          reads=["m1", ("mg", t8)], writes=[("mg", t8)])
            mkeys = [("mg", q) for q in range(KC)]

            dbg("p4")
            wv0, wk0 = wget(U_WO)
            wv1, wk1 = wget(U_WO + 1, held=1)
            prev_back = None
            for tt in range(4):
                for hf in range(2):
                    wv, wk = (wv0, wk0) if hf == 0 else (wv1, wk1)
                    ak, ab = next_acc()

                    def f(e, wv=wv, ab=ab, tt=tt):
                        ins = None
                        for kc in range(KC):
                            ins = e.matmul(ab, lhsT=mg[:, kc, tt * 128:(tt + 1) * 128], rhs=wv[:, kc * 512:(kc + 1) * 512],
                                           start=(kc == 0), stop=(kc == KC - 1))
                        return ins
                    S.add(PE, f, reads=[wk] + mkeys, writes=[ak])
                    hv = h[:, tt, hf * 512:(hf + 1) * 512]
                    S.add(DVE, lambda e, ab=ab, hv=hv: e.scalar_tensor_tensor(out=hv, in0=ab, scalar=0.5, in1=hv, op0=ALU.mult, op1=ALU.add),
                          reads=[ak, ("h", tt)], writes=[("h", tt)] if hf == 1 else [("hx", tt)])
                if prev_back is not None:
                    prev_back()
                prev_back = norm_tile(h[:, tt, :], ("h", tt), 128, hnT, [("hnT", tt)], tt * 128, defer=True)
            prev_back()

            dbg("p6")
            pfronts = p1_prefetch(*nxt) if nxt is not None else []
            pbacks = []
            for ug in range(11):
                wv, wk = wget(U_WGU + ug)
                if pfronts and ug in (3, 5, 7, 9):
                    pbacks.append(pfronts.pop(0)())
                for t2 in range(2):
                    j = 2 * ug + t2
                    gk, gb_ = fm_tile(wv, wk, 2 * t2, hnT, hkeys, TQ)
                    uk, ub_ = fm_tile(wv, wk, 2 * t2 + 1, hnT, hkeys, TQ)
                    S.add(ACT, lambda e, gb_=gb_: e.activation(out=swt[:], in_=gb_, func=AF.Tanh, scale=0.5), reads=[gk], writes=["swt"])
                    S.add(DVE, lambda e, gb_=gb_: e.scalar_tensor_tensor(out=swa[:], in0=swt[:], scalar=1.0, in1=gb_, op0=ALU.add, op1=ALU.mult),
                          reads=[gk, "swt"], writes=["swa"])
                    S.add(DVE, lambda e, ub_=ub_, j=j: e.tensor_tensor(out=actT[:, j, :], in0=ub_, in1=swa[:], op=ALU.mult),
                          reads=[uk, "swa"], writes=[("actT", j)])

            dbg("p7")
            for hf in range(2):
                accs = [next_acc() for _ in range(4)]
                for g3 in range(3):
                    wv, wk = wget(U_WD + hf * 3 + g3)
                    nj = 8 if g3 < 2 else 6
                    for jj in range(nj):
                        j = g3 * 8 + jj
                        if pbacks and ((hf == 0 and j in (8, 16)) or (hf == 1 and j in (3, 12))):
                            pbacks.pop(0)()
                        for tt in range(4):
                            ak, ab = accs[tt]
                            S.add(PE, lambda e, wv=wv, jj=jj, j=j, tt=tt, ab=ab: e.matmul(ab, lhsT=actT[:, j, tt * 128:(tt + 1) * 128], rhs=wv[:, jj * 512:(jj + 1) * 512],
                                                                                        start=(j == 0), stop=(j == NJ - 1)),
                                  reads=[wk, ("actT", j)], writes=[ak])
                for tt in range(4):
                    ak, ab = accs[tt]
                    hv = h[:, tt, hf * 512:(hf + 1) * 512]
                    S.add(DVE, lambda e, ab=ab, hv=hv: e.scalar_tensor_tensor(out=hv, in0=ab, scalar=0.5, in1=hv, op0=ALU.mult, op1=ALU.add),
                          reads=[ak, ("h", tt)], writes=[("h", tt)] if hf == 1 else [("hx", tt)])

            for tt in range(4):
                S.add(ACT, lambda e, tt=tt: e.activation(out=junk[:], in_=h[:, tt, :], func=AF.Square, accum_out=stat[:, 40 + tt:41 + tt]),
                      reads=[("h", tt)], writes=["junk", ("fs", tt)])
                S.add(DVE, lambda e, tt=tt: e.tensor_scalar(out=stat[:, 44 + tt:45 + tt], in0=stat[:, 40 + tt:41 + tt], scalar1=1.0 / D, scalar2=EPS,
                                                            op0=ALU.mult, op1=ALU.add), reads=[("fs", tt)], writes=[("fv", tt)])
                S.add(POOL, lambda e, tt=tt: e.tensor_tensor(out=stat[:, 48 + tt:49 + tt], in0=stat[:, 44 + tt:45 + tt], in1=nhalf[:], op=ALU.pow),
                      reads=[("fv", tt), "nhalf"], writes=[("fr", tt)])
                S.add(DVE, lambda e, tt=tt: e.scalar_tensor_tensor(out=h[:, tt, :], in0=h[:, tt, :], scalar=stat[:, 48 + tt:49 + tt], in1=nfin[:],
                                                                   op0=ALU.mult, op1=ALU.mult),
                      reads=[("h", tt), ("fr", tt), "nfin"], writes=[("h", tt)])
                S.add(POOL, lambda e, tt=tt: e.dma_start(out=y_d[s, t0 + tt * 128: t0 + (tt + 1) * 128, :], in_=h[:, tt, :]),
                      reads=[("h", tt)], dma="y%d" % tt)

        order = [(s_, c_) for s_ in range(nseq) for c_ in range(nch)]
        for f_ in p1_prefetch(*order[0]):
            f_()()
        try:
            for i_, (s_, c_) in enumerate(order):
                do_chunk(s_, c_, order[i_ + 1] if i_ + 1 < len(order) else None, restore_ones=(JIT and i_ == 1))
            S.emit(final_dma_keys=["y0", "y1", "y2", "y3"])
        except _Stop:
            pass
    return nc


def _host_consts(T):
    pos = np.arange(NM + T, dtype=np.float32)
    inv_freq = (1.0 / (10000.0 ** (np.arange(0, 64, 2, dtype=np.float32) / np.float32(64)))).astype(np.float32)
    ang = pos[:, None] * inv_freq[None, :]
    ang = np.concatenate([ang, ang], axis=-1)
    cos = np.cos(ang).astype(np.float32).T
    sin = np.sin(ang).astype(np.float32).T
    sgn = np.where(np.arange(64) < 32, -1.0, 1.0).astype(np.float32)[:, None]
    cos128 = np.concatenate([cos, cos], 0)
    sin128 = np.concatenate([sin * sgn, sin * sgn], 0)
    bf = ml_dtypes.bfloat16
    ident = np.eye(128, dtype=np.float32).astype(bf)
    perm = np.zeros((128, 128), np.float32)
    for m in range(128):
        k = (m % 64 + 32) % 64 + 64 * (m // 64)
        perm[k, m] = 1.0
    mask = np.where(np.arange(128)[None, :] >= np.arange(128)[:, None], 0.0, -30000.0).astype(np.float32)
    return (np.ascontiguousarray(cos128), np.ascontiguousarray(sin128), ident, perm.astype(bf), mask.astype(bf))


def _a_unit(W, cols):
    return W[:, cols].reshape(KC, 128, 512).transpose(1, 0, 2).reshape(128, 4096)


def _host_units(w_in, wpc, wpa, wo, wgu, wd):
    units = np.zeros((NU, 128, 4096), np.float32)
    starts = {U_Q: 0, U_K: 1024, U_V: 2048, U_CB: 3072, U_CC: 4096, U_CX: 5120, U_GA: 6144, U_GB: 7168}
    for u0, c0 in starts.items():
        for i in range(2):
            units[u0 + i] = _a_unit(w_in, np.arange(c0 + 512 * i, c0 + 512 * (i + 1)))
    for i in range(2):
        units[U_WPC + i] = _a_unit(wpc, np.arange(512 * i, 512 * (i + 1)))
        units[U_WPA + i] = _a_unit(wpa, np.arange(512 * i, 512 * (i + 1)))
        units[U_WO + i] = _a_unit(wo, np.arange(512 * i, 512 * (i + 1)))
    for i in range(11):
        cols = np.concatenate([np.arange(128 * (2 * i), 128 * (2 * i + 1)), DFF + np.arange(128 * (2 * i), 128 * (2 * i + 1)),
                               np.arange(128 * (2 * i + 1), 128 * (2 * i + 2)), DFF + np.arange(128 * (2 * i + 1), 128 * (2 * i + 2))])
        units[U_WGU + i] = _a_unit(wgu, cols)
    for hf in range(2):
        for g3 in range(3):
            nj = 8 if g3 < 2 else 6
            blk = wd[1024 * g3: 1024 * g3 + 128 * nj, 512 * hf: 512 * (hf + 1)].reshape(nj, 128, 512).transpose(1, 0, 2).reshape(128, nj * 512)
            units[U_WD + hf * 3 + g3, :, :nj * 512] = blk
    return units


def _host_inputs(inputs, nseq, nch, ncores):
    T = nch * TQ
    f = lambda a: np.ascontiguousarray(np.asarray(a, dtype=np.float32))
    cos128, sin128, ident, perm, mask = _host_consts(T)
    units = _host_units(f(inputs["w_in"])[0], f(inputs["w_proj_conv"])[0], f(inputs["w_proj_attn"])[0], f(inputs["w_out"])[0],
                        f(inputs["w_gate_up"])[0], f(inputs["w_down"])[0])
    col8 = lambda v: np.ascontiguousarray(f(v).reshape(KC, 128).T)
    convw = np.ascontiguousarray(f(inputs["conv_w"])[0].reshape(3, KC, 128).transpose(2, 1, 0).reshape(128, KC * 3))
    lam = np.concatenate([f(inputs["lambda_q1"])[0], f(inputs["lambda_k1"])[0], f(inputs["lambda_q2"])[0], f(inputs["lambda_k2"])[0]])
    common = dict(
        meta=f(inputs["meta_tokens"]), wun=units, cos=cos128, sin=sin128, ident=ident, perm=perm, mask=mask,
        nmw=col8(inputs["norm_mix_w"][0]), nfw=col8(inputs["norm_ffn_w"][0]), convw=convw,
        subw=np.ascontiguousarray(f(inputs["subln_w"])[0].reshape(128, 1)),
        nfin=np.ascontiguousarray(np.broadcast_to(f(inputs["norm_final_w"])[None, :], (128, D))),
        lam=np.ascontiguousarray(np.broadcast_to(lam[None, :], (128, 256))),
    )
    x = f(inputs["x"])
    maps = []
    for ci in range(ncores):
        m = dict(common)
        m["x"] = np.ascontiguousarray(x[ci * nseq:(ci + 1) * nseq, :T])
        maps.append(m)
    return maps


def kernel(**inputs):
    nseq = BATCH // N_CORES
    nch = SEQ // TQ
    maps = _host_inputs(inputs, nseq, nch, N_CORES)
    nc = build(nseq, nch)
    res = run_bass_kernel_spmd(nc, maps, core_ids=list(range(N_CORES)))
    out = np.concatenate([np.asarray(r["y"]) for r in res.results], axis=0)
    return out.astype(np.float32)
```
